# Optimizing a Trainium2 kernel written in Bass

```python
import math
import jax, jax.numpy as jnp
from jax import lax
import numpy as np


D_MODEL = 1024
BATCH = 4
SEQ = 8192
DEPTH = 1

MEM_LEN = 256

RWKV_HEADS = 8
RWKV_HEAD_DIM = 64
RWKV_DIM = RWKV_HEADS * RWKV_HEAD_DIM
DECAY_LORA = 64
AAA_LORA = 64
GATE_LORA = 128
RWKV_GN_EPS = 64e-5

NSA_HEADS = 8
NSA_KV_GROUPS = 2
NSA_HPG = NSA_HEADS // NSA_KV_GROUPS
NSA_HEAD_DIM = 64
NSA_DIM = NSA_HEADS * NSA_HEAD_DIM
NSA_KV_DIM = NSA_KV_GROUPS * NSA_HEAD_DIM
CMP_LEN = 32
CMP_STRIDE = 16
CMP_HIDDEN = 128
SLC_LEN = 64
N_SEL = 16
WINDOW = 512
Q_BLOCK = 128
FORCED_SCORE = 1e4

NUM_BUCKETS = 32
MAX_DISTANCE = 128

XATTN_HEADS = 4
XATTN_HEAD_DIM = D_MODEL // XATTN_HEADS

D_FF = 2816
LN_EPS = 1e-5

DEEPNORM_ALPHA = (2.0 * DEPTH) ** 0.25
DEEPNORM_BETA = (8.0 * DEPTH) ** -0.25

MIX_DIM = RWKV_DIM + NSA_DIM
RWKV_SPLITS = (RWKV_DIM, RWKV_DIM, RWKV_DIM, DECAY_LORA, AAA_LORA, GATE_LORA)
NSA_SPLITS = (NSA_DIM,) + (NSA_KV_DIM,) * 6 + (3 * NSA_HEADS,)
RWKV_IN = sum(RWKV_SPLITS)
NSA_IN = sum(NSA_SPLITS)
IN_DIM = RWKV_IN + NSA_IN

kernel_name = 'hybrid_rwkv7_nsa_macaron_deepnorm'


def _offsets(sizes):
    return [int(s) for s in np.cumsum(sizes)[:-1]]


def layer_norm(x, g, b):
    xf = x.astype(jnp.float32)
    mu = jnp.mean(xf, -1, keepdims=True)
    var = jnp.mean(jnp.square(xf - mu), -1, keepdims=True)
    return ((xf - mu) * lax.rsqrt(var + LN_EPS) * g + b).astype(x.dtype)


def swiglu(x, w_gate, w_up, w_down):
    return (jax.nn.silu(x @ w_gate) * (x @ w_up)) @ w_down


def masked_softmax(logits, mask):
    logits = jnp.where(mask, logits.astype(jnp.float32), -1e30)
    m = jnp.max(logits, -1, keepdims=True)
    e = jnp.where(mask, jnp.exp(logits - m), 0.0)
    return e / jnp.maximum(jnp.sum(e, -1, keepdims=True), 1e-30)


def t5_bucket(dist):
    n = jnp.maximum(dist, 0)
    max_exact = NUM_BUCKETS // 2
    nf = jnp.maximum(n, max_exact).astype(jnp.float32)
    large = max_exact + (jnp.log(nf / max_exact) / math.log(MAX_DISTANCE / max_exact)
                         * (NUM_BUCKETS - max_exact)).astype(jnp.int32)
    large = jnp.minimum(large, NUM_BUCKETS - 1)
    return jnp.where(n < max_exact, n, large)


def token_shift(z, mu):
    prev = jnp.pad(z, ((0, 0), (1, 0), (0, 0)))[:, :-1]
    return z + (prev - z) * mu


def rwkv7_group(z, mu, w0, w_up, a0, a_up, g_up, k_k, k_a, r_k, gn_g, gn_b):
    B, S, _ = z.shape
    heads = lambda t: t.reshape(B, S, RWKV_HEADS, RWKV_HEAD_DIM)
    z = token_shift(z, mu)
    r, k, v, wl, al, gl = jnp.split(z, _offsets(RWKV_SPLITS), axis=-1)
    w = -jax.nn.softplus(-(w0 + jnp.tanh(wl) @ w_up)) - 0.5
    decay = jnp.exp(-jnp.exp(w.astype(jnp.float32)))
    a = jax.nn.sigmoid(a0 + al @ a_up)
    g = jax.nn.sigmoid(gl) @ g_up
    kk = heads(k * k_k).astype(jnp.float32)
    kk = kk * lax.rsqrt(jnp.maximum(jnp.sum(kk * kk, -1, keepdims=True), 1e-24))
    k = k * (1.0 + (a - 1.0) * k_a)
    rh, kh, vh, ah, wh = [heads(t).astype(jnp.float32) for t in (r, k, v, a, decay)]

    def step(state, inp):
        r_t, w_t, k_t, v_t, kk_t, a_t = inp
        sa = jnp.einsum('bhij,bhj->bhi', state, -kk_t)
        state = (state * w_t[:, :, None, :] + sa[..., None] * (kk_t * a_t)[:, :, None, :]
                 + v_t[..., None] * k_t[:, :, None, :])
        return state, jnp.einsum('bhij,bhj->bhi', state, r_t)

    xs = tuple(jnp.moveaxis(t, 1, 0) for t in (rh, wh, kh, vh, kk, ah))
    state0 = jnp.zeros((B, RWKV_HEADS, RWKV_HEAD_DIM, RWKV_HEAD_DIM), jnp.float32)
    _, y = lax.scan(step, state0, xs)
    y = jnp.moveaxis(y, 0, 1)
    mu_y = jnp.mean(y, -1, keepdims=True)
    var_y = jnp.mean(jnp.square(y - mu_y), -1, keepdims=True)
    y = ((y - mu_y) * lax.rsqrt(var_y + RWKV_GN_EPS)).reshape(B, S, RWKV_DIM) * gn_g + gn_b
    r_k_h = r_k.reshape(RWKV_HEADS, RWKV_HEAD_DIM).astype(jnp.float32)
    bonus = jnp.sum(rh * kh * r_k_h, -1, keepdims=True) * vh
    return ((y + bonus.reshape(B, S, RWKV_DIM)) * g).astype(z.dtype)


def compress(kv, pe, w1, w2):
    S = kv.shape[2]
    n_cmp = (S - CMP_LEN) // CMP_STRIDE + 1
    idx = jnp.arange(n_cmp)[:, None] * CMP_STRIDE + jnp.arange(CMP_LEN)[None, :]
    blocks = kv[:, :, idx, :] + pe
    flat = blocks.reshape(*blocks.shape[:3], CMP_LEN * NSA_HEAD_DIM)
    return jax.nn.gelu(flat @ w1) @ w2


def nsa_group(z, pe_k, w1_k, w2_k, pe_v, w1_v, w2_v, rel_bias):
    B, S, _ = z.shape
    G, P, dk = NSA_KV_GROUPS, NSA_HPG, NSA_HEAD_DIM
    q, kc, vc, ks, vs, kw, vw, gate = jnp.split(z, _offsets(NSA_SPLITS), axis=-1)
    q = q.reshape(B, S, G, P, dk).transpose(0, 2, 3, 1, 4)
    kvh = lambda t: t.reshape(B, S, G, dk).transpose(0, 2, 1, 3)
    kc, vc, ks, vs, kw, vw = [kvh(t) for t in (kc, vc, ks, vs, kw, vw)]
    kc = compress(kc, pe_k, w1_k, w2_k)
    vc = compress(vc, pe_v, w1_v, w2_v)
    n_cmp = kc.shape[2]
    cmp_end = jnp.arange(n_cmp) * CMP_STRIDE + CMP_LEN - 1
    cmp_start = cmp_end - CMP_LEN + 1
    n_slc = S // SLC_LEN
    n_sel = min(N_SEL, n_slc)
    slc_start = jnp.arange(n_slc) * SLC_LEN
    overlap = ((cmp_start[:, None] < slc_start[None, :] + SLC_LEN)
               & (cmp_end[:, None] >= slc_start[None, :])).astype(jnp.float32)
    ks_flat = ks.reshape(B * G * n_slc, SLC_LEN, dk)
    vs_flat = vs.reshape(B * G * n_slc, SLC_LEN, dk)
    bg_off = (jnp.arange(B * G) * n_slc).reshape(B, G, 1, 1)
    kw_pad = jnp.pad(kw, ((0, 0), (0, 0), (WINDOW, 0), (0, 0)))
    vw_pad = jnp.pad(vw, ((0, 0), (0, 0), (WINDOW, 0), (0, 0)))
    bias_gp = rel_bias.reshape(NUM_BUCKETS, G, P)
    g_idx = jnp.arange(G).reshape(1, G, 1, 1)
    scale = dk ** -0.5

    def shared_bias(dist):
        return bias_gp[t5_bucket(dist)].transpose(2, 3, 0, 1)

    def block(qi):
        q0 = qi * Q_BLOCK
        qb = lax.dynamic_slice_in_dim(q, q0, Q_BLOCK, axis=3)
        t = q0 + jnp.arange(Q_BLOCK)
        dist_c = t[:, None] - cmp_end[None, :]
        lc = jnp.einsum('bgpqd,bgnd->bgpqn', qb, kc) * scale + shared_bias(dist_c)
        pc = masked_softmax(lc, dist_c >= 0)
        oc = jnp.einsum('bgpqn,bgnd->bgpqd', pc.astype(vc.dtype), vc)
        imp = jnp.einsum('bgpqn,nj->bgqj', pc, overlap)
        blk = jnp.arange(n_slc)[None, :]
        cur = (t // SLC_LEN)[:, None]
        valid = slc_start[None, :] <= t[:, None]
        forced = (blk == 0) | (blk == cur) | (blk == cur - 1)
        score = jnp.where(valid, jnp.where(forced, FORCED_SCORE, imp), -1.0)
        _, sel = lax.top_k(score, n_sel)
        ksel = ks_flat[sel + bg_off].reshape(B, G, Q_BLOCK, n_sel * SLC_LEN, dk)
        vsel = vs_flat[sel + bg_off].reshape(B, G, Q_BLOCK, n_sel * SLC_LEN, dk)
        pos = (sel[..., None] * SLC_LEN + jnp.arange(SLC_LEN)).reshape(B, G, Q_BLOCK, n_sel * SLC_LEN)
        dist_s = t[:, None] - pos
        bias_s = jnp.moveaxis(bias_gp[t5_bucket(dist_s), g_idx], -1, 2)
        ls = jnp.einsum('bgpqd,bgqkd->bgpqk', qb, ksel) * scale + bias_s
        ps = masked_softmax(ls, (dist_s >= 0)[:, :, None])
        osel = jnp.einsum('bgpqk,bgqkd->bgpqd', ps.astype(vs.dtype), vsel)
        kwb = lax.dynamic_slice_in_dim(kw_pad, q0, WINDOW + Q_BLOCK, axis=2)
        vwb = lax.dynamic_slice_in_dim(vw_pad, q0, WINDOW + Q_BLOCK, axis=2)
        pos_w = q0 - WINDOW + jnp.arange(WINDOW + Q_BLOCK)
        dist_w = t[:, None] - pos_w[None, :]
        mask_w = (dist_w >= 0) & (dist_w < WINDOW) & (pos_w[None, :] >= 0)
        lw = jnp.einsum('bgpqd,bgkd->bgpqk', qb, kwb) * scale + shared_bias(dist_w)
        pw = masked_softmax(lw, mask_w)
        ow = jnp.einsum('bgpqk,bgkd->bgpqd', pw.astype(vw.dtype), vwb)
        return oc, osel, ow

    oc, osel, ow = lax.map(block, jnp.arange(S // Q_BLOCK))
    merge = lambda o: o.transpose(1, 0, 4, 2, 3, 5).reshape(B, S, NSA_HEADS, dk)
    gate = jax.nn.sigmoid(gate.reshape(B, S, NSA_HEADS, 3))
    o = gate[..., 0:1] * merge(oc) + gate[..., 1:2] * merge(osel) + gate[..., 2:3] * merge(ow)
    return o.reshape(B, S, NSA_DIM)


def cross_attention(h, mem, wq, wk, wv, wo):
    B, S, _ = h.shape
    M = mem.shape[1]
    q = (h @ wq).reshape(B, S, XATTN_HEADS, XATTN_HEAD_DIM)
    k = (mem @ wk).reshape(B, M, XATTN_HEADS, XATTN_HEAD_DIM)
    v = (mem @ wv).reshape(B, M, XATTN_HEADS, XATTN_HEAD_DIM)
    logits = jnp.einsum('bqhd,bkhd->bhqk', q, k).astype(jnp.float32) * XATTN_HEAD_DIM ** -0.5
    p = jax.nn.softmax(logits, -1).astype(v.dtype)
    o = jnp.einsum('bhqk,bkhd->bqhd', p, v).reshape(B, S, D_MODEL)
    return o @ wo


def setup_inputs(seed: int = 0) -> dict:
    key = jax.random.key(seed)
    keys = iter(jax.random.split(key, 48))
    L = DEPTH
    nrm = lambda shape, s: jax.random.normal(next(keys), shape, jnp.float32) * s
    gain = lambda n: 1.0 + nrm((L, n), 0.02)
    bias = lambda n: nrm((L, n), 0.02)
    D, F = D_MODEL, D_FF
    return {
        'x': nrm((BATCH, SEQ, D), 1.0),
        'mem': nrm((BATCH, MEM_LEN, D), 1.0),
        'ffn1_w_gate': nrm((L, D, F), D ** -0.5),
        'ffn1_w_up': nrm((L, D, F), D ** -0.5),
        'ffn1_w_down': nrm((L, F, D), F ** -0.5 * DEEPNORM_BETA),
        'ln1_g': gain(D), 'ln1_b': bias(D),
        'mix_w_in': nrm((L, D, IN_DIM), D ** -0.5),
        'rwkv_mu': jax.random.uniform(next(keys), (L, RWKV_IN), jnp.float32, 0.0, 1.0),
        'rwkv_w0': jax.random.uniform(next(keys), (L, RWKV_DIM), jnp.float32, -6.0, -1.0),
        'rwkv_w_up': nrm((L, DECAY_LORA, RWKV_DIM), 0.1 * DECAY_LORA ** -0.5),
        'rwkv_a0': nrm((L, RWKV_DIM), 0.1),
        'rwkv_a_up': nrm((L, AAA_LORA, RWKV_DIM), AAA_LORA ** -0.5),
        'rwkv_g_up': nrm((L, GATE_LORA, RWKV_DIM), GATE_LORA ** -0.5),
        'rwkv_k_k': 0.85 + nrm((L, RWKV_DIM), 0.02),
        'rwkv_k_a': 1.0 + nrm((L, RWKV_DIM), 0.02),
        'rwkv_r_k': nrm((L, RWKV_DIM), 0.1),
        'rwkv_gn_g': gain(RWKV_DIM), 'rwkv_gn_b': bias(RWKV_DIM),
        'nsa_pe_k': nrm((L, CMP_LEN, NSA_HEAD_DIM), 0.02),
        'nsa_w1_k': nrm((L, CMP_LEN * NSA_HEAD_DIM, CMP_HIDDEN), (CMP_LEN * NSA_HEAD_DIM) ** -0.5),
        'nsa_w2_k': nrm((L, CMP_HIDDEN, NSA_HEAD_DIM), CMP_HIDDEN ** -0.5),
        'nsa_pe_v': nrm((L, CMP_LEN, NSA_HEAD_DIM), 0.02),
        'nsa_w1_v': nrm((L, CMP_LEN * NSA_HEAD_DIM, CMP_HIDDEN), (CMP_LEN * NSA_HEAD_DIM) ** -0.5),
        'nsa_w2_v': nrm((L, CMP_HIDDEN, NSA_HEAD_DIM), CMP_HIDDEN ** -0.5),
        'mix_w_out': nrm((L, MIX_DIM, D), MIX_DIM ** -0.5 * DEEPNORM_BETA),
        'ln2_g': gain(D), 'ln2_b': bias(D),
        'xattn_wq': nrm((L, D, D), D ** -0.5),
        'xattn_wk': nrm((L, D, D), D ** -0.5),
        'xattn_wv': nrm((L, D, D), D ** -0.5),
        'xattn_wo': nrm((L, D, D), D ** -0.5 * DEEPNORM_BETA),
        'ln3_g': gain(D), 'ln3_b': bias(D),
        'ffn2_w_gate': nrm((L, D, F), D ** -0.5),
        'ffn2_w_up': nrm((L, D, F), D ** -0.5),
        'ffn2_w_down': nrm((L, F, D), F ** -0.5 * DEEPNORM_BETA),
        'ln4_g': gain(D), 'ln4_b': bias(D),
        'rel_bias': nrm((NUM_BUCKETS, NSA_HEADS), 0.1),
    }


def reference(x, mem, ffn1_w_gate, ffn1_w_up, ffn1_w_down, ln1_g, ln1_b, mix_w_in,
              rwkv_mu, rwkv_w0, rwkv_w_up, rwkv_a0, rwkv_a_up, rwkv_g_up, rwkv_k_k, rwkv_k_a,
              rwkv_r_k, rwkv_gn_g, rwkv_gn_b, nsa_pe_k, nsa_w1_k, nsa_w2_k, nsa_pe_v, nsa_w1_v,
              nsa_w2_v, mix_w_out, ln2_g, ln2_b, xattn_wq, xattn_wk, xattn_wv, xattn_wo,
              ln3_g, ln3_b, ffn2_w_gate, ffn2_w_up, ffn2_w_down, ln4_g, ln4_b, rel_bias):
    a = DEEPNORM_ALPHA
    for l in range(DEPTH):
        x = layer_norm(a * x + 0.5 * swiglu(x, ffn1_w_gate[l], ffn1_w_up[l], ffn1_w_down[l]), ln1_g[l], ln1_b[l])
        z = x @ mix_w_in[l]
        y_rwkv = rwkv7_group(z[..., :RWKV_IN], rwkv_mu[l], rwkv_w0[l], rwkv_w_up[l], rwkv_a0[l],
                             rwkv_a_up[l], rwkv_g_up[l], rwkv_k_k[l], rwkv_k_a[l], rwkv_r_k[l],
                             rwkv_gn_g[l], rwkv_gn_b[l])
        y_nsa = nsa_group(z[..., RWKV_IN:], nsa_pe_k[l], nsa_w1_k[l], nsa_w2_k[l],
                          nsa_pe_v[l], nsa_w1_v[l], nsa_w2_v[l], rel_bias)
        mixed = jnp.concatenate([y_rwkv, y_nsa.astype(y_rwkv.dtype)], axis=-1) @ mix_w_out[l]
        x = layer_norm(a * x + mixed, ln2_g[l], ln2_b[l])
        x = layer_norm(a * x + cross_attention(x, mem, xattn_wq[l], xattn_wk[l], xattn_wv[l], xattn_wo[l]),
                       ln3_g[l], ln3_b[l])
        x = layer_norm(a * x + 0.5 * swiglu(x, ffn2_w_gate[l], ffn2_w_up[l], ffn2_w_down[l]), ln4_g[l], ln4_b[l])
    return x
```

```python
import math
from contextlib import ExitStack
import numpy as np
import concourse.bass as bass
import concourse.mybir as mybir
from concourse.bass_utils import run_bass_kernel_spmd

F32 = mybir.dt.float32
BF16 = mybir.dt.bfloat16
AF = mybir.ActivationFunctionType
ALU = mybir.AluOpType
AX = mybir.AxisListType

D = 1024
DFF = 2816
NFC = DFF // 128
LN_EPS = 1e-5
ALPHA = 2.0 ** 0.25
NEG = -30000.0


class Tok:
    __slots__ = ("name", "w", "r")

    def __init__(self, name=""):
        self.name = name
        self.w = None
        self.r = {}


def toks(n, name=""):
    return [Tok(name + str(i)) for i in range(n)]


class Sched:
    NDMA = 24

    def __init__(self, nc, stack, same_engine_sync=True):
        self.nc = nc
        self.engs = {"pe": nc.tensor, "act": nc.scalar, "dve": nc.vector,
                     "pool": nc.gpsimd, "sp": nc.sync}
        self.sem = {}
        self.cnt = {}
        for k in self.engs:
            self.sem[k] = stack.enter_context(nc.semaphore("s_" + k))
            self.cnt[k] = 0
        for i in range(self.NDMA):
            k = "d%d" % i
            self.sem[k] = stack.enter_context(nc.semaphore("s_" + k))
            self.cnt[k] = 0
        self.seen = {k: {} for k in self.engs}
        self.dma_i = {"sw": 0, "hw": 0}
        self.ses = same_engine_sync
        self.ninst = 0

    def _wait(self, eng, deps):
        e = self.engs[eng]
        seen = self.seen[eng]
        for (c, v) in deps:
            if c == eng and (eng == "pe" or eng == "sp" or not self.ses):
                continue
            if seen.get(c, 0) >= v:
                continue
            e.wait_ge(self.sem[c], v)
            seen[c] = v

    def _deps(self, r, w):
        deps = []
        for t in r:
            if t.w is not None:
                deps.append(t.w)
        for t in w:
            if t.w is not None:
                deps.append(t.w)
            for c, v in t.r.items():
                deps.append((c, v))
        return deps

    def op(self, eng, fn, r=(), w=()):
        self._wait(eng, self._deps(r, w))
        ins = fn(self.engs[eng])
        self.cnt[eng] += 1
        v = self.cnt[eng]
        ins.then_inc(self.sem[eng], 1)
        for t in r:
            t.r[eng] = v
        for t in w:
            t.w = (eng, v)
            t.r = {}
        self.ninst += 1

    def dma(self, q, out, in_, r=(), w=(), **kw):
        if q == "pool":
            slot = "d%d" % (self.dma_i["sw"] % 8)
            self.dma_i["sw"] += 1
        else:
            slot = "d%d" % (8 + self.dma_i["hw"] % (self.NDMA - 8))
            self.dma_i["hw"] += 1
        deps = self._deps(r, w)
        if self.cnt[slot] > 0:
            deps.append((slot, self.cnt[slot]))
        self._wait(q, deps)
        ins = self.engs[q].dma_start(out=out, in_=in_, **kw)
        self.cnt[slot] += 16
        v = self.cnt[slot]
        ins.then_inc(self.sem[slot], 16)
        for t in r:
            t.r[slot] = v
        for t in w:
            t.w = (slot, v)
            t.r = {}
        self.ninst += 1

    def barrier(self, engs=("pe", "act", "dve", "pool", "sp")):
        allc = [(c, v) for c, v in self.cnt.items() if v > 0]
        for e in engs:
            self._wait(e, [(c, v) for (c, v) in allc if c != e or e not in ("pe", "sp")])

    def finish(self, tks, eng="sp"):
        deps = [t.w for t in tks if t.w is not None]
        self._wait(eng, deps)

    def mm(self, out, lhsT, rhs, start, stop, r, w):
        self.op("pe", lambda e: e.matmul(out, lhsT=lhsT, rhs=rhs, start=start, stop=stop,
                                         skip_group_check=True), r=r, w=w)

    def tr(self, out, in_, ident, r, w):
        self.op("pe", lambda e: e.transpose(out, in_, ident), r=r, w=w)

    def act(self, out, in_, func, r, w, eng="act", **kw):
        self.op(eng, lambda e: e.activation(out=out, in_=in_, func=func, **kw), r=r, w=w)

    def tt(self, eng, out, in0, in1, op, r, w):
        self.op(eng, lambda e: e.tensor_tensor(out=out, in0=in0, in1=in1, op=op), r=r, w=w)

    def ts(self, eng, out, in0, s1, s2, op0, op1, r, w, **kw):
        if op1 is None:
            self.op(eng, lambda e: e.tensor_scalar(out=out, in0=in0, scalar1=s1, scalar2=None, op0=op0, **kw), r=r, w=w)
        else:
            self.op(eng, lambda e: e.tensor_scalar(out=out, in0=in0, scalar1=s1, scalar2=s2, op0=op0, op1=op1, **kw), r=r, w=w)

    def stt(self, out, in0, scalar, in1, op0, op1, r, w):
        self.op("dve", lambda e: e.scalar_tensor_tensor(out=out, in0=in0, scalar=scalar, in1=in1, op0=op0, op1=op1), r=r, w=w)

    def cp(self, eng, out, in_, r, w):
        if eng == "act":
            self.op("act", lambda e: e.copy(out=out, in_=in_), r=r, w=w)
        else:
            self.op(eng, lambda e: e.tensor_copy(out=out, in_=in_), r=r, w=w)

    def memset(self, eng, ap, val, w):
        self.op(eng, lambda e: e.memset(ap, val), r=(), w=w)


class Ctx:
    pass


def load_cast(S, q, dst, src, w, r=()):
    S.dma("pool", dst, src, r=r, w=w, max_dma_last_dim=4096)


def layer_norm_tile(S, C, r_ap, out_ap, g_t, b_t, valid_ap, Tr, Tout, tmp):
    nc = C.nc
    st, mv, rstd = tmp["st"], tmp["mv"], tmp["rstd"]
    Tst = tmp["T"]
    S.op("dve", lambda e: e.bn_stats(out=st[:, 0, :], in_=r_ap[:, 0:512]), r=[Tr], w=[Tst])
    S.op("dve", lambda e: e.bn_stats(out=st[:, 1, :], in_=r_ap[:, 512:1024]), r=[Tr], w=[Tst])
    S.op("dve", lambda e: e.bn_aggr(out=mv[:], in_=st[:]), r=[Tst], w=[Tst])
    S.act(rstd[:], mv[:, 1:2], AF.Sqrt, r=[Tst], w=[Tst], bias=LN_EPS, scale=1.0)
    S.op("dve", lambda e: e.reciprocal(out=rstd[:], in_=rstd[:]), r=[Tst], w=[Tst])
    S.ts("dve", r_ap, r_ap, mv[:, 0:1], rstd[:, 0:1], ALU.subtract, ALU.mult, r=[Tst, Tr], w=[Tr])
    if valid_ap is not None:
        S.stt(r_ap, r_ap, valid_ap, g_t[:], ALU.mult, ALU.mult, r=[Tr, C.Tconst], w=[Tr])
        S.stt(out_ap, b_t[:], valid_ap, r_ap, ALU.mult, ALU.add, r=[Tr, C.Tconst], w=[Tout])
    else:
        S.tt("dve", r_ap, r_ap, g_t[:], ALU.mult, r=[Tr, C.Tconst], w=[Tr])
        S.tt("dve", out_ap, r_ap, b_t[:], ALU.add, r=[Tr, C.Tconst], w=[Tout])


def load_xT(S, C, src_rows_ap, xt, Txt, xT, TxT, tps, Ttps, ident, TG=2):
    S.dma("sp", xt[:], src_rows_ap.rearrange("(s p) d -> p s d", p=128), w=[Txt])
    make_xT(S, C, xt, Txt, xT, TxT, tps, Ttps, ident, TG)


def make_xT(S, C, xt, Txt, xT, TxT, tps, Ttps, ident, TG=2):
    k = 0
    for s in range(TG):
        for hb in range(2):
            pb = tps[k % 2]
            Tp = Ttps[k % 2]
            k += 1
            for j in range(4):
                dc = hb * 4 + j
                S.tr(pb[:, j * 128:(j + 1) * 128], xt[:, s, dc * 128:(dc + 1) * 128], ident[:], r=[Txt, C.Tconst], w=[Tp])
            eng = "dve" if (k % 2) else "act"
            S.cp(eng, xT[:, hb * 4:(hb + 1) * 4, s * 128:(s + 1) * 128],
                 pb[:].rearrange("p (j t) -> p j t", j=4), r=[Tp], w=[TxT])


def ffn_phase(S, C, src, dst, wg_d, wu_d, wd_d, g_d, b_d, ngroups, use_valid, name):
    nc = C.nc
    TG = 2
    GT = TG * 128
    with ExitStack() as st:
        sb = lambda n, shape, dt: st.enter_context(nc.sbuf_tensor(name + n, shape, dt))
        ps = lambda n: st.enter_context(nc.psum_tensor(name + n, [128, 512], F32))
        wg = sb("wg", [128, 8, DFF], BF16)
        wu = sb("wu", [128, 8, DFF], BF16)
        wd = sb("wd", [128, NFC, D], BF16)
        Twg, Twu, Twd = toks(8, "wg"), toks(8, "wu"), toks(NFC, "wd")
        gt = sb("g", [128, D], F32)
        bt = sb("b", [128, D], F32)
        S.dma("sp", gt[:], g_d.partition_broadcast(128), w=[C.Tconst])
        S.dma("sp", bt[:], b_d.partition_broadcast(128), w=[C.Tconst])
        for dc in range(8):
            load_cast(S, "pool", wg[:, dc, :], wg_d[dc * 128:(dc + 1) * 128, :], w=[Twg[dc]])
            load_cast(S, "pool", wu[:, dc, :], wu_d[dc * 128:(dc + 1) * 128, :], w=[Twu[dc]])
        for fc in range(NFC):
            load_cast(S, "pool", wd[:, fc, :], wd_d[fc * 128:(fc + 1) * 128, :], w=[Twd[fc]])
        xt = [sb("xt%d" % i, [128, TG, D], F32) for i in range(2)]
        Txt = toks(2, "xt")
        xT = [sb("xT%d" % i, [128, 8, GT], BF16) for i in range(2)]
        TxT = toks(2, "xT")
        hT = [sb("hT%d" % i, [128, NFC, GT], BF16) for i in range(2)]
        ThT = toks(2, "hT")
        sg = [sb("sg%d" % i, [128, GT], F32) for i in range(2)]
        Tsg = toks(2, "sg")
        rr = [sb("rr%d" % i, [128, D], F32) for i in range(2)]
        Trr = toks(2, "rr")
        oo = [sb("oo%d" % i, [128, D], F32) for i in range(2)]
        Too = toks(2, "oo")
        lnt = {"st": sb("lnst", [128, 2, 6], F32), "mv": sb("lnmv", [128, 2], F32),
               "rstd": sb("lnrs", [128, 1], F32), "T": Tok("lnt")}
        tps = [ps("tp0"), ps("tp1")]
        Ttps = toks(2, "tp")
        gups = [ps("g0"), ps("g1")]
        Tgu = toks(2, "gu")
        yps = [ps("y0"), ps("y1")]
        Ty = toks(2, "y")
        ti = 0
        for g in range(ngroups):
            b2 = g % 2
            load_xT(S, C, src[g * GT:(g + 1) * GT, :], xt[b2], Txt[b2], xT[b2], TxT[b2], tps, Ttps, C.ident, TG)
            for fc in range(NFC):
                p2 = fc % 2
                for dc in range(8):
                    S.mm(gups[p2][:, 0:GT], wg[:, dc, fc * 128:(fc + 1) * 128], xT[b2][:, dc, :], dc == 0, dc == 7,
                         r=[Twg[dc], TxT[b2]], w=[Tgu[p2]])
                for dc in range(8):
                    S.mm(gups[p2][:, GT:2 * GT], wu[:, dc, fc * 128:(fc + 1) * 128], xT[b2][:, dc, :], dc == 0, dc == 7,
                         r=[Twu[dc], TxT[b2]], w=[Tgu[p2]])
                S.act(sg[p2][:], gups[p2][:, 0:GT], AF.Silu, r=[Tgu[p2]], w=[Tsg[p2]])
                S.tt("dve", hT[b2][:, fc, :], sg[p2][:], gups[p2][:, GT:2 * GT], ALU.mult, r=[Tsg[p2], Tgu[p2]], w=[ThT[b2]])
            for s in range(TG):
                r2 = ti % 2
                ti += 1
                for nb in range(2):
                    for fc in range(NFC):
                        S.mm(yps[nb][:], hT[b2][:, fc, s * 128:(s + 1) * 128], wd[:, fc, nb * 512:(nb + 1) * 512],
                             fc == 0, fc == NFC - 1, r=[ThT[b2], Twd[fc]], w=[Ty[nb]])
                for nb in range(2):
                    S.act(rr[r2][:, nb * 512:(nb + 1) * 512], yps[nb][:], AF.Identity, r=[Ty[nb]], w=[Trr[r2]], scale=0.5)
                S.stt(rr[r2][:], xt[b2][:, s, :], ALPHA, rr[r2][:], ALU.mult, ALU.add, r=[Txt[b2], Trr[r2]], w=[Trr[r2]])
                tile = g * TG + s
                vap = C.valid[:, tile:tile + 1] if use_valid else None
                layer_norm_tile(S, C, rr[r2][:], oo[r2][:], gt, bt, vap, Trr[r2], Too[r2], lnt)
                S.dma("sp", dst[tile * 128:(tile + 1) * 128, :], oo[r2][:], r=[Too[r2]], w=[Tok()])
        S.barrier()


class Ring:
    def __init__(self, tiles, name):
        self.t = tiles
        self.T = toks(len(tiles), name)
        self.i = 0

    def next(self):
        i = self.i % len(self.t)
        self.i += 1
        return self.t[i], self.T[i]


HD = 64
NSA_OFF = 1792
SG_C = -0.6065306597126334


def mix_phase(S, C, d):
    nc = C.nc
    SL = C.SL
    NT = SL // 128
    OWN_T = NT // 2
    with ExitStack() as st:
        sb = lambda n, shape, dt: st.enter_context(nc.sbuf_tensor("m_" + n, shape, dt))
        pring = Ring([st.enter_context(nc.psum_tensor("m_ps%d" % i, [128, 512], F32)) for i in range(8)], "mps")
        Tw = Tok("mixw")
        W = sb("W", [128, 8, 3096], BF16)
        mub = sb("mub", [128, 1536], BF16)
        omb = sb("omb", [128, 1536], BF16)
        mucol = sb("mucol", [128, 4], F32)
        load_cast(S, "pool", mub[:], d["rwkv_mu"][0:1536].partition_broadcast(128), w=[Tw])
        S.ts("dve", omb[:], mub[:], -1.0, 1.0, ALU.mult, ALU.add, r=[Tw], w=[Tw])
        for c_ in range(2):
            S.dma("sp", mucol[:, c_:c_ + 1], d["rwkv_mu"][1536 + c_ * 128:1536 + (c_ + 1) * 128].rearrange("(p o) -> p o", o=1), w=[Tw])
        S.ts("dve", mucol[:, 2:4], mucol[:, 0:2], -1.0, 1.0, ALU.mult, ALU.add, r=[Tw], w=[Tw])
        for dc in range(8):
            load_cast(S, "pool", W[:, dc, :], d["mix_w_in"][dc * 128:(dc + 1) * 128, :], w=[Tw])
        lup = sb("lup", [128, 512], BF16)
        gup = sb("gup", [128, 512], BF16)
        load_cast(S, "pool", lup[0:64, :], d["rwkv_w_up"][:, :], w=[Tw])
        load_cast(S, "pool", lup[64:128, :], d["rwkv_a_up"][:, :], w=[Tw])
        load_cast(S, "pool", gup[:, :], d["rwkv_g_up"][:, :], w=[Tw])
        rows = {}
        for n in ["rwkv_w0", "rwkv_a0", "rwkv_k_k", "rwkv_k_a", "rwkv_r_k", "rwkv_gn_g", "rwkv_gn_b"]:
            rows[n] = sb(n, [128, 512], F32)
            S.dma("sp", rows[n][:], d[n].partition_broadcast(128), w=[Tw])
        cst = {}
        for n in ["tri_incl", "tri_excl", "tri_after", "m_su", "m_sl", "m_iu", "bdmask"]:
            cst[n] = sb(n, [128, 128], F32)
            S.dma("sp", cst[n][:], d[n][:, :], w=[Tw])
        sel2 = sb("sel2", [128, 2], F32)
        S.dma("sp", sel2[:], d["sel2"][:, :], w=[Tw])
        identb = sb("identb", [128, 128], BF16)
        S.cp("dve", identb[:], C.ident[:], r=[C.Tconst], w=[Tw])
        ident = C.ident

        def bc4(t):
            return t[:].unsqueeze(1).to_broadcast([128, 4, 128])

        def bc8(t):
            return t[:].unsqueeze(1).to_broadcast([128, 8, 128])

        xt = sb("xt", [128, 2, D], BF16)
        Txt = Tok("xt")
        xTe = [sb("xTe%d" % i, [128, 8, 257], BF16) for i in range(2)]
        TxTe = toks(2, "xTe")
        f32r = Ring([sb("f%d" % i, [128, 512], F32) for i in range(22)], "f32r")
        b16r = Ring([sb("h%d" % i, [128, 512], BF16) for i in range(21)], "b16r")
        mr = Ring([sb("M%d" % i, [128, 8, 128], BF16) for i in range(8)], "mr")
        keepr = Ring([sb("MK%d" % i, [128, 8, 128], BF16) for i in range(3)], "keepr")
        xtr = Ring([sb("XT%d" % i, [128, 4, 128], BF16) for i in range(8)], "xtr")
        smr = Ring([sb("sm%d" % i, [128, 16], F32) for i in range(12)], "smr")
        lw = [sb("lw%d" % i, [128, 256], BF16) for i in range(2)]
        lg = [sb("lg%d" % i, [128, 256], BF16) for i in range(2)]
        Tl = toks(2, "lora")
        STr = Ring([sb("ST%d" % i, [128, 4, 64], F32) for i in range(3)], "ST")
        GTr = Ring([sb("GT%d" % i, [128, 8, 128], F32) for i in range(1)], "GT")
        Hsr = Ring([sb("Hs%d" % i, [128, 8, 64], F32) for i in range(1)], "Hs")
        RHr = Ring([sb("RH%d" % i, [128, 4, 128], F32) for i in range(1)], "RH")
        kst = sb("kst", [64, 8, 256], BF16)
        Tkst = Tok("kst")
        qst = sb("qst", [64, 8, 256], BF16)
        Tqst = Tok("qst")
        vst = Ring([sb("vst%d" % i, [128, 256], BF16) for i in range(2)], "vst")
        gst = Ring([sb("gst%d" % i, [128, 24], F32) for i in range(2)], "gst")

        ST, TST = STr.next()
        S.memset("dve", ST[:], 0.0, w=[TST])
        stv = [ST, TST]
        S.memset("dve", xTe[1][:, :, 256:257], 0.0, w=[TxTe[1]])

        def evac_eng(k):
            return "act" if k % 2 else "dve"

        ek = [0]

        def tile_gen(g, s):
            b2 = g % 2
            own_g = (g * 2) >= OWN_T
            xcur = lambda dc, a, b, _x=xTe[b2]: _x[:, dc, 1 + a:1 + b]
            xprv = lambda dc, a, b, _x=xTe[b2]: _x[:, dc, a:b]
            TX = TxTe[b2]
            if s == 0:
                load_cast(S, "pool", xt[:], d["x1"][g * 256:(g + 1) * 256, :].rearrange("(s p) d -> p s d", p=128), w=[Txt])
                S.cp("pool", xTe[b2][:, :, 0:1], xTe[1 - b2][:, :, 256:257], r=[TxTe[1 - b2]], w=[TxTe[b2]])
                for s_ in range(2):
                    for hb in range(2):
                        pb, Tp = pring.next()
                        pbb_ = pb[:].bitcast(BF16)
                        for j in range(4):
                            dc = hb * 4 + j
                            S.tr(pbb_[:, j * 128:(j + 1) * 128], xt[:, s_, dc * 128:(dc + 1) * 128], identb[:], r=[Txt, Tw], w=[Tp])
                        ek[0] += 1
                        S.cp(evac_eng(ek[0]), xTe[b2][:, hb * 4:(hb + 1) * 4, 1 + s_ * 128:1 + (s_ + 1) * 128],
                             pbb_[:, 0:512].rearrange("p (j t) -> p j t", j=4), r=[Tp], w=[TxTe[b2]])
                pbz, Tpz = pring.next()
                pbp, Tpp = pring.next()
                for half, c0 in ((0, 1536), (1, 1664)):
                    for dc in range(8):
                        S.mm(pbz[:, half * 256:(half + 1) * 256], W[:, dc, c0:c0 + 128], xcur(dc, 0, 256), dc == 0, dc == 7, r=[Tw, TX], w=[Tpz])
                    for dc in range(8):
                        S.mm(pbp[:, half * 256:(half + 1) * 256], W[:, dc, c0:c0 + 128], xprv(dc, 0, 256), dc == 0, dc == 7, r=[Tw, TX], w=[Tpp])
                lz, Tlz = f32r.next()
                for half in range(2):
                    hs = slice(half * 256, (half + 1) * 256)
                    S.act(lz[:, hs], pbz[:, hs], AF.Identity, r=[Tpz, Tw], w=[Tlz], scale=mucol[:, 2 + half:3 + half])
                    S.stt(lz[:, hs], pbp[:, hs], mucol[:, half:half + 1], lz[:, hs], ALU.mult, ALU.add, r=[Tpp, Tlz, Tw], w=[Tlz])
                S.act(lw[b2][0:64, :], lz[0:64, 0:256], AF.Tanh, r=[Tlz], w=[Tl[b2]])
                S.cp("dve", lw[b2][64:128, :], lz[64:128, 0:256], r=[Tlz], w=[Tl[b2]])
                S.act(lg[b2][:], lz[:, 256:512], AF.Sigmoid, r=[Tlz], w=[Tl[b2]])
                slots = [512, 576, 640, 704, 768, 832, 1024, 1088]
                for bk in range(4):
                    pb, Tp = pring.next()
                    for k2 in range(2):
                        c0 = NSA_OFF + slots[bk * 2 + k2]
                        for dc in range(8):
                            S.mm(pb[0:64, k2 * 256:(k2 + 1) * 256], W[:, dc, c0:c0 + 64], xcur(dc, 0, 256), dc == 0, dc == 7, r=[Tw, TX], w=[Tp])
                    ek[0] += 1
                    S.cp(evac_eng(ek[0]), kst[:, bk * 2:(bk + 1) * 2, :], pb[0:64, :].rearrange("p (k t) -> p k t", k=2), r=[Tp], w=[Tkst])
                for k_ in range(8):
                    S.dma("sp", d["KT"][k_, :, g * 256:(g + 1) * 256], kst[:, k_, :], r=[Tkst], w=[Tok()])
                if own_g:
                    go = g - OWN_T // 2
                    for bk in range(4):
                        pb, Tp = pring.next()
                        for k2 in range(2):
                            c0 = NSA_OFF + (bk * 2 + k2) * 64
                            for dc in range(8):
                                S.mm(pb[0:64, k2 * 256:(k2 + 1) * 256], W[:, dc, c0:c0 + 64], xcur(dc, 0, 256), dc == 0, dc == 7, r=[Tw, TX], w=[Tp])
                        ek[0] += 1
                        S.cp(evac_eng(ek[0]), qst[:, bk * 2:(bk + 1) * 2, :], pb[0:64, :].rearrange("p (k t) -> p k t", k=2), r=[Tp], w=[Tqst])
                    for k_ in range(8):
                        S.dma("sp", d["QT"][k_, :, go * 256:(go + 1) * 256], qst[:, k_, :], r=[Tqst], w=[Tok()])
                yield "G"
            for _once in (0,):
                tile = g * 2 + s
                own = tile >= OWN_T
                a0, a1 = s * 128, (s + 1) * 128
                pb, Tp = pring.next()
                for k2, c0 in ((0, NSA_OFF + 896), (1, NSA_OFF + 1152)):
                    for dc in range(8):
                        S.mm(pb[:, k2 * 128:(k2 + 1) * 128], xcur(dc, a0, a1), W[:, dc, c0:c0 + 128], dc == 0, dc == 7, r=[Tw, TX], w=[Tp])
                if own:
                    for dc in range(8):
                        S.mm(pb[:, 256:280], xcur(dc, a0, a1), W[:, dc, NSA_OFF + 1280:NSA_OFF + 1304], dc == 0, dc == 7, r=[Tw, TX], w=[Tp])
                vt, Tv = vst.next()
                S.cp("act", vt[:], pb[:, 0:256], r=[Tp], w=[Tv])
                S.dma("sp", d["VSW"][tile * 128:(tile + 1) * 128, :], vt[:], r=[Tv], w=[Tok()])
                if own:
                    gt_, Tg_ = gst.next()
                    S.act(gt_[:], pb[:, 256:280], AF.Sigmoid, r=[Tp], w=[Tg_])
                    S.dma("sp", d["GATE"][(tile - OWN_T) * 128:(tile - OWN_T + 1) * 128, :], gt_[:], r=[Tg_], w=[Tok()])
                yield "S"
                sbs = []
                for q in range(3):
                    pbz, Tpz = pring.next()
                    pbp, Tpp = pring.next()
                    for dc in range(8):
                        S.mm(pbz[:], xcur(dc, a0, a1), W[:, dc, q * 512:(q + 1) * 512], dc == 0, dc == 7, r=[Tw, TX], w=[Tpz])
                    for dc in range(8):
                        S.mm(pbp[:], xprv(dc, a0, a1), W[:, dc, q * 512:(q + 1) * 512], dc == 0, dc == 7, r=[Tw, TX], w=[Tpp])
                    z_sb, Tzs = f32r.next()
                    S.tt("dve", z_sb[:], pbz[:], omb[:, q * 512:(q + 1) * 512], ALU.mult, r=[Tpz, Tw], w=[Tzs])
                    zp, Tzp = f32r.next()
                    S.tt("dve", zp[:], pbp[:], mub[:, q * 512:(q + 1) * 512], ALU.mult, r=[Tpp, Tw], w=[Tzp])
                    S.tt("pool", z_sb[:], z_sb[:], zp[:], ALU.add, r=[Tzs, Tzp], w=[Tzs])
                    sbs.append((z_sb, Tzs))
                (r_sb, Trs), (k_sb, Tks), (v_sb, Tvs) = sbs
                yield "S"
                w_ps, Twp = pring.next()
                S.mm(w_ps[:], lw[b2][0:64, a0:a1], lup[0:64, :], True, True, r=[Tl[b2], Tw], w=[Twp])
                a_ps, Tap = pring.next()
                S.mm(a_ps[:], lw[b2][64:128, a0:a1], lup[64:128, :], True, True, r=[Tl[b2], Tw], w=[Tap])
                Lt, TLt = f32r.next()
                S.tt("dve", Lt[:], w_ps[:], rows["rwkv_w0"][:], ALU.add, r=[Twp, Tw], w=[TLt])
                S.act(Lt[:], Lt[:], AF.Sigmoid, r=[TLt], w=[TLt])
                asg, Tas = f32r.next()
                S.tt("dve", asg[:], a_ps[:], rows["rwkv_a0"][:], ALU.add, r=[Tap, Tw], w=[Tas])
                S.act(asg[:], asg[:], AF.Sigmoid, r=[Tas], w=[Tas])
                if own:
                    g_ps, Tgp = pring.next()
                    S.mm(g_ps[:], lg[b2][:, a0:a1], gup[:, :], True, True, r=[Tl[b2], Tw], w=[Tgp])
                    g_sb, Tgs = b16r.next()
                    S.cp("act", g_sb[:], g_ps[:], r=[Tgp], w=[Tgs])
                yield "S"
                kk, Tkk = f32r.next()
                S.tt("pool", kk[:], k_sb[:], rows["rwkv_k_k"][:], ALU.mult, r=[Tks, Tw], w=[Tkk])
                tmp, Ttmp = f32r.next()
                S.tt("pool", tmp[:], kk[:], kk[:], ALU.mult, r=[Tkk], w=[Ttmp])
                sm, Tsm = smr.next()
                S.op("dve", lambda e, o=sm, i=tmp: e.tensor_reduce(out=o[:, 0:8], in_=i[:].rearrange("p (h j) -> p h j", h=8), axis=AX.X, op=ALU.add), r=[Ttmp], w=[Tsm])
                S.ts("dve", sm[:, 0:8], sm[:, 0:8], 1e-24, None, ALU.max, None, r=[Tsm], w=[Tsm])
                S.act(sm[:, 0:8], sm[:, 0:8], AF.Sqrt, r=[Tsm], w=[Tsm])
                S.op("dve", lambda e, o=sm: e.reciprocal(out=o[:, 0:8], in_=o[:, 0:8]), r=[Tsm], w=[Tsm])
                kk3 = kk[:].rearrange("p (h j) -> p h j", h=8)
                S.tt("dve", kk3, kk3, sm[:, 0:8].unsqueeze(2).to_broadcast([128, 8, 64]), ALU.mult, r=[Tkk, Tsm], w=[Tkk])
                yield "S"
                km, Tkm = f32r.next()
                S.stt(km[:], asg[:], -1.0, rows["rwkv_k_a"][:], ALU.add, ALU.mult, r=[Tas, Tw], w=[Tkm])
                S.stt(km[:], km[:], 1.0, k_sb[:], ALU.add, ALU.mult, r=[Tkm, Tks], w=[Tkm])
                yield "S"
                bv, Tbv = f32r.next()
                S.tt("pool", bv[:], kk[:], asg[:], ALU.mult, r=[Tkk, Tas], w=[Tbv])
                yield "S"
                if own:
                    bt_, Tbt = f32r.next()
                    S.tt("pool", bt_[:], r_sb[:], km[:], ALU.mult, r=[Trs, Tkm], w=[Tbt])
                    S.tt("pool", bt_[:], bt_[:], rows["rwkv_r_k"][:], ALU.mult, r=[Tbt, Tw], w=[Tbt])
                    S.op("dve", lambda e, o=sm, i=bt_: e.tensor_reduce(out=o[:, 8:16], in_=i[:].rearrange("p (h j) -> p h j", h=8), axis=AX.X, op=ALU.add), r=[Tbt], w=[Tsm])
                yield "S"
                c_ps, Tcp = pring.next()
                S.mm(c_ps[:], cst["tri_incl"][:], Lt[:], True, True, r=[Tw, TLt], w=[Tcp])
                x_ps, Txp = pring.next()
                S.mm(x_ps[:], cst["tri_excl"][:], Lt[:], True, True, r=[Tw, TLt], w=[Txp])
                d_ps, Tdp = pring.next()
                S.mm(d_ps[:], cst["tri_after"][:], Lt[:], True, True, r=[Tw, TLt], w=[Tdp])
                EL, TEL = f32r.next()
                ENL, TENL = f32r.next()
                ELm, TELm = f32r.next()
                EG, TEG = f32r.next()
                S.act(EL[:], c_ps[:], AF.Exp, r=[Tcp], w=[TEL], scale=SG_C)
                S.act(ENL[:], c_ps[:], AF.Exp, r=[Tcp], w=[TENL], scale=-SG_C)
                S.act(ELm[:], x_ps[:], AF.Exp, r=[Txp], w=[TELm], scale=SG_C)
                S.act(EG[:], d_ps[:], AF.Exp, r=[Tdp], w=[TEG], scale=SG_C)
                pbg, Tpg = pring.next()
                for p_ in range(4):
                    S.mm(pbg[:, p_ * 2:(p_ + 1) * 2], EL[:, p_ * 128:(p_ + 1) * 128], sel2[:], True, True, r=[TEL, Tw], w=[Tpg])
                gam, Tgam = smr.next()
                S.cp("dve", gam[:, 0:8], pbg[:, 0:8], r=[Tpg], w=[Tgam])
                yield "S"
                RT, TRT = b16r.next()
                KT, TKT = b16r.next()
                BT, TBT = b16r.next()
                AT, TAT = b16r.next()
                BG, TBG = b16r.next()
                KG, TKG = b16r.next()
                Vb, TVb = b16r.next()
                S.tt("dve", RT[:], r_sb[:], EL[:], ALU.mult, r=[Trs, TEL], w=[TRT])
                S.tt("pool", KT[:], km[:], ENL[:], ALU.mult, r=[Tkm, TENL], w=[TKT])
                S.tt("dve", BT[:], bv[:], ENL[:], ALU.mult, r=[Tbv, TENL], w=[TBT])
                S.stt(AT[:], kk[:], -1.0, ELm[:], ALU.mult, ALU.mult, r=[Tkk, TELm], w=[TAT])
                S.tt("pool", BG[:], bv[:], EG[:], ALU.mult, r=[Tbv, TEG], w=[TBG])
                S.tt("dve", KG[:], km[:], EG[:], ALU.mult, r=[Tkm, TEG], w=[TKG])
                S.cp("pool", Vb[:], v_sb[:], r=[Tvs], w=[TVb])
                yield "S"
                XT = {}
                for nm, X, TXq in (("r", RT, TRT), ("k", KT, TKT), ("b", BT, TBT), ("a", AT, TAT)):
                    pb, Tp = pring.next()
                    pbb = pb[:].bitcast(BF16)
                    for p in range(4):
                        S.tr(pbb[:, p * 128:(p + 1) * 128], X[:, p * 128:(p + 1) * 128], identb[:], r=[TXq, Tw], w=[Tp])
                    xT_, TxT_ = xtr.next()
                    ek[0] += 1
                    S.cp(evac_eng(ek[0]), xT_[:], pbb[:, 0:512].rearrange("p (q t) -> p q t", q=4), r=[Tp], w=[TxT_])
                    XT[nm] = (xT_, TxT_)

                yield "XDONE"
                def hsl(h):
                    return slice((h % 2) * 64, (h % 2) * 64 + 64), h // 2

                def mmat(lname, rname, mask, ring=None):
                    lx, Tlx = XT[lname]
                    rx, Trx = XT[rname]
                    M_, TM_ = (ring or mr).next()
                    for par in range(2):
                        pb, Tp = pring.next()
                        for hh in range(4):
                            h = hh * 2 + par
                            ps_, p_ = hsl(h)
                            S.mm(pb[:, hh * 128:(hh + 1) * 128], lx[ps_, p_, :], rx[ps_, p_, :], True, True, r=[Tlx, Trx], w=[Tp])
                        S.tt("dve", M_[:, par:8:2, :], pb[:].rearrange("p (h t) -> p h t", h=4), bc4(cst[mask]), ALU.mult, r=[Tp, Tw], w=[TM_])
                    return M_, TM_

                Mab, TMab = mmat("b", "a", "m_su")
                MabT, TMabT = mmat("a", "b", "m_sl")
                Mak, TMak = mmat("k", "a", "m_su", keepr)
                if own:
                    Mbr, TMbr = mmat("b", "r", "m_iu", keepr)
                    Mkr, TMkr = mmat("k", "r", "m_iu", keepr)

                def hmat(L_, TL_, R_, TR_, add=None):
                    O_, TO_ = mr.next()
                    for half in range(2):
                        pb, Tp = pring.next()
                        for hh in range(4):
                            h = half * 4 + hh
                            S.mm(pb[:, hh * 128:(hh + 1) * 128], L_[:, h, :], R_[:, h, :], True, True, r=[TL_, TR_], w=[Tp])
                        if add is None:
                            ek[0] += 1
                            S.cp(evac_eng(ek[0]), O_[:, half * 4:(half + 1) * 4, :], pb[:].rearrange("p (h t) -> p h t", h=4), r=[Tp], w=[TO_])
                        else:
                            A_, TA_ = add
                            S.tt("dve", O_[:, half * 4:(half + 1) * 4, :], pb[:].rearrange("p (h t) -> p h t", h=4),
                                 A_[:, half * 4:(half + 1) * 4, :], ALU.add, r=[Tp, TA_], w=[TO_])
                    return O_, TO_

                N_, TN_ = Mab, TMab
                NT_, TNT_ = MabT, TMabT
                P_, TP_ = mr.next()
                S.tt("dve", P_[:], N_[:], bc8(identb), ALU.add, r=[TN_, Tw], w=[TP_])
                for lvl in range(5):
                    NT2, TNT2 = hmat(N_, TN_, NT_, TNT_)
                    if lvl < 4:
                        N2, TN2 = hmat(NT_, TNT_, N_, TN_)
                    P_, TP_ = hmat(NT2, TNT2, P_, TP_, add=(P_, TP_))
                    NT_, TNT_ = NT2, TNT2
                    if lvl < 4:
                        N_, TN_ = N2, TN2
                    yield "L"
                Tm, TTm = P_, TP_

                def tokmat(L_, TL_, R_, TR_):
                    pb, Tp = pring.next()
                    for h in range(8):
                        S.mm(pb[:, h * 64:(h + 1) * 64], L_[:, h, :], R_[:, h * 64:(h + 1) * 64], True, True, r=[TL_, TR_], w=[Tp])
                    return pb, Tp

                pb, Tp = tokmat(Tm, TTm, AT, TAT)
                AH, TAH = b16r.next()
                S.cp("act", AH[:], pb[:], r=[Tp], w=[TAH])
                pb, Tp = tokmat(Mak, TMak, Vb, TVb)
                Wm_, TWm_ = b16r.next()
                S.cp("dve", Wm_[:], pb[:], r=[Tp], w=[TWm_])
                pb, Tp = tokmat(Tm, TTm, Wm_, TWm_)
                U0, TU0 = b16r.next()
                S.cp("act", U0[:], pb[:], r=[Tp], w=[TU0])
                if own:
                    pb, Tp = pring.next()
                    for h in range(8):
                        ps_, p_ = hsl(h)
                        S.mm(pb[ps_, p_ * 128:(p_ + 1) * 128], AH[:, h * 64:(h + 1) * 64], Mbr[:, h, :], True, True, r=[TAH, TMbr], w=[Tp])
                    RH, TRH = RHr.next()
                    S.tt("dve", RH[:], pb[:].rearrange("p (q t) -> p q t", q=4), XT["r"][0][:], ALU.add, r=[Tp, XT["r"][1]], w=[TRH])
                    pb, Tp = pring.next()
                    for h in range(8):
                        S.mm(pb[:, h * 64:(h + 1) * 64], Mbr[:, h, :], U0[:, h * 64:(h + 1) * 64], True, False, r=[TMbr, TU0], w=[Tp])
                        S.mm(pb[:, h * 64:(h + 1) * 64], Mkr[:, h, :], Vb[:, h * 64:(h + 1) * 64], False, True, r=[TMkr, TVb], w=[Tp])
                    Y0, TY0 = f32r.next()
                    S.cp("act", Y0[:], pb[:], r=[Tp], w=[TY0])
                GT, TGT = GTr.next()
                for c_ in range(2):
                    pb, Tp = pring.next()
                    for p_ in range(4):
                        S.mm(pb[:, p_ * 128:(p_ + 1) * 128], AH[c_ * 64:(c_ + 1) * 64, p_ * 128:(p_ + 1) * 128],
                             BG[c_ * 64:(c_ + 1) * 64, p_ * 128:(p_ + 1) * 128], True, True, r=[TAH, TBG], w=[Tp])
                    S.tt("dve", GT[:, c_:8:2, :], pb[:].rearrange("p (q t) -> p q t", q=4), bc4(cst["bdmask"]), ALU.mult, r=[Tp, Tw], w=[TGT])
                for pc_ in range(8):
                    S.stt(GT[:, pc_, :], ident[:], gam[:, pc_:pc_ + 1], GT[:, pc_, :], ALU.mult, ALU.add, r=[TGT, Tgam, C.Tconst], w=[TGT])
                Hs, THs = Hsr.next()
                for c_ in range(2):
                    pb, Tp = pring.next()
                    for p_ in range(4):
                        rs_ = slice(c_ * 64, (c_ + 1) * 64)
                        cs_ = slice(p_ * 128, (p_ + 1) * 128)
                        S.mm(pb[:, p_ * 128:(p_ + 1) * 128], BG[rs_, cs_], U0[rs_, cs_], True, False, r=[TBG, TU0], w=[Tp])
                        S.mm(pb[:, p_ * 128:(p_ + 1) * 128], KG[rs_, cs_], Vb[rs_, cs_], False, True, r=[TKG, TVb], w=[Tp])
                    pv = pb[:].rearrange("p (q t) -> p q t", q=4)
                    S.cp("dve", Hs[0:64, c_:8:2, :], pv[0:64, :, 0:64], r=[Tp], w=[THs])
                    S.cp("act", Hs[64:128, c_:8:2, :], pv[64:128, :, 64:128], r=[Tp], w=[THs])
                if own:
                    y_ps0, Typ0 = pring.next()
                    y_ps1, Typ1 = pring.next()
                ST, TST = stv
                for c_ in range(2):
                    if own:
                        for h in range(8):
                            ps_, p_ = hsl(h)
                            ypb, Typb = (y_ps0, Typ0) if h % 2 == 0 else (y_ps1, Typ1)
                            S.mm(ypb[c_ * 64:(c_ + 1) * 64, p_ * 64:(p_ + 1) * 64], RH[ps_, p_, c_ * 64:(c_ + 1) * 64],
                                 ST[ps_, p_, :], True, True, r=[TRH, TST], w=[Typb])
                    s_ps, Tsp = pring.next()
                    for p_ in range(4):
                        S.mm(s_ps[:, p_ * 64:(p_ + 1) * 64], GT[:, p_ * 2 + c_, :], ST[:, p_, :], True, True, r=[TGT, TST], w=[Tsp])
                    STn, TSTn = STr.next()
                    Hv = Hs[:].rearrange("p (q c) i -> p q c i", c=2)
                    S.tt("dve", STn[:], s_ps[:, 0:256].rearrange("p (q i) -> p q i", q=4), Hv[:, :, c_, :], ALU.add, r=[Tsp, THs], w=[TSTn])
                    ST, TST = STn, TSTn
                    stv[0], stv[1] = ST, TST
                if own:
                    y, Ty_ = f32r.next()
                    y3 = y[:].rearrange("p (h j) -> p h j", h=8)
                    Y03 = Y0[:].rearrange("p (h j) -> p h j", h=8)
                    S.tt("dve", y3[:, 0:8:2, :], y_ps0[:, 0:256].rearrange("p (q j) -> p q j", q=4), Y03[:, 0:8:2, :], ALU.add, r=[Typ0, TY0], w=[Ty_])
                    S.tt("dve", y3[:, 1:8:2, :], y_ps1[:, 0:256].rearrange("p (q j) -> p q j", q=4), Y03[:, 1:8:2, :], ALU.add, r=[Typ1, TY0], w=[Ty_])
                    st1, Tst1 = smr.next()
                    S.op("dve", lambda e, o=st1, i=y3: e.tensor_reduce(out=o[:, 0:8], in_=i, axis=AX.X, op=ALU.add), r=[Ty_], w=[Tst1])
                    sq, Tsq = f32r.next()
                    S.tt("pool", sq[:], y[:], y[:], ALU.mult, r=[Ty_], w=[Tsq])
                    S.op("dve", lambda e, o=st1, i=sq: e.tensor_reduce(out=o[:, 8:16], in_=i[:].rearrange("p (h j) -> p h j", h=8), axis=AX.X, op=ALU.add), r=[Tsq], w=[Tst1])
                    S.ts("dve", st1[:, 0:8], st1[:, 0:8], 1.0 / 64, None, ALU.mult, None, r=[Tst1], w=[Tst1])
                    st2, Tst2 = smr.next()
                    S.tt("dve", st2[:, 0:8], st1[:, 0:8], st1[:, 0:8], ALU.mult, r=[Tst1], w=[Tst2])
                    S.stt(st2[:, 0:8], st1[:, 8:16], 1.0 / 64, st2[:, 0:8], ALU.mult, ALU.subtract, r=[Tst1, Tst2], w=[Tst2])
                    S.act(st2[:, 0:8], st2[:, 0:8], AF.Sqrt, r=[Tst2], w=[Tst2], bias=64e-5, scale=1.0)
                    S.op("dve", lambda e, o=st2: e.reciprocal(out=o[:, 0:8], in_=o[:, 0:8]), r=[Tst2], w=[Tst2])
                    S.tt("dve", y3, y3, st1[:, 0:8].unsqueeze(2).to_broadcast([128, 8, 64]), ALU.subtract, r=[Ty_, Tst1], w=[Ty_])
                    S.tt("dve", y3, y3, st2[:, 0:8].unsqueeze(2).to_broadcast([128, 8, 64]), ALU.mult, r=[Ty_, Tst2], w=[Ty_])
                    S.tt("pool", y[:], y[:], rows["rwkv_gn_g"][:], ALU.mult, r=[Ty_, Tw], w=[Ty_])
                    S.tt("pool", y[:], y[:], rows["rwkv_gn_b"][:], ALU.add, r=[Ty_, Tw], w=[Ty_])
                    S.tt("dve", sq[:].rearrange("p (h j) -> p h j", h=8), Vb[:].rearrange("p (h j) -> p h j", h=8),
                         sm[:, 8:16].unsqueeze(2).to_broadcast([128, 8, 64]), ALU.mult, r=[TVb, Tsm], w=[Tsq])
                    S.tt("pool", y[:], y[:], sq[:], ALU.add, r=[Ty_, Tsq], w=[Ty_])
                    S.tt("dve", y[:], y[:], g_sb[:], ALU.mult, r=[Ty_, Tgs], w=[Ty_])
                    S.dma("sp", d["YR"][(tile - OWN_T) * 128:(tile - OWN_T + 1) * 128, :], y[:], r=[Ty_], w=[Tok()])

        tiles_ = [(g, s) for g in range(SL // 256) for s in range(2)]
        gens = [tile_gen(g, s) for (g, s) in tiles_]

        def run_x(gen):
            for v in gen:
                if v == "XDONE":
                    return

        run_x(gens[0])
        for i in range(len(gens)):
            a = gens[i]
            b = gens[i + 1] if i + 1 < len(gens) else None
            a_done = False
            b_done = b is None
            while not (a_done and b_done):
                if not a_done:
                    try:
                        next(a)
                    except StopIteration:
                        a_done = True
                if not b_done:
                    if next(b) == "XDONE":
                        b_done = True
        S.barrier()


SCALE = 0.125
BIG8 = -240000.0


def nsa_phase(S, C, d):
    nc = C.nc
    SL = C.SL
    NT = SL // 128
    QB0 = NT // 2
    NCMP = SL // 16 - 1
    with ExitStack() as st:
        sb = lambda n, shape, dt: st.enter_context(nc.sbuf_tensor("n_" + n, shape, dt))
        pring = Ring([st.enter_context(nc.psum_tensor("n_ps%d" % i, [128, 512], F32)) for i in range(5)], "nps")
        acc_ps = [st.enter_context(nc.psum_tensor("n_acc%d" % i, [128, 512], F32)) for i in range(3)]
        Tacc = toks(3, "nacc")
        Tc = Tok("nsac")
        identb = sb("identb", [128, 128], BF16)
        S.cp("dve", identb[:], C.ident[:], r=[C.Tconst], w=[Tc])
        ident = C.ident
        jmb = sb("jmb", [128, 128], BF16)
        load_cast(S, "pool", jmb[:], d["jmat"][:, :], w=[Tc])
        RB = sb("RB", [33, 8], F32)
        S.dma("sp", RB[0:32, :], d["rel_bias"][:, :], w=[Tc])
        S.op("act", lambda e: e.mul(out=RB[0:32, :], in_=RB[0:32, :], mul=8.0), r=[Tc], w=[Tc])
        S.memset("dve", RB[32:33, :], BIG8, w=[Tc])
        OHF = sb("OHF", [33, 768], F32)
        S.dma("sp", OHF[:], d["ohf"][:, :], w=[Tc])
        fb_sb = sb("fb", [8, 768], F32)
        for hf in range(2):
            pb, Tp = pring.next()
            S.mm(pb[0:8, 0:384], RB[:, :], OHF[:, hf * 384:(hf + 1) * 384], True, True, r=[Tc], w=[Tp])
            S.cp("dve", fb_sb[:, hf * 384:(hf + 1) * 384], pb[0:8, 0:384], r=[Tp], w=[Tc])
        Tfb = Tok("fbd")
        S.dma("sp", d["FB"][:, :], fb_sb[:], r=[Tc], w=[Tfb])
        FBt = d["FB"].tensor
        Btab = sb("Btab", [128, 2, 3, 512], BF16)
        PatC = sb("PatC", [128, 2, 4, 24], BF16)
        bstg = sb("bstg", [128, 128], F32)
        Tbs = Tok("bstg")
        with nc.allow_non_contiguous_dma(reason="one-time small bias table build"):
            for g in range(2):
                for di, dl in enumerate((0, 1, 4)):
                    for h in range(4):
                        off = (g * 4 + h) * 768 + dl * 128
                        src = bass.AP(FBt, off, [[1, 128], [1, 128]])
                        S.dma("sp", bstg[:], src, r=[Tfb], w=[Tbs])
                        S.cp("dve", Btab[:, g, di, h * 128:(h + 1) * 128], bstg[:], r=[Tbs], w=[Tc])
                for h in range(4):
                    off = (g * 4 + h) * 768 + 96 + 256
                    src = bass.AP(FBt, off, [[1, 128], [-16, 23]])
                    S.dma("sp", bstg[:, 0:23], src, r=[Tfb], w=[Tbs])
                    S.cp("dve", PatC[:, g, h, 0:23], bstg[:, 0:23], r=[Tbs], w=[Tc])
        keyb = sb("keyb", [128, NT], F32)
        S.dma("sp", keyb[:], d["keyb"][:, :], w=[Tc])
        cvrow = sb("cvrow", [1, 512], BF16)
        load_cast(S, "pool", cvrow[:], d["cvrow"][:, :], w=[Tc])
        onesr = sb("onesr", [1, 128], BF16)
        S.memset("dve", onesr[:], 1.0, w=[Tc])
        D0 = sb("D0", [128, 128], F32)
        S.dma("sp", D0[:], d["d0"][:, :], w=[Tc])
        firstr = sb("firstr", [128, 128], F32)
        S.dma("sp", firstr[:], d["firstrow"][:, :], w=[Tc])
        KE = sb("KE", [128, SL], BF16)
        KW = sb("KW", [128, SL], BF16)
        for c0 in range(0, SL, 2048):
            c1 = min(SL, c0 + 2048)
            load_cast(S, "pool", KE[64:128, c0:c1], d["emat2"][:, c0:c1], w=[Tc])
        S.memset("pool", KW[64:128, :], 0.0, w=[Tc])
        oh0 = sb("oh0", [128, 128], BF16)
        S.memset("dve", oh0[:], 0.0, w=[Tc])
        S.memset("dve", oh0[0:1, :], 1.0, w=[Tc])
        cv128 = sb("cv128", [128, 512], BF16)
        S.memset("dve", cv128[:], 0.0, w=[Tc])
        S.cp("dve", cv128[0:1, :], cvrow[:], r=[Tc], w=[Tc])
        w1 = [sb("w1%d" % i, [64, 32, 128], BF16) for i in range(2)]
        w2 = [sb("w2%d" % i, [128, 64], BF16) for i in range(2)]
        peT = [sb("peT%d" % i, [64, 32], BF16) for i in range(2)]
        with nc.allow_non_contiguous_dma(reason="small transposed pe load"):
            for i, (a, b, c) in enumerate((("nsa_w1_k", "nsa_w2_k", "nsa_pe_k"), ("nsa_w1_v", "nsa_w2_v", "nsa_pe_v"))):
                for l0 in range(0, 32, 8):
                    load_cast(S, "pool", w1[i][:, l0:l0 + 8, :], d[a][l0 * 64:(l0 + 8) * 64, :].rearrange("(l d) h -> d l h", d=64), w=[Tc])
                load_cast(S, "pool", w2[i][:, :], d[b][:, :], w=[Tc])
                load_cast(S, "pool", peT[i][:, :], d[c].rearrange("l d -> d l"), w=[Tc])
        cT = sb("cT", [64, SL], BF16)
        vs = sb("vs", [128, NT, 65], BF16)
        vw = sb("vw", [128, NT, 65], BF16)
        KcT = sb("KcT", [128, 512], BF16)
        S.memset("dve", KcT[64:128, :], 0.0, w=[Tc])
        vc = sb("vc", [128, 4, 65], BF16)
        hidb = sb("hidb", [128, 512], BF16)
        Tkv = Tok("kv")
        f32r = Ring([sb("f%d" % i, [128, 512], F32) for i in range(10)], "nf32")
        b16r = Ring([sb("b%d" % i, [128, 512], BF16) for i in range(8)], "nb16")
        pcTr = Ring([sb("pcT%d" % i, [128, 4, 128], BF16) for i in range(3)], "pcT")
        pcbr = Ring([sb("pcb%d" % i, [128, 512], BF16) for i in range(8)], "pcb")
        smr = Ring([sb("s%d" % i, [128, 16], F32) for i in range(12)], "nsm")
        sqr = Ring([sb("q%d" % i, [128, 128], F32) for i in range(10)], "nsq")
        QNs = [sb("QN%d" % i, [128, 2, 512], BF16) for i in range(4)]
        for q_ in QNs:
            S.memset("pool", q_[:], 0.0, w=[Tc])
        nsb = Ring([sb("nsb%d" % i, [128, 2, 128], BF16) for i in range(2)], "nsb")
        TqTs = toks(4, "qT")
        TnsTs = toks(4, "nsT")
        gtr = Ring([sb("gt%d" % i, [128, 24], F32) for i in range(3)], "gt")
        yor = Ring([sb("yo%d" % i, [128, 256], F32) for i in range(2)], "yo")
        otr = Ring([sb("ot%d" % i, [128, 4, 65], F32) for i in range(6)], "ot")

        for g in range(2):
            S.dma("sp", KE[0:64, :], d["KT"][4 + g, :, :], w=[Tkv])
            S.dma("sp", KW[0:64, :], d["KT"][6 + g, :, :], w=[Tkv])
            S.memset("dve", vs[:, :, 64:65], 1.0, w=[Tkv])
            S.memset("dve", vw[:, :, 64:65], 1.0, w=[Tkv])
            S.memset("dve", vc[:, :, 64:65], 1.0, w=[Tkv])
            for t0 in range(0, NT, 8):
                t1 = min(NT, t0 + 8)
                S.dma("sp", vs[:, t0:t1, 0:64], d["VSW"][t0 * 128:t1 * 128, g * 64:(g + 1) * 64].rearrange("(t p) c -> p t c", p=128), w=[Tkv])
                S.dma("sp", vw[:, t0:t1, 0:64], d["VSW"][t0 * 128:t1 * 128, 128 + g * 64:128 + (g + 1) * 64].rearrange("(t p) c -> p t c", p=128), w=[Tkv])
            for i in range(2):
                S.dma("sp", cT[:], d["KT"][i * 2 + g, :, :], w=[Tkv])
                hp, Thp = pring.next()
                src_ap = cT[:]
                for l in range(32):
                    rhs = bass.AP(cT[:].tensor, cT[:, l:l + 1].offset, [list(cT[:].ap[0]), [16, NCMP]])
                    S.mm(hp[:, 0:NCMP], w1[i][:, l, :], rhs, l == 0, l == 31, r=[Tc, Tkv], w=[Thp])
                cp_, Tcp_ = pring.next()
                for l in range(32):
                    S.mm(cp_[:, 0:1], w1[i][:, l, :], peT[i][:, l:l + 1], l == 0, l == 31, r=[Tc], w=[Tcp_])
                cpe, Tcpe = smr.next()
                S.cp("dve", cpe[:, 0:1], cp_[:, 0:1], r=[Tcp_], w=[Tcpe])
                u, Tu = f32r.next()
                S.act(u[:, 0:NCMP], hp[:, 0:NCMP], AF.Identity, r=[Thp, Tcpe], w=[Tu], bias=cpe[:, 0:1], scale=1.0)
                t, Tt = f32r.next()
                S.tt("dve", t[:, 0:NCMP], u[:, 0:NCMP], u[:, 0:NCMP], ALU.mult, r=[Tu], w=[Tt])
                S.ts("dve", t[:, 0:NCMP], t[:, 0:NCMP], 0.044715, 1.0, ALU.mult, ALU.add, r=[Tt], w=[Tt])
                S.tt("dve", t[:, 0:NCMP], t[:, 0:NCMP], u[:, 0:NCMP], ALU.mult, r=[Tt, Tu], w=[Tt])
                S.act(t[:, 0:NCMP], t[:, 0:NCMP], AF.Sigmoid, r=[Tt], w=[Tt], scale=1.5957691216057308)
                S.memset("pool", hidb[:], 0.0, w=[Tkv])
                S.tt("dve", hidb[:, 0:NCMP], t[:, 0:NCMP], u[:, 0:NCMP], ALU.mult, r=[Tt, Tu], w=[Tkv])
                if i == 0:
                    kp, Tkp = pring.next()
                    S.mm(kp[0:64, :], w2[0][:, :], hidb[:, :], True, True, r=[Tc, Tkv], w=[Tkp])
                    S.cp("act", KcT[0:64, :], kp[0:64, :], r=[Tkp], w=[Tkv])
                else:
                    kp, Tkp = pring.next()
                    for c in range(4):
                        S.mm(kp[:, c * 64:(c + 1) * 64], hidb[:, c * 128:(c + 1) * 128], w2[1][:, :], True, True, r=[Tc, Tkv], w=[Tkp])
                    S.cp("act", vc[:, :, 0:64], kp[:, 0:256].rearrange("p (c d) -> p c d", c=4), r=[Tkp], w=[Tkv])
            def block_gen(qb):
                qo = qb - QB0
                QN = QNs[qb % 4]
                TqT = TqTs[qb % 4]
                TnsT = TnsTs[qb % 4]
                for lh in range(2):
                    S.dma("sp", QN[0:64, lh, :].rearrange("p (h q) -> p h q", h=4),
                          d["QT"][g * 4:(g + 1) * 4, :, qo * 128:(qo + 1) * 128].rearrange("h p q -> p h q"), w=[TqT])
                gt_, Tgt = gtr.next()
                S.dma("sp", gt_[:], d["GATE"][qo * 128:(qo + 1) * 128, :], w=[Tgt])
                nvis = min(8 * qb + 7, NCMP)
                c0 = max(0, 8 * qb - 16)
                c1 = nvis
                oc_ps, Toc = acc_ps[0], Tacc[0]
                zc, Tzc = smr.next()
                es = []
                for h in range(4):
                    lp, Tlp = pring.next()
                    S.mm(lp[:, 0:nvis], QN[:, 0, h * 128:(h + 1) * 128], KcT[:, 0:nvis], True, False, r=[TqT, Tkv], w=[Tlp])
                    S.mm(lp[:, c0:c1], identb[:], PatC[:, g, h, c0 - (8 * qb - 16):c1 - (8 * qb - 16)], False, False, r=[Tc], w=[Tlp])
                    S.mm(lp[:, 0:nvis], oh0[:], cv128[:, 0:nvis], False, True, r=[Tc], w=[Tlp])
                    e, Te = f32r.next()
                    if nvis < 512:
                        S.memset("pool", e[:, nvis:512], 0.0, w=[Te])
                    S.act(e[:, 0:nvis], lp[:, 0:nvis], AF.Exp, r=[Tlp], w=[Te, Tzc], scale=SCALE, accum_out=zc[:, h:h + 1])
                    es.append((e, Te))
                pcbs = []
                for h in range(4):
                    e, Te = es[h]
                    pcb, Tpcb = pcbr.next()
                    S.cp("dve", pcb[:], e[:], r=[Te], w=[Tpcb])
                    pcbs.append((pcb, Tpcb))

                def attn_branch(kts, kT_, v_, acc, Tac, bias_fn, extra_fn):
                    LOOK = 2
                    pend = []

                    def logits(kt):
                        lp, Tlp = pring.next()
                        dl = qb - kt
                        bt = bias_fn(dl)
                        if extra_fn is not None:
                            S.mm(lp[:], kT_[:, kt * 128:(kt + 1) * 128], QN[:, kt // 32, :], True, bt is None, r=[Tkv, TqT, TnsT], w=[Tlp])
                        else:
                            S.mm(lp[:], kT_[:, kt * 128:(kt + 1) * 128], QN[:, 0, :], True, bt is None, r=[Tkv, TqT], w=[Tlp])
                        if bt is not None:
                            S.mm(lp[:], jmb[:], bt, False, True, r=[Tc], w=[Tlp])
                        pT, TpT = b16r.next()
                        S.act(pT[:], lp[:], AF.Exp, r=[Tlp, Tc], w=[TpT], scale=SCALE, bias=keyb[:, kt:kt + 1])
                        pend.append((kt, pT, TpT))

                    def pv(first, last):
                        kt, pT, TpT = pend.pop(0)
                        for h in range(4):
                            S.mm(acc[:, h * 65:(h + 1) * 65], pT[:, h * 128:(h + 1) * 128], v_[:, kt, :], first and h == 0, last and h == 3,
                                 r=[TpT, Tkv], w=[Tac])

                    n = len(kts)
                    for i in range(min(LOOK, n)):
                        logits(kts[i])
                    for i in range(n):
                        if i + LOOK < n:
                            logits(kts[i + LOOK])
                        pv(i == 0, i == n - 1)

                ow_ps, Tow = acc_ps[2], Tacc[2]
                kt0 = max(0, qb - 4)
                attn_branch(list(range(kt0, qb + 1)), KW, vw, ow_ps, Tow,
                            lambda dl: Btab[:, g, {0: 0, 1: 1, 4: 2}[dl], :] if dl in (0, 1, 4) else None, None)
                pcsum, Tpcs = f32r.next()
                for h in range(4):
                    e, Te = es[h]
                    S.ts("dve", zc[:, h:h + 1], zc[:, h:h + 1], 1e-30, None, ALU.max, None, r=[Tzc], w=[Tzc])
                    S.op("dve", lambda e_, o=zc, hh=h: e_.reciprocal(out=o[:, 4 + hh:5 + hh], in_=o[:, hh:hh + 1]), r=[Tzc], w=[Tzc])
                    if h == 0:
                        S.ts("dve", pcsum[:], e[:], zc[:, 4:5], None, ALU.mult, None, r=[Te, Tzc], w=[Tpcs])
                    else:
                        S.stt(pcsum[:], e[:], zc[:, 4 + h:5 + h], pcsum[:], ALU.mult, ALU.add, r=[Te, Tzc, Tpcs], w=[Tpcs])
                imp, Timp = sqr.next()
                pc3 = pcsum[:].rearrange("p (j r) -> p j r", r=4)
                S.op("dve", lambda e_, o=imp, i=pc3: e_.tensor_reduce(out=o[:], in_=i, axis=AX.X, op=ALU.add), r=[Tpcs], w=[Timp])
                S.tt("dve", imp[:, 1:128], imp[:, 1:128], pc3[:, 0:127, 3], ALU.add, r=[Tpcs, Timp], w=[Timp])
                val, Tval = sqr.next()
                S.ts("dve", val[:], D0[:], float(2 * qb), None, ALU.is_le, None, r=[Tc], w=[Tval])
                fc, Tfc = sqr.next()
                S.ts("dve", fc[:], D0[:], float(2 * qb - 1), None, ALU.is_ge, None, r=[Tc], w=[Tfc])
                S.tt("dve", fc[:], fc[:], val[:], ALU.mult, r=[Tfc, Tval], w=[Tfc])
                S.tt("dve", fc[:], fc[:], firstr[:], ALU.add, r=[Tfc, Tc], w=[Tfc])
                sc, Tsc = sqr.next()
                S.stt(sc[:], fc[:], 1e4, imp[:], ALU.mult, ALU.max, r=[Tfc, Timp], w=[Tsc])
                S.stt(sc[:], sc[:], 1.0, val[:], ALU.add, ALU.mult, r=[Tsc, Tval], w=[Tsc])
                S.ts("dve", sc[:], sc[:], -1.0, None, ALU.add, None, r=[Tsc], w=[Tsc])
                m8, Tm8 = smr.next()
                S.op("dve", lambda e_, o=m8, i=sc: e_.max(out=o[:, 0:8], in_=i[:]), r=[Tsc], w=[Tm8])
                sc2, Tsc2 = sqr.next()
                S.op("dve", lambda e_, o=sc2, a=m8, i=sc: e_.match_replace(out=o[:], in_to_replace=a[:, 0:8], in_values=i[:], imm_value=-2.0), r=[Tsc, Tm8], w=[Tsc2])
                S.op("dve", lambda e_, o=m8, i=sc2: e_.max(out=o[:, 8:16], in_=i[:]), r=[Tsc2], w=[Tm8])
                nsl, Tnsl = nsb.next()
                S.ts("dve", sc2[:], sc[:], m8[:, 15:16], None, ALU.is_ge, None, r=[Tsc, Tm8], w=[Tsc2])
                S.ts("dve", nsl[:, 0, :], sc2[:], -1.0, -BIG8, ALU.add, ALU.mult, r=[Tsc2], w=[Tnsl])
                S.cp("dve", nsl[:, 1, 0:64], nsl[:, 0, 64:128], r=[Tnsl], w=[Tnsl])
                S.cp("dve", nsl[:, 1, 64:128], nsl[:, 0, 0:64], r=[Tnsl], w=[Tnsl])
                yield "A"
                tm_, Ttm = pring.next()
                tmb = tm_[:].bitcast(BF16)
                S.tr(tmb[:, 0:128], nsl[:, 0, :], identb[:], r=[Tnsl, Tc], w=[Ttm])
                S.tr(tmb[:, 128:256], nsl[:, 1, :], identb[:], r=[Tnsl, Tc], w=[Ttm])
                S.cp("dve", QN[64:128, 1, :].rearrange("p (h q) -> p h q", h=4), tmb[64:128, 0:128].unsqueeze(1).to_broadcast([64, 4, 128]), r=[Ttm], w=[TnsT])
                S.cp("dve", QN[64:128, 0, :].rearrange("p (h q) -> p h q", h=4), tmb[64:128, 128:256].unsqueeze(1).to_broadcast([64, 4, 128]), r=[Ttm], w=[TnsT])
                for h in range(4):
                    pcb, Tpcb = pcbs[h]
                    tp_, Ttp = pring.next()
                    tpb = tp_[:].bitcast(BF16)
                    for c in range(4):
                        S.tr(tpb[:, c * 128:(c + 1) * 128], pcb[:, c * 128:(c + 1) * 128], identb[:], r=[Tpcb, Tc], w=[Ttp])
                    pcT, TpcT = pcTr.next()
                    S.cp("dve", pcT[:], tpb[:, 0:512].rearrange("p (c q) -> p c q", c=4), r=[Ttp], w=[TpcT])
                    for c in range(4):
                        S.mm(oc_ps[:, h * 65:(h + 1) * 65], pcT[:, c, :], vc[:, c, :], c == 0, c == 3, r=[TpcT, Tkv], w=[Toc])
                ow_sb, Tows = otr.next()
                S.cp("dve", ow_sb[:], ow_ps[:, 0:260].rearrange("p (h c) -> p h c", h=4), r=[Tow], w=[Tows])
                oc_sb, Tocs = otr.next()
                S.cp("dve", oc_sb[:], oc_ps[:, 0:260].rearrange("p (h c) -> p h c", h=4), r=[Toc], w=[Tocs])
                yield "B"
                os_ps, Tos = acc_ps[1], Tacc[1]

                attn_branch(list(range(qb + 1)), KE, vs, os_ps, Tos,
                            lambda dl: Btab[:, g, dl, :] if dl <= 1 else None, True)
                os_sb, Toss = otr.next()
                S.cp("act", os_sb[:], os_ps[:, 0:260].rearrange("p (h c) -> p h c", h=4), r=[Tos], w=[Toss])
                yield "S"
                yo, Tyo = yor.next()
                yo3 = yo[:].rearrange("p (h c) -> p h c", h=4)
                cf, Tcf = smr.next()
                g3 = gt_[:].rearrange("p (h b) -> p h b", b=3)
                for bi, (osb, Tosb) in enumerate(((oc_sb, Tocs), (os_sb, Toss), (ow_sb, Tows))):
                    S.ts("dve", cf[:, bi * 4:(bi + 1) * 4], osb[:, :, 64], 1e-30, None, ALU.max, None, r=[Tosb], w=[Tcf])
                    S.op("dve", lambda e_, o=cf, b_=bi: e_.reciprocal(out=o[:, b_ * 4:(b_ + 1) * 4], in_=o[:, b_ * 4:(b_ + 1) * 4]), r=[Tcf], w=[Tcf])
                    S.tt("dve", cf[:, bi * 4:(bi + 1) * 4], cf[:, bi * 4:(bi + 1) * 4], g3[:, g * 4:(g + 1) * 4, bi], ALU.mult, r=[Tcf, Tgt], w=[Tcf])
                    cfb = cf[:, bi * 4:(bi + 1) * 4].unsqueeze(2).to_broadcast([128, 4, 64])
                    if bi == 0:
                        S.tt("dve", yo3, osb[:, :, 0:64], cfb, ALU.mult, r=[Tosb, Tcf], w=[Tyo])
                    else:
                        S.tt("dve", osb[:, :, 0:64], osb[:, :, 0:64], cfb, ALU.mult, r=[Tosb, Tcf], w=[Tosb])
                        S.tt("dve", yo3, yo3, osb[:, :, 0:64], ALU.add, r=[Tosb, Tyo], w=[Tyo])
                S.dma("pool", d["YN"][qo * 128:(qo + 1) * 128, g * 256:(g + 1) * 256], yo[:], r=[Tyo], w=[Tok()])

            gens = [block_gen(qb) for qb in range(QB0, NT)]
            n_ = len(gens)
            next(gens[0])
            next(gens[0])
            for i in range(n_):
                if i + 1 < n_:
                    next(gens[i + 1])
                next(gens[i])
                if i + 1 < n_:
                    next(gens[i + 1])
                for _ in gens[i]:
                    pass
        S.barrier()


def post_phase(S, C, d):
    nc = C.nc
    SL = C.SL
    OWN = SL // 2
    XS = 1.0 / 16.0
    with ExitStack() as st:
        sb = lambda n, shape, dt: st.enter_context(nc.sbuf_tensor("p_" + n, shape, dt))
        pring = Ring([st.enter_context(nc.psum_tensor("p_ps%d" % i, [128, 512], F32)) for i in range(8)], "pps")
        Tw = Tok("pw")
        ident = C.ident
        wout = sb("wout", [128, 8, D], BF16)
        wq = sb("wq", [128, 8, D], BF16)
        wo = sb("wo", [128, 8, D], BF16)
        for dc in range(8):
            load_cast(S, "pool", wout[:, dc, :], d["mix_w_out"][dc * 128:(dc + 1) * 128, :], w=[Tw])
            load_cast(S, "pool", wq[:, dc, :], d["xattn_wq"][dc * 128:(dc + 1) * 128, :], w=[Tw])
            load_cast(S, "pool", wo[:, dc, :], d["xattn_wo"][dc * 128:(dc + 1) * 128, :], w=[Tw])
        lnp = {}
        for n in ["ln2_g", "ln2_b", "ln3_g", "ln3_b"]:
            lnp[n] = sb(n, [128, D], F32)
            S.dma("sp", lnp[n][:], d[n].partition_broadcast(128), w=[C.Tconst])
        KT = sb("KT", [128, 8, 256], BF16)
        V = sb("V", [128, 2, 4, 257], BF16)
        with ExitStack() as st2:
            sb2 = lambda n, shape, dt: st2.enter_context(nc.sbuf_tensor("p2_" + n, shape, dt))
            wk = sb2("wk", [128, 8, D], BF16)
            wv = sb2("wv", [128, 8, D], BF16)
            for dc in range(8):
                load_cast(S, "pool", wk[:, dc, :], d["xattn_wk"][dc * 128:(dc + 1) * 128, :], w=[Tw])
                load_cast(S, "pool", wv[:, dc, :], d["xattn_wv"][dc * 128:(dc + 1) * 128, :], w=[Tw])
            mt_ = sb2("mem", [128, 2, D], F32)
            Tm = Tok("mem")
            memT = sb2("memT", [128, 8, 256], BF16)
            TmT = Tok("memT")
            S.dma("sp", mt_[:], d["mem"].rearrange("(s p) d -> p s d", p=128), w=[Tm])
            for s in range(2):
                for hb in range(2):
                    pb, Tp = pring.next()
                    for j in range(4):
                        dc = hb * 4 + j
                        S.tr(pb[:, j * 128:(j + 1) * 128], mt_[:, s, dc * 128:(dc + 1) * 128], ident[:], r=[Tm, C.Tconst], w=[Tp])
                    S.cp("dve", memT[:, hb * 4:(hb + 1) * 4, s * 128:(s + 1) * 128], pb[:].rearrange("p (j t) -> p j t", j=4), r=[Tp], w=[TmT])
            for cc in range(0, 8, 2):
                pb, Tp = pring.next()
                for k2 in range(2):
                    for dc in range(8):
                        S.mm(pb[:, k2 * 256:(k2 + 1) * 256], wk[:, dc, (cc + k2) * 128:(cc + k2 + 1) * 128], memT[:, dc, :], dc == 0, dc == 7, r=[Tw, TmT], w=[Tp])
                S.cp("act", KT[:, cc:cc + 2, :], pb[:].rearrange("p (k t) -> p k t", k=2), r=[Tp], w=[Tw])
            S.memset("dve", V[:, :, :, 256:257], 1.0, w=[Tw])
            for mt in range(2):
                for nb in range(2):
                    pb, Tp = pring.next()
                    for dc in range(8):
                        S.mm(pb[:], memT[:, dc, mt * 128:(mt + 1) * 128], wv[:, dc, nb * 512:(nb + 1) * 512], dc == 0, dc == 7, r=[Tw, TmT], w=[Tp])
                    S.cp("act", V[:, mt, nb * 2:(nb + 1) * 2, 0:256], pb[:].rearrange("p (h c) -> p h c", h=2), r=[Tp], w=[Tw])
            S.barrier()
        ycat = sb("ycat", [128, 2, D], F32)
        Tyc = Tok("ycat")
        x1t = sb("x1t", [128, 2, D], F32)
        Tx1 = Tok("x1t")
        x2t = sb("x2t", [128, 2, D], F32)
        Tx2 = Tok("x2t")
        oat = sb("oat", [128, 2, D], F32)
        Toa = Tok("oat")
        x3t = sb("x3t", [128, 2, D], F32)
        Tx3 = Tok("x3t")
        yT = sb("yT", [128, 8, 256], BF16)
        TyT = Tok("yT")
        x2T = sb("x2T", [128, 8, 256], BF16)
        Tx2T = Tok("x2T")
        oT = sb("oT", [128, 8, 256], BF16)
        ToT = Tok("oT")
        QxT = sb("QxT", [128, 8, 256], BF16)
        TQx = Tok("QxT")
        pTr = Ring([sb("pT%d" % i, [128, 2, 256], BF16) for i in range(2)], "ppT")
        rr = sb("rr", [128, D], F32)
        Trr = Tok("rr")
        lnt = {"st": sb("lnst", [128, 2, 6], F32), "mv": sb("lnmv", [128, 2], F32), "rstd": sb("lnrs", [128, 1], F32), "T": Tok("lnt")}
        zr = Ring([sb("z%d" % i, [128, 4], F32) for i in range(4)], "pz")

        def transp(src, Tsrc, dst, Tdst):
            k = 0
            for s in range(2):
                for hb in range(2):
                    pb, Tp = pring.next()
                    for j in range(4):
                        dc = hb * 4 + j
                        S.tr(pb[:, j * 128:(j + 1) * 128], src[:, s, dc * 128:(dc + 1) * 128], ident[:], r=[Tsrc, C.Tconst], w=[Tp])
                    k += 1
                    S.cp("act" if k % 2 else "dve", dst[:, hb * 4:(hb + 1) * 4, s * 128:(s + 1) * 128],
                         pb[:].rearrange("p (j t) -> p j t", j=4), r=[Tp], w=[Tdst])

        def proj_res_ln(xT_, TxT_, w_, res, Tres, gname, bname, out_t, Tout):
            for s in range(2):
                for nb in range(2):
                    pb, Tp = pring.next()
                    for dc in range(8):
                        S.mm(pb[:], xT_[:, dc, s * 128:(s + 1) * 128], w_[:, dc, nb * 512:(nb + 1) * 512], dc == 0, dc == 7, r=[TxT_, Tw], w=[Tp])
                    S.stt(rr[:, nb * 512:(nb + 1) * 512], res[:, s, nb * 512:(nb + 1) * 512], ALPHA, pb[:], ALU.mult, ALU.add, r=[Tres, Tp], w=[Trr])
                layer_norm_tile(S, C, rr[:], out_t[:, s, :], lnp[gname], lnp[bname], None, Trr, Tout, lnt)

        for g in range(OWN // 256):
            r0 = g * 256
            S.dma("sp", ycat[:, :, 0:512], d["YR"][r0:r0 + 256, :].rearrange("(s p) c -> p s c", p=128), w=[Tyc])
            S.dma("sp", ycat[:, :, 512:1024], d["YN"][r0:r0 + 256, :].rearrange("(s p) c -> p s c", p=128), w=[Tyc])
            S.dma("sp", x1t[:], d["x1"][OWN + r0:OWN + r0 + 256, :].rearrange("(s p) c -> p s c", p=128), w=[Tx1])
            transp(ycat, Tyc, yT, TyT)
            proj_res_ln(yT, TyT, wout, x1t, Tx1, "ln2_g", "ln2_b", x2t, Tx2)
            transp(x2t, Tx2, x2T, Tx2T)
            for cc in range(0, 8, 2):
                pb, Tp = pring.next()
                for k2 in range(2):
                    for dc in range(8):
                        S.mm(pb[:, k2 * 256:(k2 + 1) * 256], wq[:, dc, (cc + k2) * 128:(cc + k2 + 1) * 128], x2T[:, dc, :], dc == 0, dc == 7, r=[Tw, Tx2T], w=[Tp])
                S.cp("act", QxT[:, cc:cc + 2, :], pb[:].rearrange("p (k t) -> p k t", k=2), r=[Tp], w=[TQx])
            for hd in range(4):
                lp, Tlp = pring.next()
                for mt in range(2):
                    for cc in range(2):
                        S.mm(lp[:, mt * 256:(mt + 1) * 256], KT[:, hd * 2 + cc, mt * 128:(mt + 1) * 128], QxT[:, hd * 2 + cc, :], cc == 0, cc == 1, r=[Tw, TQx], w=[Tlp])
                pT, TpT = pTr.next()
                S.act(pT[:], lp[:].rearrange("p (m q) -> p m q", m=2), AF.Exp, r=[Tlp], w=[TpT], scale=XS)
                for s in range(2):
                    op_, Top = pring.next()
                    for mt in range(2):
                        S.mm(op_[:, 0:257], pT[:, mt, s * 128:(s + 1) * 128], V[:, mt, hd, :], mt == 0, mt == 1, r=[TpT, Tw], w=[Top])
                    z, Tz = zr.next()
                    S.op("dve", lambda e_, o=z, i=op_: e_.reciprocal(out=o[:, 0:1], in_=i[:, 256:257]), r=[Top], w=[Tz])
                    S.ts("dve", oat[:, s, hd * 256:(hd + 1) * 256], op_[:, 0:256], z[:, 0:1], None, ALU.mult, None, r=[Top, Tz], w=[Toa])
            transp(oat, Toa, oT, ToT)
            proj_res_ln(oT, ToT, wo, x2t, Tx2, "ln3_g", "ln3_b", x3t, Tx3)
            S.dma("sp", d["X3"][r0:r0 + 256, :].rearrange("(s p) c -> p s c", p=128), x3t[:], r=[Tx3], w=[Tok()])
        S.barrier()


def host_consts():
    t = np.arange(128)
    same = (t[:, None] // 64) == (t[None, :] // 64)
    c = {}
    c["ident"] = np.eye(128, dtype=np.float32)
    c["jmat"] = np.ascontiguousarray(np.eye(128, dtype=np.float32)[::-1])
    c["tri_incl"] = (same & (t[:, None] <= t[None, :])).astype(np.float32)
    c["tri_excl"] = (same & (t[:, None] < t[None, :])).astype(np.float32)
    c["tri_after"] = (same & (t[:, None] > t[None, :])).astype(np.float32)
    c["m_su"] = c["tri_excl"].copy()
    c["m_sl"] = c["tri_after"].copy()
    c["m_iu"] = c["tri_incl"].copy()
    c["bdmask"] = same.astype(np.float32)
    sel2 = np.zeros((128, 2), np.float32)
    sel2[63, 0] = 1.0
    sel2[127, 1] = 1.0
    c["sel2"] = sel2
    return c


def t5_bucket_np(dist):
    n = np.maximum(dist, 0)
    nf = np.maximum(n, 16).astype(np.float32)
    large = 16 + (np.log(nf / np.float32(16)) / np.float32(math.log(128 / 16)) * np.float32(16)).astype(np.int32)
    large = np.minimum(large, 31)
    return np.where(n < 16, n, large)


def host_consts2(SL, pad):
    c = {}
    NT = SL // 128
    dist = np.arange(768) - 127
    bk = t5_bucket_np(dist)
    ohf = np.zeros((33, 768), np.float32)
    ohf[bk, np.arange(768)] = 1.0
    ohf[31, :] -= 1.0
    ohf[32, :] = ((dist < 0) | (dist >= 512)).astype(np.float32)
    c["ohf"] = ohf
    tok = np.arange(SL)
    keyb = np.where(tok >= pad, 0.0, NEG).astype(np.float32).reshape(NT, 128).T
    c["keyb"] = np.ascontiguousarray(keyb)
    n = np.arange(512)
    c["cvrow"] = np.where((16 * n >= pad) & (n < SL // 16 - 1), 0.0, BIG8).astype(np.float32).reshape(1, 512)
    i = np.arange(128)
    c["d0"] = (np.arange(128)[None, :] - (i[:, None] >= 64)).astype(np.float32)
    fr = np.zeros((128, 128), np.float32)
    fr[:, pad // 64] = 1.0
    c["firstrow"] = fr
    c["emat2"] = ((np.arange(SL)[None, :] // 64) % 64 == np.arange(64)[:, None]).astype(np.float32)
    return c


def const2_shapes(SL):
    return [("ohf", [33, 768]), ("keyb", [128, SL // 128]), ("cvrow", [1, 512]), ("d0", [128, 128]),
            ("firstrow", [128, 128]), ("emat2", [64, SL])]


CONST_SHAPES = [("ident", [128, 128]), ("jmat", [128, 128]), ("tri_incl", [128, 128]), ("tri_excl", [128, 128]), ("tri_after", [128, 128]),
                ("m_su", [128, 128]), ("m_sl", [128, 128]), ("m_iu", [128, 128]), ("bdmask", [128, 128]), ("sel2", [128, 2])]

WEIGHT_SHAPES = [
    ("ffn1_w_gate", [D, DFF]), ("ffn1_w_up", [D, DFF]), ("ffn1_w_down", [DFF, D]), ("ln1_g", [D]), ("ln1_b", [D]),
    ("mix_w_in", [D, 3096]), ("rwkv_mu", [1792]), ("rwkv_w0", [512]), ("rwkv_w_up", [64, 512]), ("rwkv_a0", [512]),
    ("rwkv_a_up", [64, 512]), ("rwkv_g_up", [128, 512]), ("rwkv_k_k", [512]), ("rwkv_k_a", [512]), ("rwkv_r_k", [512]),
    ("rwkv_gn_g", [512]), ("rwkv_gn_b", [512]),
    ("nsa_pe_k", [32, 64]), ("nsa_w1_k", [2048, 128]), ("nsa_w2_k", [128, 64]),
    ("nsa_pe_v", [32, 64]), ("nsa_w1_v", [2048, 128]), ("nsa_w2_v", [128, 64]),
    ("mix_w_out", [D, D]), ("ln2_g", [D]), ("ln2_b", [D]),
    ("xattn_wq", [D, D]), ("xattn_wk", [D, D]), ("xattn_wv", [D, D]), ("xattn_wo", [D, D]),
    ("ln3_g", [D]), ("ln3_b", [D]),
    ("ffn2_w_gate", [D, DFF]), ("ffn2_w_up", [D, DFF]), ("ffn2_w_down", [DFF, D]), ("ln4_g", [D]), ("ln4_b", [D]),
]


def build(SL=8192, upto=99, debug=False, stop=99):
    nc = bass.Bass("TRN2", target_bir_lowering=False)
    C = Ctx()
    C.nc = nc
    C.SL = SL
    C.stop = stop
    NT = SL // 128
    OWN = SL // 2
    ext = lambda n, shape: nc.dram_tensor(n, shape, F32, kind="ExternalInput").ap()
    kind_scr = "ExternalOutput" if debug else "Internal"
    scr = lambda n, shape, dt=F32: nc.dram_tensor(n, shape, dt, kind=kind_scr).ap()
    d = {}
    d["x"] = ext("x", [SL, D])
    d["valid"] = ext("valid", [128, NT])
    for n, shape in CONST_SHAPES + WEIGHT_SHAPES + const2_shapes(SL):
        d[n] = ext(n, shape)
    d["x1"] = scr("x1", [SL, D])
    d["KT"] = scr("KT", [8, 64, SL], BF16)
    d["QT"] = scr("QT", [8, 64, OWN], BF16)
    d["VSW"] = scr("VSW", [SL, 256], BF16)
    d["GATE"] = scr("GATE", [OWN, 24])
    d["YR"] = scr("YR", [OWN, 512])
    d["YN"] = scr("YN", [OWN, 512])
    d["X3"] = scr("X3", [OWN, D])
    d["FB"] = scr("FB", [8, 768])
    d["mem"] = ext("mem", [256, D])
    d["rel_bias"] = ext("rel_bias", [32, 8])
    out = nc.dram_tensor("out", [OWN, D], F32, kind="ExternalOutput").ap()
    with ExitStack() as st:
        S = Sched(nc, st)
        C.S = S
        C.Tconst = Tok("const")
        C.ident = st.enter_context(nc.sbuf_tensor("sb_ident", [128, 128], F32))
        C.valid = st.enter_context(nc.sbuf_tensor("sb_valid", [128, NT], F32))
        S.dma("sp", C.ident[:], d["ident"][:, :], w=[C.Tconst])
        S.dma("sp", C.valid[:], d["valid"][:, :], w=[C.Tconst])
        ffn_phase(S, C, d["x"], d["x1"], d["ffn1_w_gate"], d["ffn1_w_up"], d["ffn1_w_down"],
                  d["ln1_g"], d["ln1_b"], SL // 256, True, "f1")
        if upto >= 2:
            mix_phase(S, C, d)
        if upto >= 3:
            nsa_phase(S, C, d)
        if upto >= 4:
            post_phase(S, C, d)
        if upto >= 5:
            d["out"] = out
            ffn_phase(S, C, d["X3"], out, d["ffn2_w_gate"], d["ffn2_w_up"], d["ffn2_w_down"],
                      d["ln4_g"], d["ln4_b"], OWN // 256, False, "f2")
        S.barrier()
        print("ninst", S.ninst)
    return nc


_NC_CACHE = {}


def kernel(**inputs):
    SL = 8192
    OWN = SL // 2
    x = np.asarray(inputs["x"], dtype=np.float32)
    B = x.shape[0]
    if "nc" not in _NC_CACHE:
        _NC_CACHE["nc"] = build(SL=SL, upto=99, debug=False)
    nc = _NC_CACHE["nc"]
    base = dict(host_consts())
    for n, shp in WEIGHT_SHAPES:
        a = np.asarray(inputs[n], dtype=np.float32)
        base[n] = np.ascontiguousarray(a.reshape(shp))
    base["rel_bias"] = np.ascontiguousarray(np.asarray(inputs["rel_bias"], dtype=np.float32))
    in_maps = []
    for c in range(8):
        b, half = c // 2, c % 2
        pad = OWN if half == 0 else 0
        m = dict(base)
        xl = np.zeros((SL, D), np.float32)
        if half == 0:
            xl[OWN:] = x[b, :OWN]
        else:
            xl[:] = x[b]
        m["x"] = xl
        valid = np.ones((128, SL // 128), np.float32)
        valid[:, :pad // 128] = 0.0
        m["valid"] = valid
        m["mem"] = np.ascontiguousarray(np.asarray(inputs["mem"], dtype=np.float32)[b])
        m.update(host_consts2(SL, pad))
        in_maps.append(m)
    res = run_bass_kernel_spmd(nc, in_maps, core_ids=list(range(8)))
    out = np.zeros((B, SL, D), np.float32)
    for c in range(8):
        b, half = c // 2, c % 2
        out[b, half * OWN:(half + 1) * OWN] = np.asarray(res.results[c]["out"], dtype=np.float32)
    return out
```

```python
import math
from contextlib import ExitStack
import numpy as np
import concourse.bass as bass
import concourse.mybir as mybir
from concourse.bass_utils import run_bass_kernel_spmd

F32 = mybir.dt.float32
BF16 = mybir.dt.bfloat16
AF = mybir.ActivationFunctionType
ALU = mybir.AluOpType
AX = mybir.AxisListType

D = 1024
DFF = 2816
NFC = DFF // 128
LN_EPS = 1e-5
ALPHA = 2.0 ** 0.25
NEG = -30000.0


class Tok:
    __slots__ = ("name", "w", "r")

    def __init__(self, name=""):
        self.name = name
        self.w = None
        self.r = {}


def toks(n, name=""):
    return [Tok(name + str(i)) for i in range(n)]


class Sched:
    NDMA = 24

    def __init__(self, nc, stack, same_engine_sync=True):
        self.nc = nc
        self.engs = {"pe": nc.tensor, "act": nc.scalar, "dve": nc.vector,
                     "pool": nc.gpsimd, "sp": nc.sync}
        self.sem = {}
        self.cnt = {}
        for k in self.engs:
            self.sem[k] = stack.enter_context(nc.semaphore("s_" + k))
            self.cnt[k] = 0
        for i in range(self.NDMA):
            k = "d%d" % i
            self.sem[k] = stack.enter_context(nc.semaphore("s_" + k))
            self.cnt[k] = 0
        self.seen = {k: {} for k in self.engs}
        self.dma_i = {"sw": 0, "hw": 0}
        self.ses = same_engine_sync
        self.ninst = 0

    def _wait(self, eng, deps):
        e = self.engs[eng]
        seen = self.seen[eng]
        for (c, v) in deps:
            if c == eng and (eng == "pe" or eng == "sp" or not self.ses):
                continue
            if seen.get(c, 0) >= v:
                continue
            e.wait_ge(self.sem[c], v)
            seen[c] = v

    def _deps(self, r, w):
        deps = []
        for t in r:
            if t.w is not None:
                deps.append(t.w)
        for t in w:
            if t.w is not None:
                deps.append(t.w)
            for c, v in t.r.items():
                deps.append((c, v))
        return deps

    def op(self, eng, fn, r=(), w=()):
        self._wait(eng, self._deps(r, w))
        ins = fn(self.engs[eng])
        self.cnt[eng] += 1
        v = self.cnt[eng]
        ins.then_inc(self.sem[eng], 1)
        for t in r:
            t.r[eng] = v
        for t in w:
            t.w = (eng, v)
            t.r = {}
        self.ninst += 1

    def dma(self, q, out, in_, r=(), w=(), **kw):
        if q == "pool":
            slot = "d%d" % (self.dma_i["sw"] % 8)
            self.dma_i["sw"] += 1
        else:
            slot = "d%d" % (8 + self.dma_i["hw"] % (self.NDMA - 8))
            self.dma_i["hw"] += 1
        deps = self._deps(r, w)
        if self.cnt[slot] > 0:
            deps.append((slot, self.cnt[slot]))
        self._wait(q, deps)
        ins = self.engs[q].dma_start(out=out, in_=in_, **kw)
        self.cnt[slot] += 16
        v = self.cnt[slot]
        ins.then_inc(self.sem[slot], 16)
        for t in r:
            t.r[slot] = v
        for t in w:
            t.w = (slot, v)
            t.r = {}
        self.ninst += 1

    def barrier(self, engs=("pe", "act", "dve", "pool", "sp")):
        allc = [(c, v) for c, v in self.cnt.items() if v > 0]
        for e in engs:
            self._wait(e, [(c, v) for (c, v) in allc if c != e or e not in ("pe", "sp")])

    def finish(self, tks, eng="sp"):
        deps = [t.w for t in tks if t.w is not None]
        self._wait(eng, deps)

    def mm(self, out, lhsT, rhs, start, stop, r, w):
        self.op("pe", lambda e: e.matmul(out, lhsT=lhsT, rhs=rhs, start=start, stop=stop,
                                         skip_group_check=True), r=r, w=w)

    def tr(self, out, in_, ident, r, w):
        self.op("pe", lambda e: e.transpose(out, in_, ident), r=r, w=w)

    def act(self, out, in_, func, r, w, eng="act", **kw):
        self.op(eng, lambda e: e.activation(out=out, in_=in_, func=func, **kw), r=r, w=w)

    def tt(self, eng, out, in0, in1, op, r, w):
        self.op(eng, lambda e: e.tensor_tensor(out=out, in0=in0, in1=in1, op=op), r=r, w=w)

    def ts(self, eng, out, in0, s1, s2, op0, op1, r, w, **kw):
        if op1 is None:
            self.op(eng, lambda e: e.tensor_scalar(out=out, in0=in0, scalar1=s1, scalar2=None, op0=op0, **kw), r=r, w=w)
        else:
            self.op(eng, lambda e: e.tensor_scalar(out=out, in0=in0, scalar1=s1, scalar2=s2, op0=op0, op1=op1, **kw), r=r, w=w)

    def stt(self, out, in0, scalar, in1, op0, op1, r, w):
        self.op("dve", lambda e: e.scalar_tensor_tensor(out=out, in0=in0, scalar=scalar, in1=in1, op0=op0, op1=op1), r=r, w=w)

    def cp(self, eng, out, in_, r, w):
        if eng == "act":
            self.op("act", lambda e: e.copy(out=out, in_=in_), r=r, w=w)
        else:
            self.op(eng, lambda e: e.tensor_copy(out=out, in_=in_), r=r, w=w)

    def memset(self, eng, ap, val, w):
        self.op(eng, lambda e: e.memset(ap, val), r=(), w=w)


class Ctx:
    pass


def load_cast(S, q, dst, src, w, r=()):
    S.dma("pool", dst, src, r=r, w=w, max_dma_last_dim=4096)


def layer_norm_tile(S, C, r_ap, out_ap, g_t, b_t, valid_ap, Tr, Tout, tmp):
    nc = C.nc
    st, mv, rstd = tmp["st"], tmp["mv"], tmp["rstd"]
    Tst = tmp["T"]
    S.op("dve", lambda e: e.bn_stats(out=st[:, 0, :], in_=r_ap[:, 0:512]), r=[Tr], w=[Tst])
    S.op("dve", lambda e: e.bn_stats(out=st[:, 1, :], in_=r_ap[:, 512:1024]), r=[Tr], w=[Tst])
    S.op("dve", lambda e: e.bn_aggr(out=mv[:], in_=st[:]), r=[Tst], w=[Tst])
    S.act(rstd[:], mv[:, 1:2], AF.Sqrt, r=[Tst], w=[Tst], bias=LN_EPS, scale=1.0)
    S.op("dve", lambda e: e.reciprocal(out=rstd[:], in_=rstd[:]), r=[Tst], w=[Tst])
    S.ts("dve", r_ap, r_ap, mv[:, 0:1], rstd[:, 0:1], ALU.subtract, ALU.mult, r=[Tst, Tr], w=[Tr])
    if valid_ap is not None:
        S.stt(r_ap, r_ap, valid_ap, g_t[:], ALU.mult, ALU.mult, r=[Tr, C.Tconst], w=[Tr])
        S.stt(out_ap, b_t[:], valid_ap, r_ap, ALU.mult, ALU.add, r=[Tr, C.Tconst], w=[Tout])
    else:
        S.tt("dve", r_ap, r_ap, g_t[:], ALU.mult, r=[Tr, C.Tconst], w=[Tr])
        S.tt("dve", out_ap, r_ap, b_t[:], ALU.add, r=[Tr, C.Tconst], w=[Tout])


def load_xT(S, C, src_rows_ap, xt, Txt, xT, TxT, tps, Ttps, ident, TG=2):
    S.dma("sp", xt[:], src_rows_ap.rearrange("(s p) d -> p s d", p=128), w=[Txt])
    make_xT(S, C, xt, Txt, xT, TxT, tps, Ttps, ident, TG)


def make_xT(S, C, xt, Txt, xT, TxT, tps, Ttps, ident, TG=2):
    k = 0
    for s in range(TG):
        for hb in range(2):
            pb = tps[k % 2]
            Tp = Ttps[k % 2]
            k += 1
            for j in range(4):
                dc = hb * 4 + j
                S.tr(pb[:, j * 128:(j + 1) * 128], xt[:, s, dc * 128:(dc + 1) * 128], ident[:], r=[Txt, C.Tconst], w=[Tp])
            eng = "dve" if (k % 2) else "act"
            S.cp(eng, xT[:, hb * 4:(hb + 1) * 4, s * 128:(s + 1) * 128],
                 pb[:].rearrange("p (j t) -> p j t", j=4), r=[Tp], w=[TxT])


def ffn_phase(S, C, src, dst, wg_d, wu_d, wd_d, g_d, b_d, ngroups, use_valid, name):
    nc = C.nc
    TG = 2
    GT = TG * 128
    with ExitStack() as st:
        sb = lambda n, shape, dt: st.enter_context(nc.sbuf_tensor(name + n, shape, dt))
        ps = lambda n: st.enter_context(nc.psum_tensor(name + n, [128, 512], F32))
        wg = sb("wg", [128, 8, DFF], BF16)
        wu = sb("wu", [128, 8, DFF], BF16)
        wd = sb("wd", [128, NFC, D], BF16)
        Twg, Twu, Twd = toks(8, "wg"), toks(8, "wu"), toks(NFC, "wd")
        gt = sb("g", [128, D], F32)
        bt = sb("b", [128, D], F32)
        S.dma("sp", gt[:], g_d.partition_broadcast(128), w=[C.Tconst])
        S.dma("sp", bt[:], b_d.partition_broadcast(128), w=[C.Tconst])
        for dc in range(8):
            load_cast(S, "pool", wg[:, dc, :], wg_d[dc * 128:(dc + 1) * 128, :], w=[Twg[dc]])
            load_cast(S, "pool", wu[:, dc, :], wu_d[dc * 128:(dc + 1) * 128, :], w=[Twu[dc]])
        for fc in range(NFC):
            load_cast(S, "pool", wd[:, fc, :], wd_d[fc * 128:(fc + 1) * 128, :], w=[Twd[fc]])
        xt = [sb("xt%d" % i, [128, TG, D], F32) for i in range(2)]
        Txt = toks(2, "xt")
        xT = [sb("xT%d" % i, [128, 8, GT], BF16) for i in range(2)]
        TxT = toks(2, "xT")
        hT = [sb("hT%d" % i, [128, NFC, GT], BF16) for i in range(2)]
        ThT = toks(2, "hT")
        sg = [sb("sg%d" % i, [128, GT], F32) for i in range(2)]
        Tsg = toks(2, "sg")
        rr = [sb("rr%d" % i, [128, D], F32) for i in range(2)]
        Trr = toks(2, "rr")
        oo = [sb("oo%d" % i, [128, D], F32) for i in range(2)]
        Too = toks(2, "oo")
        lnt = {"st": sb("lnst", [128, 2, 6], F32), "mv": sb("lnmv", [128, 2], F32),
               "rstd": sb("lnrs", [128, 1], F32), "T": Tok("lnt")}
        tps = [ps("tp0"), ps("tp1")]
        Ttps = toks(2, "tp")
        gups = [ps("g0"), ps("g1")]
        Tgu = toks(2, "gu")
        yps = [ps("y0"), ps("y1")]
        Ty = toks(2, "y")
        ti = 0
        for g in range(ngroups):
            b2 = g % 2
            load_xT(S, C, src[g * GT:(g + 1) * GT, :], xt[b2], Txt[b2], xT[b2], TxT[b2], tps, Ttps, C.ident, TG)
            for fc in range(NFC):
                p2 = fc % 2
                for dc in range(8):
                    S.mm(gups[p2][:, 0:GT], wg[:, dc, fc * 128:(fc + 1) * 128], xT[b2][:, dc, :], dc == 0, dc == 7,
                         r=[Twg[dc], TxT[b2]], w=[Tgu[p2]])
                for dc in range(8):
                    S.mm(gups[p2][:, GT:2 * GT], wu[:, dc, fc * 128:(fc + 1) * 128], xT[b2][:, dc, :], dc == 0, dc == 7,
                         r=[Twu[dc], TxT[b2]], w=[Tgu[p2]])
                S.act(sg[p2][:], gups[p2][:, 0:GT], AF.Silu, r=[Tgu[p2]], w=[Tsg[p2]])
                S.tt("dve", hT[b2][:, fc, :], sg[p2][:], gups[p2][:, GT:2 * GT], ALU.mult, r=[Tsg[p2], Tgu[p2]], w=[ThT[b2]])
            for s in range(TG):
                r2 = ti % 2
                ti += 1
                for nb in range(2):
                    for fc in range(NFC):
                        S.mm(yps[nb][:], hT[b2][:, fc, s * 128:(s + 1) * 128], wd[:, fc, nb * 512:(nb + 1) * 512],
                             fc == 0, fc == NFC - 1, r=[ThT[b2], Twd[fc]], w=[Ty[nb]])
                for nb in range(2):
                    S.act(rr[r2][:, nb * 512:(nb + 1) * 512], yps[nb][:], AF.Identity, r=[Ty[nb]], w=[Trr[r2]], scale=0.5)
                S.stt(rr[r2][:], xt[b2][:, s, :], ALPHA, rr[r2][:], ALU.mult, ALU.add, r=[Txt[b2], Trr[r2]], w=[Trr[r2]])
                tile = g * TG + s
                vap = C.valid[:, tile:tile + 1] if use_valid else None
                layer_norm_tile(S, C, rr[r2][:], oo[r2][:], gt, bt, vap, Trr[r2], Too[r2], lnt)
                S.dma("sp", dst[tile * 128:(tile + 1) * 128, :], oo[r2][:], r=[Too[r2]], w=[Tok()])
        S.barrier()


class Ring:
    def __init__(self, tiles, name):
        self.t = tiles
        self.T = toks(len(tiles), name)
        self.i = 0

    def next(self):
        i = self.i % len(self.t)
        self.i += 1
        return self.t[i], self.T[i]


HD = 64
NSA_OFF = 1792
SG_C = -0.6065306597126334


def mix_phase(S, C, d):
    nc = C.nc
    SL = C.SL
    NT = SL // 128
    OWN_T = NT // 2
    with ExitStack() as st:
        sb = lambda n, shape, dt: st.enter_context(nc.sbuf_tensor("m_" + n, shape, dt))
        pring = Ring([st.enter_context(nc.psum_tensor("m_ps%d" % i, [128, 512], F32)) for i in range(8)], "mps")
        Tw = Tok("mixw")
        W = sb("W", [128, 8, 3096], BF16)
        mub = sb("mub", [128, 1536], BF16)
        omb = sb("omb", [128, 1536], BF16)
        mucol = sb("mucol", [128, 4], F32)
        load_cast(S, "pool", mub[:], d["rwkv_mu"][0:1536].partition_broadcast(128), w=[Tw])
        S.ts("dve", omb[:], mub[:], -1.0, 1.0, ALU.mult, ALU.add, r=[Tw], w=[Tw])
        for c_ in range(2):
            S.dma("sp", mucol[:, c_:c_ + 1], d["rwkv_mu"][1536 + c_ * 128:1536 + (c_ + 1) * 128].rearrange("(p o) -> p o", o=1), w=[Tw])
        S.ts("dve", mucol[:, 2:4], mucol[:, 0:2], -1.0, 1.0, ALU.mult, ALU.add, r=[Tw], w=[Tw])
        for dc in range(8):
            load_cast(S, "pool", W[:, dc, :], d["mix_w_in"][dc * 128:(dc + 1) * 128, :], w=[Tw])
        lup = sb("lup", [128, 512], BF16)
        gup = sb("gup", [128, 512], BF16)
        load_cast(S, "pool", lup[0:64, :], d["rwkv_w_up"][:, :], w=[Tw])
        load_cast(S, "pool", lup[64:128, :], d["rwkv_a_up"][:, :], w=[Tw])
        load_cast(S, "pool", gup[:, :], d["rwkv_g_up"][:, :], w=[Tw])
        rows = {}
        for n in ["rwkv_w0", "rwkv_a0", "rwkv_k_k", "rwkv_k_a", "rwkv_r_k", "rwkv_gn_g", "rwkv_gn_b"]:
            rows[n] = sb(n, [128, 512], F32)
            S.dma("sp", rows[n][:], d[n].partition_broadcast(128), w=[Tw])
        cst = {}
        for n in ["tri_incl", "tri_excl", "tri_after", "m_su", "m_sl", "m_iu", "bdmask"]:
            cst[n] = sb(n, [128, 128], F32)
            S.dma("sp", cst[n][:], d[n][:, :], w=[Tw])
        sel2 = sb("sel2", [128, 2], F32)
        S.dma("sp", sel2[:], d["sel2"][:, :], w=[Tw])
        identb = sb("identb", [128, 128], BF16)
        S.cp("dve", identb[:], C.ident[:], r=[C.Tconst], w=[Tw])
        ident = C.ident

        def bc4(t):
            return t[:].unsqueeze(1).to_broadcast([128, 4, 128])

        def bc8(t):
            return t[:].unsqueeze(1).to_broadcast([128, 8, 128])

        xt = sb("xt", [128, 2, D], BF16)
        Txt = Tok("xt")
        xTe = [sb("xTe%d" % i, [128, 8, 257], BF16) for i in range(2)]
        TxTe = toks(2, "xTe")
        f32r = Ring([sb("f%d" % i, [128, 512], F32) for i in range(22)], "f32r")
        b16r = Ring([sb("h%d" % i, [128, 512], BF16) for i in range(21)], "b16r")
        mr = Ring([sb("M%d" % i, [128, 8, 128], BF16) for i in range(8)], "mr")
        keepr = Ring([sb("MK%d" % i, [128, 8, 128], BF16) for i in range(3)], "keepr")
        xtr = Ring([sb("XT%d" % i, [128, 4, 128], BF16) for i in range(8)], "xtr")
        smr = Ring([sb("sm%d" % i, [128, 16], F32) for i in range(12)], "smr")
        lw = [sb("lw%d" % i, [128, 256], BF16) for i in range(2)]
        lg = [sb("lg%d" % i, [128, 256], BF16) for i in range(2)]
        Tl = toks(2, "lora")
        STr = Ring([sb("ST%d" % i, [128, 4, 64], F32) for i in range(3)], "ST")
        GTr = Ring([sb("GT%d" % i, [128, 8, 128], F32) for i in range(1)], "GT")
        Hsr = Ring([sb("Hs%d" % i, [128, 8, 64], F32) for i in range(1)], "Hs")
        RHr = Ring([sb("RH%d" % i, [128, 4, 128], F32) for i in range(1)], "RH")
        kst = sb("kst", [64, 8, 256], BF16)
        Tkst = Tok("kst")
        qst = sb("qst", [64, 8, 256], BF16)
        Tqst = Tok("qst")
        vst = Ring([sb("vst%d" % i, [128, 256], BF16) for i in range(2)], "vst")
        gst = Ring([sb("gst%d" % i, [128, 24], F32) for i in range(2)], "gst")

        ST, TST = STr.next()
        S.memset("dve", ST[:], 0.0, w=[TST])
        stv = [ST, TST]
        S.memset("dve", xTe[1][:, :, 256:257], 0.0, w=[TxTe[1]])

        def evac_eng(k):
            return "act" if k % 2 else "dve"

        ek = [0]

        def tile_gen(g, s):
            b2 = g % 2
            own_g = (g * 2) >= OWN_T
            xcur = lambda dc, a, b, _x=xTe[b2]: _x[:, dc, 1 + a:1 + b]
            xprv = lambda dc, a, b, _x=xTe[b2]: _x[:, dc, a:b]
            TX = TxTe[b2]
            if s == 0:
                load_cast(S, "pool", xt[:], d["x1"][g * 256:(g + 1) * 256, :].rearrange("(s p) d -> p s d", p=128), w=[Txt])
                S.cp("pool", xTe[b2][:, :, 0:1], xTe[1 - b2][:, :, 256:257], r=[TxTe[1 - b2]], w=[TxTe[b2]])
                for s_ in range(2):
                    for hb in range(2):
                        pb, Tp = pring.next()
                        pbb_ = pb[:].bitcast(BF16)
                        for j in range(4):
                            dc = hb * 4 + j
                            S.tr(pbb_[:, j * 128:(j + 1) * 128], xt[:, s_, dc * 128:(dc + 1) * 128], identb[:], r=[Txt, Tw], w=[Tp])
                        ek[0] += 1
                        S.cp(evac_eng(ek[0]), xTe[b2][:, hb * 4:(hb + 1) * 4, 1 + s_ * 128:1 + (s_ + 1) * 128],
                             pbb_[:, 0:512].rearrange("p (j t) -> p j t", j=4), r=[Tp], w=[TxTe[b2]])
                pbz, Tpz = pring.next()
                pbp, Tpp = pring.next()
                for half, c0 in ((0, 1536), (1, 1664)):
                    for dc in range(8):
                        S.mm(pbz[:, half * 256:(half + 1) * 256], W[:, dc, c0:c0 + 128], xcur(dc, 0, 256), dc == 0, dc == 7, r=[Tw, TX], w=[Tpz])
                    for dc in range(8):
                        S.mm(pbp[:, half * 256:(half + 1) * 256], W[:, dc, c0:c0 + 128], xprv(dc, 0, 256), dc == 0, dc == 7, r=[Tw, TX], w=[Tpp])
                lz, Tlz = f32r.next()
                for half in range(2):
                    hs = slice(half * 256, (half + 1) * 256)
                    S.act(lz[:, hs], pbz[:, hs], AF.Identity, r=[Tpz, Tw], w=[Tlz], scale=mucol[:, 2 + half:3 + half])
                    S.stt(lz[:, hs], pbp[:, hs], mucol[:, half:half + 1], lz[:, hs], ALU.mult, ALU.add, r=[Tpp, Tlz, Tw], w=[Tlz])
                S.act(lw[b2][0:64, :], lz[0:64, 0:256], AF.Tanh, r=[Tlz], w=[Tl[b2]])
                S.cp("dve", lw[b2][64:128, :], lz[64:128, 0:256], r=[Tlz], w=[Tl[b2]])
                S.act(lg[b2][:], lz[:, 256:512], AF.Sigmoid, r=[Tlz], w=[Tl[b2]])
                slots = [512, 576, 640, 704, 768, 832, 1024, 1088]
                for bk in range(4):
                    pb, Tp = pring.next()
                    for k2 in range(2):
                        c0 = NSA_OFF + slots[bk * 2 + k2]
                        for dc in range(8):
                            S.mm(pb[0:64, k2 * 256:(k2 + 1) * 256], W[:, dc, c0:c0 + 64], xcur(dc, 0, 256), dc == 0, dc == 7, r=[Tw, TX], w=[Tp])
                    ek[0] += 1
                    S.cp(evac_eng(ek[0]), kst[:, bk * 2:(bk + 1) * 2, :], pb[0:64, :].rearrange("p (k t) -> p k t", k=2), r=[Tp], w=[Tkst])
                for k_ in range(8):
                    S.dma("sp", d["KT"][k_, :, g * 256:(g + 1) * 256], kst[:, k_, :], r=[Tkst], w=[Tok()])
                if own_g:
                    go = g - OWN_T // 2
                    for bk in range(4):
                        pb, Tp = pring.next()
                        for k2 in range(2):
                            c0 = NSA_OFF + (bk * 2 + k2) * 64
                            for dc in range(8):
                                S.mm(pb[0:64, k2 * 256:(k2 + 1) * 256], W[:, dc, c0:c0 + 64], xcur(dc, 0, 256), dc == 0, dc == 7, r=[Tw, TX], w=[Tp])
                        ek[0] += 1
                        S.cp(evac_eng(ek[0]), qst[:, bk * 2:(bk + 1) * 2, :], pb[0:64, :].rearrange("p (k t) -> p k t", k=2), r=[Tp], w=[Tqst])
                    for k_ in range(8):
                        S.dma("sp", d["QT"][k_, :, go * 256:(go + 1) * 256], qst[:, k_, :], r=[Tqst], w=[Tok()])
                yield "G"
            for _once in (0,):
                tile = g * 2 + s
                own = tile >= OWN_T
                a0, a1 = s * 128, (s + 1) * 128
                pb, Tp = pring.next()
                for k2, c0 in ((0, NSA_OFF + 896), (1, NSA_OFF + 1152)):
                    for dc in range(8):
                        S.mm(pb[:, k2 * 128:(k2 + 1) * 128], xcur(dc, a0, a1), W[:, dc, c0:c0 + 128], dc == 0, dc == 7, r=[Tw, TX], w=[Tp])
                if own:
                    for dc in range(8):
                        S.mm(pb[:, 256:280], xcur(dc, a0, a1), W[:, dc, NSA_OFF + 1280:NSA_OFF + 1304], dc == 0, dc == 7, r=[Tw, TX], w=[Tp])
                vt, Tv = vst.next()
                S.cp("act", vt[:], pb[:, 0:256], r=[Tp], w=[Tv])
                S.dma("sp", d["VSW"][tile * 128:(tile + 1) * 128, :], vt[:], r=[Tv], w=[Tok()])
                if own:
                    gt_, Tg_ = gst.next()
                    S.act(gt_[:], pb[:, 256:280], AF.Sigmoid, r=[Tp], w=[Tg_])
                    S.dma("sp", d["GATE"][(tile - OWN_T) * 128:(tile - OWN_T + 1) * 128, :], gt_[:], r=[Tg_], w=[Tok()])
                yield "S"
                sbs = []
                for q in range(3):
                    pbz, Tpz = pring.next()
                    pbp, Tpp = pring.next()
                    for dc in range(8):
                        S.mm(pbz[:], xcur(dc, a0, a1), W[:, dc, q * 512:(q + 1) * 512], dc == 0, dc == 7, r=[Tw, TX], w=[Tpz])
                    for dc in range(8):
                        S.mm(pbp[:], xprv(dc, a0, a1), W[:, dc, q * 512:(q + 1) * 512], dc == 0, dc == 7, r=[Tw, TX], w=[Tpp])
                    z_sb, Tzs = f32r.next()
                    S.tt("dve", z_sb[:], pbz[:], omb[:, q * 512:(q + 1) * 512], ALU.mult, r=[Tpz, Tw], w=[Tzs])
                    zp, Tzp = f32r.next()
                    S.tt("dve", zp[:], pbp[:], mub[:, q * 512:(q + 1) * 512], ALU.mult, r=[Tpp, Tw], w=[Tzp])
                    S.tt("pool", z_sb[:], z_sb[:], zp[:], ALU.add, r=[Tzs, Tzp], w=[Tzs])
                    sbs.append((z_sb, Tzs))
                (r_sb, Trs), (k_sb, Tks), (v_sb, Tvs) = sbs
                yield "S"
                w_ps, Twp = pring.next()
                S.mm(w_ps[:], lw[b2][0:64, a0:a1], lup[0:64, :], True, True, r=[Tl[b2], Tw], w=[Twp])
                a_ps, Tap = pring.next()
                S.mm(a_ps[:], lw[b2][64:128, a0:a1], lup[64:128, :], True, True, r=[Tl[b2], Tw], w=[Tap])
                Lt, TLt = f32r.next()
                S.tt("dve", Lt[:], w_ps[:], rows["rwkv_w0"][:], ALU.add, r=[Twp, Tw], w=[TLt])
                S.act(Lt[:], Lt[:], AF.Sigmoid, r=[TLt], w=[TLt])
                asg, Tas = f32r.next()
                S.tt("dve", asg[:], a_ps[:], rows["rwkv_a0"][:], ALU.add, r=[Tap, Tw], w=[Tas])
                S.act(asg[:], asg[:], AF.Sigmoid, r=[Tas], w=[Tas])
                if own:
                    g_ps, Tgp = pring.next()
                    S.mm(g_ps[:], lg[b2][:, a0:a1], gup[:, :], True, True, r=[Tl[b2], Tw], w=[Tgp])
                    g_sb, Tgs = b16r.next()
                    S.cp("act", g_sb[:], g_ps[:], r=[Tgp], w=[Tgs])
                yield "S"
                kk, Tkk = f32r.next()
                S.tt("pool", kk[:], k_sb[:], rows["rwkv_k_k"][:], ALU.mult, r=[Tks, Tw], w=[Tkk])
                tmp, Ttmp = f32r.next()
                S.tt("pool", tmp[:], kk[:], kk[:], ALU.mult, r=[Tkk], w=[Ttmp])
                sm, Tsm = smr.next()
                S.op("dve", lambda e, o=sm, i=tmp: e.tensor_reduce(out=o[:, 0:8], in_=i[:].rearrange("p (h j) -> p h j", h=8), axis=AX.X, op=ALU.add), r=[Ttmp], w=[Tsm])
                S.ts("dve", sm[:, 0:8], sm[:, 0:8], 1e-24, None, ALU.max, None, r=[Tsm], w=[Tsm])
                S.act(sm[:, 0:8], sm[:, 0:8], AF.Sqrt, r=[Tsm], w=[Tsm])
                S.op("dve", lambda e, o=sm: e.reciprocal(out=o[:, 0:8], in_=o[:, 0:8]), r=[Tsm], w=[Tsm])
                kk3 = kk[:].rearrange("p (h j) -> p h j", h=8)
                S.tt("dve", kk3, kk3, sm[:, 0:8].unsqueeze(2).to_broadcast([128, 8, 64]), ALU.mult, r=[Tkk, Tsm], w=[Tkk])
                yield "S"
                km, Tkm = f32r.next()
                S.stt(km[:], asg[:], -1.0, rows["rwkv_k_a"][:], ALU.add, ALU.mult, r=[Tas, Tw], w=[Tkm])
                S.stt(km[:], km[:], 1.0, k_sb[:], ALU.add, ALU.mult, r=[Tkm, Tks], w=[Tkm])
                yield "S"
                bv, Tbv = f32r.next()
                S.tt("pool", bv[:], kk[:], asg[:], ALU.mult, r=[Tkk, Tas], w=[Tbv])
                yield "S"
                if own:
                    bt_, Tbt = f32r.next()
                    S.tt("pool", bt_[:], r_sb[:], km[:], ALU.mult, r=[Trs, Tkm], w=[Tbt])
                    S.tt("pool", bt_[:], bt_[:], rows["rwkv_r_k"][:], ALU.mult, r=[Tbt, Tw], w=[Tbt])
                    S.op("dve", lambda e, o=sm, i=bt_: e.tensor_reduce(out=o[:, 8:16], in_=i[:].rearrange("p (h j) -> p h j", h=8), axis=AX.X, op=ALU.add), r=[Tbt], w=[Tsm])
                yield "S"
                c_ps, Tcp = pring.next()
                S.mm(c_ps[:], cst["tri_incl"][:], Lt[:], True, True, r=[Tw, TLt], w=[Tcp])
                x_ps, Txp = pring.next()
                S.mm(x_ps[:], cst["tri_excl"][:], Lt[:], True, True, r=[Tw, TLt], w=[Txp])
                d_ps, Tdp = pring.next()
                S.mm(d_ps[:], cst["tri_after"][:], Lt[:], True, True, r=[Tw, TLt], w=[Tdp])
                EL, TEL = f32r.next()
                ENL, TENL = f32r.next()
                ELm, TELm = f32r.next()
                EG, TEG = f32r.next()
                S.act(EL[:], c_ps[:], AF.Exp, r=[Tcp], w=[TEL], scale=SG_C)
                S.act(ENL[:], c_ps[:], AF.Exp, r=[Tcp], w=[TENL], scale=-SG_C)
                S.act(ELm[:], x_ps[:], AF.Exp, r=[Txp], w=[TELm], scale=SG_C)
                S.act(EG[:], d_ps[:], AF.Exp, r=[Tdp], w=[TEG], scale=SG_C)
                pbg, Tpg = pring.next()
                for p_ in range(4):
                    S.mm(pbg[:, p_ * 2:(p_ + 1) * 2], EL[:, p_ * 128:(p_ + 1) * 128], sel2[:], True, True, r=[TEL, Tw], w=[Tpg])
                gam, Tgam = smr.next()
                S.cp("dve", gam[:, 0:8], pbg[:, 0:8], r=[Tpg], w=[Tgam])
                yield "S"
                RT, TRT = b16r.next()
                KT, TKT = b16r.next()
                BT, TBT = b16r.next()
                AT, TAT = b16r.next()
                BG, TBG = b16r.next()
                KG, TKG = b16r.next()
                Vb, TVb = b16r.next()
                S.tt("dve", RT[:], r_sb[:], EL[:], ALU.mult, r=[Trs, TEL], w=[TRT])
                S.tt("pool", KT[:], km[:], ENL[:], ALU.mult, r=[Tkm, TENL], w=[TKT])
                S.tt("dve", BT[:], bv[:], ENL[:], ALU.mult, r=[Tbv, TENL], w=[TBT])
                S.stt(AT[:], kk[:], -1.0, ELm[:], ALU.mult, ALU.mult, r=[Tkk, TELm], w=[TAT])
                S.tt("pool", BG[:], bv[:], EG[:], ALU.mult, r=[Tbv, TEG], w=[TBG])
                S.tt("dve", KG[:], km[:], EG[:], ALU.mult, r=[Tkm, TEG], w=[TKG])
                S.cp("pool", Vb[:], v_sb[:], r=[Tvs], w=[TVb])
                yield "S"
                XT = {}
                for nm, X, TXq in (("r", RT, TRT), ("k", KT, TKT), ("b", BT, TBT), ("a", AT, TAT)):
                    pb, Tp = pring.next()
                    pbb = pb[:].bitcast(BF16)
                    for p in range(4):
                        S.tr(pbb[:, p * 128:(p + 1) * 128], X[:, p * 128:(p + 1) * 128], identb[:], r=[TXq, Tw], w=[Tp])
                    xT_, TxT_ = xtr.next()
                    ek[0] += 1
                    S.cp(evac_eng(ek[0]), xT_[:], pbb[:, 0:512].rearrange("p (q t) -> p q t", q=4), r=[Tp], w=[TxT_])
                    XT[nm] = (xT_, TxT_)

                yield "XDONE"
                def hsl(h):
                    return slice((h % 2) * 64, (h % 2) * 64 + 64), h // 2

                def mmat(lname, rname, mask, ring=None):
                    lx, Tlx = XT[lname]
                    rx, Trx = XT[rname]
                    M_, TM_ = (ring or mr).next()
                    for par in range(2):
                        pb, Tp = pring.next()
                        for hh in range(4):
                            h = hh * 2 + par
                            ps_, p_ = hsl(h)
                            S.mm(pb[:, hh * 128:(hh + 1) * 128], lx[ps_, p_, :], rx[ps_, p_, :], True, True, r=[Tlx, Trx], w=[Tp])
                        S.tt("dve", M_[:, par:8:2, :], pb[:].rearrange("p (h t) -> p h t", h=4), bc4(cst[mask]), ALU.mult, r=[Tp, Tw], w=[TM_])
                    return M_, TM_

                Mab, TMab = mmat("b", "a", "m_su")
                MabT, TMabT = mmat("a", "b", "m_sl")
                Mak, TMak = mmat("k", "a", "m_su", keepr)
                if own:
                    Mbr, TMbr = mmat("b", "r", "m_iu", keepr)
                    Mkr, TMkr = mmat("k", "r", "m_iu", keepr)

                def hmat(L_, TL_, R_, TR_, add=None):
                    O_, TO_ = mr.next()
                    for half in range(2):
                        pb, Tp = pring.next()
                        for hh in range(4):
                            h = half * 4 + hh
                            S.mm(pb[:, hh * 128:(hh + 1) * 128], L_[:, h, :], R_[:, h, :], True, True, r=[TL_, TR_], w=[Tp])
                        if add is None:
                            ek[0] += 1
                            S.cp(evac_eng(ek[0]), O_[:, half * 4:(half + 1) * 4, :], pb[:].rearrange("p (h t) -> p h t", h=4), r=[Tp], w=[TO_])
                        else:
                            A_, TA_ = add
                            S.tt("dve", O_[:, half * 4:(half + 1) * 4, :], pb[:].rearrange("p (h t) -> p h t", h=4),
                                 A_[:, half * 4:(half + 1) * 4, :], ALU.add, r=[Tp, TA_], w=[TO_])
                    return O_, TO_

                N_, TN_ = Mab, TMab
                NT_, TNT_ = MabT, TMabT
                P_, TP_ = mr.next()
                S.tt("dve", P_[:], N_[:], bc8(identb), ALU.add, r=[TN_, Tw], w=[TP_])
                for lvl in range(5):
                    NT2, TNT2 = hmat(N_, TN_, NT_, TNT_)
                    if lvl < 4:
                        N2, TN2 = hmat(NT_, TNT_, N_, TN_)
                    P_, TP_ = hmat(NT2, TNT2, P_, TP_, add=(P_, TP_))
                    NT_, TNT_ = NT2, TNT2
                    if lvl < 4:
                        N_, TN_ = N2, TN2
                    yield "L"
                Tm, TTm = P_, TP_

                def tokmat(L_, TL_, R_, TR_):
                    pb, Tp = pring.next()
                    for h in range(8):
                        S.mm(pb[:, h * 64:(h + 1) * 64], L_[:, h, :], R_[:, h * 64:(h + 1) * 64], True, True, r=[TL_, TR_], w=[Tp])
                    return pb, Tp

                pb, Tp = tokmat(Tm, TTm, AT, TAT)
                AH, TAH = b16r.next()
                S.cp("act", AH[:], pb[:], r=[Tp], w=[TAH])
                pb, Tp = tokmat(Mak, TMak, Vb, TVb)
                Wm_, TWm_ = b16r.next()
                S.cp("dve", Wm_[:], pb[:], r=[Tp], w=[TWm_])
                pb, Tp = tokmat(Tm, TTm, Wm_, TWm_)
                U0, TU0 = b16r.next()
                S.cp("act", U0[:], pb[:], r=[Tp], w=[TU0])
                if own:
                    pb, Tp = pring.next()
                    for h in range(8):
                        ps_, p_ = hsl(h)
                        S.mm(pb[ps_, p_ * 128:(p_ + 1) * 128], AH[:, h * 64:(h + 1) * 64], Mbr[:, h, :], True, True, r=[TAH, TMbr], w=[Tp])
                    RH, TRH = RHr.next()
                    S.tt("dve", RH[:], pb[:].rearrange("p (q t) -> p q t", q=4), XT["r"][0][:], ALU.add, r=[Tp, XT["r"][1]], w=[TRH])
                    pb, Tp = pring.next()
                    for h in range(8):
                        S.mm(pb[:, h * 64:(h + 1) * 64], Mbr[:, h, :], U0[:, h * 64:(h + 1) * 64], True, False, r=[TMbr, TU0], w=[Tp])
                        S.mm(pb[:, h * 64:(h + 1) * 64], Mkr[:, h, :], Vb[:, h * 64:(h + 1) * 64], False, True, r=[TMkr, TVb], w=[Tp])
                    Y0, TY0 = f32r.next()
                    S.cp("act", Y0[:], pb[:], r=[Tp], w=[TY0])
                GT, TGT = GTr.next()
                for c_ in range(2):
                    pb, Tp = pring.next()
                    for p_ in range(4):
                        S.mm(pb[:, p_ * 128:(p_ + 1) * 128], AH[c_ * 64:(c_ + 1) * 64, p_ * 128:(p_ + 1) * 128],
                             BG[c_ * 64:(c_ + 1) * 64, p_ * 128:(p_ + 1) * 128], True, True, r=[TAH, TBG], w=[Tp])
                    S.tt("dve", GT[:, c_:8:2, :], pb[:].rearrange("p (q t) -> p q t", q=4), bc4(cst["bdmask"]), ALU.mult, r=[Tp, Tw], w=[TGT])
                for pc_ in range(8):
                    S.stt(GT[:, pc_, :], ident[:], gam[:, pc_:pc_ + 1], GT[:, pc_, :], ALU.mult, ALU.add, r=[TGT, Tgam, C.Tconst], w=[TGT])
                Hs, THs = Hsr.next()
                for c_ in range(2):
                    pb, Tp = pring.next()
                    for p_ in range(4):
                        rs_ = slice(c_ * 64, (c_ + 1) * 64)
                        cs_ = slice(p_ * 128, (p_ + 1) * 128)
                        S.mm(pb[:, p_ * 128:(p_ + 1) * 128], BG[rs_, cs_], U0[rs_, cs_], True, False, r=[TBG, TU0], w=[Tp])
                        S.mm(pb[:, p_ * 128:(p_ + 1) * 128], KG[rs_, cs_], Vb[rs_, cs_], False, True, r=[TKG, TVb], w=[Tp])
                    pv = pb[:].rearrange("p (q t) -> p q t", q=4)
                    S.cp("dve", Hs[0:64, c_:8:2, :], pv[0:64, :, 0:64], r=[Tp], w=[THs])
                    S.cp("act", Hs[64:128, c_:8:2, :], pv[64:128, :, 64:128], r=[Tp], w=[THs])
                if own:
                    y_ps0, Typ0 = pring.next()
                    y_ps1, Typ1 = pring.next()
                ST, TST = stv
                for c_ in range(2):
                    if own:
                        for h in range(8):
                            ps_, p_ = hsl(h)
                            ypb, Typb = (y_ps0, Typ0) if h % 2 == 0 else (y_ps1, Typ1)
                            S.mm(ypb[c_ * 64:(c_ + 1) * 64, p_ * 64:(p_ + 1) * 64], RH[ps_, p_, c_ * 64:(c_ + 1) * 64],
                                 ST[ps_, p_, :], True, True, r=[TRH, TST], w=[Typb])
                    s_ps, Tsp = pring.next()
                    for p_ in range(4):
                        S.mm(s_ps[:, p_ * 64:(p_ + 1) * 64], GT[:, p_ * 2 + c_, :], ST[:, p_, :], True, True, r=[TGT, TST], w=[Tsp])
                    STn, TSTn = STr.next()
                    Hv = Hs[:].rearrange("p (q c) i -> p q c i", c=2)
                    S.tt("dve", STn[:], s_ps[:, 0:256].rearrange("p (q i) -> p q i", q=4), Hv[:, :, c_, :], ALU.add, r=[Tsp, THs], w=[TSTn])
                    ST, TST = STn, TSTn
                    stv[0], stv[1] = ST, TST
                if own:
                    y, Ty_ = f32r.next()
                    y3 = y[:].rearrange("p (h j) -> p h j", h=8)
                    Y03 = Y0[:].rearrange("p (h j) -> p h j", h=8)
                    S.tt("dve", y3[:, 0:8:2, :], y_ps0[:, 0:256].rearrange("p (q j) -> p q j", q=4), Y03[:, 0:8:2, :], ALU.add, r=[Typ0, TY0], w=[Ty_])
                    S.tt("dve", y3[:, 1:8:2, :], y_ps1[:, 0:256].rearrange("p (q j) -> p q j", q=4), Y03[:, 1:8:2, :], ALU.add, r=[Typ1, TY0], w=[Ty_])
                    st1, Tst1 = smr.next()
                    S.op("dve", lambda e, o=st1, i=y3: e.tensor_reduce(out=o[:, 0:8], in_=i, axis=AX.X, op=ALU.add), r=[Ty_], w=[Tst1])
                    sq, Tsq = f32r.next()
                    S.tt("pool", sq[:], y[:], y[:], ALU.mult, r=[Ty_], w=[Tsq])
                    S.op("dve", lambda e, o=st1, i=sq: e.tensor_reduce(out=o[:, 8:16], in_=i[:].rearrange("p (h j) -> p h j", h=8), axis=AX.X, op=ALU.add), r=[Tsq], w=[Tst1])
                    S.ts("dve", st1[:, 0:8], st1[:, 0:8], 1.0 / 64, None, ALU.mult, None, r=[Tst1], w=[Tst1])
                    st2, Tst2 = smr.next()
                    S.tt("dve", st2[:, 0:8], st1[:, 0:8], st1[:, 0:8], ALU.mult, r=[Tst1], w=[Tst2])
                    S.stt(st2[:, 0:8], st1[:, 8:16], 1.0 / 64, st2[:, 0:8], ALU.mult, ALU.subtract, r=[Tst1, Tst2], w=[Tst2])
                    S.act(st2[:, 0:8], st2[:, 0:8], AF.Sqrt, r=[Tst2], w=[Tst2], bias=64e-5, scale=1.0)
                    S.op("dve", lambda e, o=st2: e.reciprocal(out=o[:, 0:8], in_=o[:, 0:8]), r=[Tst2], w=[Tst2])
                    S.tt("dve", y3, y3, st1[:, 0:8].unsqueeze(2).to_broadcast([128, 8, 64]), ALU.subtract, r=[Ty_, Tst1], w=[Ty_])
                    S.tt("dve", y3, y3, st2[:, 0:8].unsqueeze(2).to_broadcast([128, 8, 64]), ALU.mult, r=[Ty_, Tst2], w=[Ty_])
                    S.tt("pool", y[:], y[:], rows["rwkv_gn_g"][:], ALU.mult, r=[Ty_, Tw], w=[Ty_])
                    S.tt("pool", y[:], y[:], rows["rwkv_gn_b"][:], ALU.add, r=[Ty_, Tw], w=[Ty_])
                    S.tt("dve", sq[:].rearrange("p (h j) -> p h j", h=8), Vb[:].rearrange("p (h j) -> p h j", h=8),
                         sm[:, 8:16].unsqueeze(2).to_broadcast([128, 8, 64]), ALU.mult, r=[TVb, Tsm], w=[Tsq])
                    S.tt("pool", y[:], y[:], sq[:], ALU.add, r=[Ty_, Tsq], w=[Ty_])
                    S.tt("dve", y[:], y[:], g_sb[:], ALU.mult, r=[Ty_, Tgs], w=[Ty_])
                    S.dma("sp", d["YR"][(tile - OWN_T) * 128:(tile - OWN_T + 1) * 128, :], y[:], r=[Ty_], w=[Tok()])

        tiles_ = [(g, s) for g in range(SL // 256) for s in range(2)]
        gens = [tile_gen(g, s) for (g, s) in tiles_]

        def run_x(gen):
            for v in gen:
                if v == "XDONE":
                    return

        run_x(gens[0])
        for i in range(len(gens)):
            a = gens[i]
            b = gens[i + 1] if i + 1 < len(gens) else None
            a_done = False
            b_done = b is None
            while not (a_done and b_done):
                if not a_done:
                    try:
                        next(a)
                    except StopIteration:
                        a_done = True
                if not b_done:
                    if next(b) == "XDONE":
                        b_done = True
        S.barrier()


SCALE = 0.125
BIG8 = -240000.0


def nsa_phase(S, C, d):
    nc = C.nc
    SL = C.SL
    NT = SL // 128
    QB0 = NT // 2
    NCMP = SL // 16 - 1
    with ExitStack() as st:
        sb = lambda n, shape, dt: st.enter_context(nc.sbuf_tensor("n_" + n, shape, dt))
        pring = Ring([st.enter_context(nc.psum_tensor("n_ps%d" % i, [128, 512], F32)) for i in range(5)], "nps")
        acc_ps = [st.enter_context(nc.psum_tensor("n_acc%d" % i, [128, 512], F32)) for i in range(3)]
        Tacc = toks(3, "nacc")
        Tc = Tok("nsac")
        identb = sb("identb", [128, 128], BF16)
        S.cp("dve", identb[:], C.ident[:], r=[C.Tconst], w=[Tc])
        ident = C.ident
        jmb = sb("jmb", [128, 128], BF16)
        load_cast(S, "pool", jmb[:], d["jmat"][:, :], w=[Tc])
        RB = sb("RB", [33, 8], F32)
        S.dma("sp", RB[0:32, :], d["rel_bias"][:, :], w=[Tc])
        S.op("act", lambda e: e.mul(out=RB[0:32, :], in_=RB[0:32, :], mul=8.0), r=[Tc], w=[Tc])
        S.memset("dve", RB[32:33, :], BIG8, w=[Tc])
        OHF = sb("OHF", [33, 768], F32)
        S.dma("sp", OHF[:], d["ohf"][:, :], w=[Tc])
        fb_sb = sb("fb", [8, 768], F32)
        for hf in range(2):
            pb, Tp = pring.next()
            S.mm(pb[0:8, 0:384], RB[:, :], OHF[:, hf * 384:(hf + 1) * 384], True, True, r=[Tc], w=[Tp])
            S.cp("dve", fb_sb[:, hf * 384:(hf + 1) * 384], pb[0:8, 0:384], r=[Tp], w=[Tc])
        Tfb = Tok("fbd")
        S.dma("sp", d["FB"][:, :], fb_sb[:], r=[Tc], w=[Tfb])
        FBt = d["FB"].tensor
        Btab = sb("Btab", [128, 2, 3, 512], BF16)
        PatC = sb("PatC", [128, 2, 4, 24], BF16)
        bstg = sb("bstg", [128, 128], F32)
        Tbs = Tok("bstg")
        with nc.allow_non_contiguous_dma(reason="one-time small bias table build"):
            for g in range(2):
                for di, dl in enumerate((0, 1, 4)):
                    for h in range(4):
                        off = (g * 4 + h) * 768 + dl * 128
                        src = bass.AP(FBt, off, [[1, 128], [1, 128]])
                        S.dma("sp", bstg[:], src, r=[Tfb], w=[Tbs])
                        S.cp("dve", Btab[:, g, di, h * 128:(h + 1) * 128], bstg[:], r=[Tbs], w=[Tc])
                for h in range(4):
                    off = (g * 4 + h) * 768 + 96 + 256
                    src = bass.AP(FBt, off, [[1, 128], [-16, 23]])
                    S.dma("sp", bstg[:, 0:23], src, r=[Tfb], w=[Tbs])
                    S.cp("dve", PatC[:, g, h, 0:23], bstg[:, 0:23], r=[Tbs], w=[Tc])
        keyb = sb("keyb", [128, NT], F32)
        S.dma("sp", keyb[:], d["keyb"][:, :], w=[Tc])
        cvrow = sb("cvrow", [1, 512], BF16)
        load_cast(S, "pool", cvrow[:], d["cvrow"][:, :], w=[Tc])
        onesr = sb("onesr", [1, 128], BF16)
        S.memset("dve", onesr[:], 1.0, w=[Tc])
        D0 = sb("D0", [128, 128], F32)
        S.dma("sp", D0[:], d["d0"][:, :], w=[Tc])
        firstr = sb("firstr", [128, 128], F32)
        S.dma("sp", firstr[:], d["firstrow"][:, :], w=[Tc])
        KE = sb("KE", [128, SL], BF16)
        KW = sb("KW", [128, SL], BF16)
        for c0 in range(0, SL, 2048):
            c1 = min(SL, c0 + 2048)
            load_cast(S, "pool", KE[64:128, c0:c1], d["emat2"][:, c0:c1], w=[Tc])
        S.memset("pool", KW[64:128, :], 0.0, w=[Tc])
        oh0 = sb("oh0", [128, 128], BF16)
        S.memset("dve", oh0[:], 0.0, w=[Tc])
        S.memset("dve", oh0[0:1, :], 1.0, w=[Tc])
        cv128 = sb("cv128", [128, 512], BF16)
        S.memset("dve", cv128[:], 0.0, w=[Tc])
        S.cp("dve", cv128[0:1, :], cvrow[:], r=[Tc], w=[Tc])
        w1 = [sb("w1%d" % i, [64, 32, 128], BF16) for i in range(2)]
        w2 = [sb("w2%d" % i, [128, 64], BF16) for i in range(2)]
        peT = [sb("peT%d" % i, [64, 32], BF16) for i in range(2)]
        with nc.allow_non_contiguous_dma(reason="small transposed pe load"):
            for i, (a, b, c) in enumerate((("nsa_w1_k", "nsa_w2_k", "nsa_pe_k"), ("nsa_w1_v", "nsa_w2_v", "nsa_pe_v"))):
                for l0 in range(0, 32, 8):
                    load_cast(S, "pool", w1[i][:, l0:l0 + 8, :], d[a][l0 * 64:(l0 + 8) * 64, :].rearrange("(l d) h -> d l h", d=64), w=[Tc])
                load_cast(S, "pool", w2[i][:, :], d[b][:, :], w=[Tc])
                load_cast(S, "pool", peT[i][:, :], d[c].rearrange("l d -> d l"), w=[Tc])
        cT = sb("cT", [64, SL], BF16)
        vs = sb("vs", [128, NT, 65], BF16)
        vw = sb("vw", [128, NT, 65], BF16)
        KcT = sb("KcT", [128, 512], BF16)
        S.memset("dve", KcT[64:128, :], 0.0, w=[Tc])
        vc = sb("vc", [128, 4, 65], BF16)
        hidb = sb("hidb", [128, 512], BF16)
        Tkv = Tok("kv")
        f32r = Ring([sb("f%d" % i, [128, 512], F32) for i in range(10)], "nf32")
        b16r = Ring([sb("b%d" % i, [128, 512], BF16) for i in range(8)], "nb16")
        pcTr = Ring([sb("pcT%d" % i, [128, 4, 128], BF16) for i in range(4)], "pcT")
        pcbr = Ring([sb("pcb%d" % i, [128, 512], BF16) for i in range(8)], "pcb")
        smr = Ring([sb("s%d" % i, [128, 16], F32) for i in range(12)], "nsm")
        sqr = Ring([sb("q%d" % i, [128, 128], F32) for i in range(10)], "nsq")
        QNs = [sb("QN%d" % i, [128, 2, 512], BF16) for i in range(4)]
        for q_ in QNs:
            S.memset("pool", q_[:], 0.0, w=[Tc])
        nsb = Ring([sb("nsb%d" % i, [128, 2, 128], BF16) for i in range(2)], "nsb")
        TqTs = toks(4, "qT")
        TnsTs = toks(4, "nsT")
        gtr = Ring([sb("gt%d" % i, [128, 24], F32) for i in range(3)], "gt")
        yor = Ring([sb("yo%d" % i, [128, 256], F32) for i in range(2)], "yo")
        otr = Ring([sb("ot%d" % i, [128, 4, 65], F32) for i in range(6)], "ot")

        for g in range(2):
            S.dma("sp", KE[0:64, :], d["KT"][4 + g, :, :], w=[Tkv])
            S.dma("sp", KW[0:64, :], d["KT"][6 + g, :, :], w=[Tkv])
            S.memset("dve", vs[:, :, 64:65], 1.0, w=[Tkv])
            S.memset("dve", vw[:, :, 64:65], 1.0, w=[Tkv])
            S.memset("dve", vc[:, :, 64:65], 1.0, w=[Tkv])
            for t0 in range(0, NT, 8):
                t1 = min(NT, t0 + 8)
                S.dma("sp", vs[:, t0:t1, 0:64], d["VSW"][t0 * 128:t1 * 128, g * 64:(g + 1) * 64].rearrange("(t p) c -> p t c", p=128), w=[Tkv])
                S.dma("sp", vw[:, t0:t1, 0:64], d["VSW"][t0 * 128:t1 * 128, 128 + g * 64:128 + (g + 1) * 64].rearrange("(t p) c -> p t c", p=128), w=[Tkv])
            for i in range(2):
                S.dma("sp", cT[:], d["KT"][i * 2 + g, :, :], w=[Tkv])
                hp, Thp = pring.next()
                src_ap = cT[:]
                for l in range(32):
                    rhs = bass.AP(cT[:].tensor, cT[:, l:l + 1].offset, [list(cT[:].ap[0]), [16, NCMP]])
                    S.mm(hp[:, 0:NCMP], w1[i][:, l, :], rhs, l == 0, l == 31, r=[Tc, Tkv], w=[Thp])
                cp_, Tcp_ = pring.next()
                for l in range(32):
                    S.mm(cp_[:, 0:1], w1[i][:, l, :], peT[i][:, l:l + 1], l == 0, l == 31, r=[Tc], w=[Tcp_])
                cpe, Tcpe = smr.next()
                S.cp("dve", cpe[:, 0:1], cp_[:, 0:1], r=[Tcp_], w=[Tcpe])
                u, Tu = f32r.next()
                S.act(u[:, 0:NCMP], hp[:, 0:NCMP], AF.Identity, r=[Thp, Tcpe], w=[Tu], bias=cpe[:, 0:1], scale=1.0)
                t, Tt = f32r.next()
                S.tt("dve", t[:, 0:NCMP], u[:, 0:NCMP], u[:, 0:NCMP], ALU.mult, r=[Tu], w=[Tt])
                S.ts("dve", t[:, 0:NCMP], t[:, 0:NCMP], 0.044715, 1.0, ALU.mult, ALU.add, r=[Tt], w=[Tt])
                S.tt("dve", t[:, 0:NCMP], t[:, 0:NCMP], u[:, 0:NCMP], ALU.mult, r=[Tt, Tu], w=[Tt])
                S.act(t[:, 0:NCMP], t[:, 0:NCMP], AF.Sigmoid, r=[Tt], w=[Tt], scale=1.5957691216057308)
                S.memset("pool", hidb[:], 0.0, w=[Tkv])
                S.tt("dve", hidb[:, 0:NCMP], t[:, 0:NCMP], u[:, 0:NCMP], ALU.mult, r=[Tt, Tu], w=[Tkv])
                if i == 0:
                    kp, Tkp = pring.next()
                    S.mm(kp[0:64, :], w2[0][:, :], hidb[:, :], True, True, r=[Tc, Tkv], w=[Tkp])
                    S.cp("act", KcT[0:64, :], kp[0:64, :], r=[Tkp], w=[Tkv])
                else:
                    kp, Tkp = pring.next()
                    for c in range(4):
                        S.mm(kp[:, c * 64:(c + 1) * 64], hidb[:, c * 128:(c + 1) * 128], w2[1][:, :], True, True, r=[Tc, Tkv], w=[Tkp])
                    S.cp("act", vc[:, :, 0:64], kp[:, 0:256].rearrange("p (c d) -> p c d", c=4), r=[Tkp], w=[Tkv])
            for t_, Tt_ in zip(f32r.t, f32r.T):
                S.memset("dve", t_[:], 0.0, w=[Tt_])

            def block_gen(qb):
                qo = qb - QB0
                QN = QNs[qb % 4]
                TqT = TqTs[qb % 4]
                TnsT = TnsTs[qb % 4]
                for lh in range(2):
                    S.dma("sp", QN[0:64, lh, :].rearrange("p (h q) -> p h q", h=4),
                          d["QT"][g * 4:(g + 1) * 4, :, qo * 128:(qo + 1) * 128].rearrange("h p q -> p h q"), w=[TqT])
                gt_, Tgt = gtr.next()
                S.dma("sp", gt_[:], d["GATE"][qo * 128:(qo + 1) * 128, :], w=[Tgt])
                nvis = min(8 * qb + 7, NCMP)
                c0 = max(0, 8 * qb - 16)
                c1 = nvis
                oc_ps, Toc = acc_ps[0], Tacc[0]
                zc, Tzc = smr.next()
                es = []
                for h in range(4):
                    lp, Tlp = pring.next()
                    S.mm(lp[:, 0:nvis], QN[:, 0, h * 128:(h + 1) * 128], KcT[:, 0:nvis], True, False, r=[TqT, Tkv], w=[Tlp])
                    S.mm(lp[:, c0:c1], identb[:], PatC[:, g, h, c0 - (8 * qb - 16):c1 - (8 * qb - 16)], False, False, r=[Tc], w=[Tlp])
                    S.mm(lp[:, 0:nvis], oh0[:], cv128[:, 0:nvis], False, True, r=[Tc], w=[Tlp])
                    e, Te = f32r.next()
                    S.act(e[:, 0:nvis], lp[:, 0:nvis], AF.Exp, r=[Tlp], w=[Te, Tzc], scale=SCALE, accum_out=zc[:, h:h + 1])
                    es.append((e, Te))
                pcbs = []
                for h in range(4):
                    e, Te = es[h]
                    pcb, Tpcb = pcbr.next()
                    S.cp("dve", pcb[:], e[:], r=[Te], w=[Tpcb])
                    pcbs.append((pcb, Tpcb))

                def attn_branch(kts, kT_, v_, acc, Tac, bias_fn, extra_fn):
                    LOOK = 2
                    pend = []

                    def logits(kt):
                        lp, Tlp = pring.next()
                        dl = qb - kt
                        bt = bias_fn(dl)
                        if extra_fn is not None:
                            S.mm(lp[:], kT_[:, kt * 128:(kt + 1) * 128], QN[:, kt // 32, :], True, bt is None, r=[Tkv, TqT, TnsT], w=[Tlp])
                        else:
                            S.mm(lp[:], kT_[:, kt * 128:(kt + 1) * 128], QN[:, 0, :], True, bt is None, r=[Tkv, TqT], w=[Tlp])
                        if bt is not None:
                            S.mm(lp[:], jmb[:], bt, False, True, r=[Tc], w=[Tlp])
                        pT, TpT = b16r.next()
                        S.act(pT[:], lp[:], AF.Exp, r=[Tlp, Tc], w=[TpT], scale=SCALE, bias=keyb[:, kt:kt + 1])
                        pend.append((kt, pT, TpT))

                    def pv(first, last):
                        kt, pT, TpT = pend.pop(0)
                        for h in range(4):
                            S.mm(acc[:, h * 65:(h + 1) * 65], pT[:, h * 128:(h + 1) * 128], v_[:, kt, :], first and h == 0, last and h == 3,
                                 r=[TpT, Tkv], w=[Tac])

                    n = len(kts)
                    for i in range(min(LOOK, n)):
                        logits(kts[i])
                    for i in range(n):
                        if i + LOOK < n:
                            logits(kts[i + LOOK])
                        pv(i == 0, i == n - 1)

                ow_ps, Tow = acc_ps[2], Tacc[2]
                kt0 = max(0, qb - 4)
                attn_branch(list(range(kt0, qb + 1)), KW, vw, ow_ps, Tow,
                            lambda dl: Btab[:, g, {0: 0, 1: 1, 4: 2}[dl], :] if dl in (0, 1, 4) else None, None)
                pcsum, Tpcs = f32r.next()
                for h in range(4):
                    e, Te = es[h]
                    S.ts("dve", zc[:, h:h + 1], zc[:, h:h + 1], 1e-30, None, ALU.max, None, r=[Tzc], w=[Tzc])
                    S.op("dve", lambda e_, o=zc, hh=h: e_.reciprocal(out=o[:, 4 + hh:5 + hh], in_=o[:, hh:hh + 1]), r=[Tzc], w=[Tzc])
                    if h == 0:
                        S.ts("dve", pcsum[:], e[:], zc[:, 4:5], None, ALU.mult, None, r=[Te, Tzc], w=[Tpcs])
                    else:
                        S.stt(pcsum[:], e[:], zc[:, 4 + h:5 + h], pcsum[:], ALU.mult, ALU.add, r=[Te, Tzc, Tpcs], w=[Tpcs])
                imp, Timp = sqr.next()
                pc3 = pcsum[:].rearrange("p (j r) -> p j r", r=4)
                S.op("dve", lambda e_, o=imp, i=pc3: e_.tensor_reduce(out=o[:], in_=i, axis=AX.X, op=ALU.add), r=[Tpcs], w=[Timp])
                S.tt("dve", imp[:, 1:128], imp[:, 1:128], pc3[:, 0:127, 3], ALU.add, r=[Tpcs, Timp], w=[Timp])
                val, Tval = sqr.next()
                S.ts("dve", val[:], D0[:], float(2 * qb), None, ALU.is_le, None, r=[Tc], w=[Tval])
                fc, Tfc = sqr.next()
                S.ts("dve", fc[:], D0[:], float(2 * qb - 1), None, ALU.is_ge, None, r=[Tc], w=[Tfc])
                S.tt("dve", fc[:], fc[:], val[:], ALU.mult, r=[Tfc, Tval], w=[Tfc])
                S.tt("dve", fc[:], fc[:], firstr[:], ALU.add, r=[Tfc, Tc], w=[Tfc])
                sc, Tsc = sqr.next()
                S.stt(sc[:], fc[:], 1e4, imp[:], ALU.mult, ALU.max, r=[Tfc, Timp], w=[Tsc])
                S.stt(sc[:], sc[:], 1.0, val[:], ALU.add, ALU.mult, r=[Tsc, Tval], w=[Tsc])
                S.ts("dve", sc[:], sc[:], -1.0, None, ALU.add, None, r=[Tsc], w=[Tsc])
                m8, Tm8 = smr.next()
                S.op("dve", lambda e_, o=m8, i=sc: e_.max(out=o[:, 0:8], in_=i[:]), r=[Tsc], w=[Tm8])
                sc2, Tsc2 = sqr.next()
                S.op("dve", lambda e_, o=sc2, a=m8, i=sc: e_.match_replace(out=o[:], in_to_replace=a[:, 0:8], in_values=i[:], imm_value=-2.0), r=[Tsc, Tm8], w=[Tsc2])
                S.op("dve", lambda e_, o=m8, i=sc2: e_.max(out=o[:, 8:16], in_=i[:]), r=[Tsc2], w=[Tm8])
                nsl, Tnsl = nsb.next()
                S.ts("dve", sc2[:], sc[:], m8[:, 15:16], None, ALU.is_ge, None, r=[Tsc, Tm8], w=[Tsc2])
                S.ts("dve", nsl[:, 0, :], sc2[:], -1.0, -BIG8, ALU.add, ALU.mult, r=[Tsc2], w=[Tnsl])
                S.cp("dve", nsl[:, 1, 0:64], nsl[:, 0, 64:128], r=[Tnsl], w=[Tnsl])
                S.cp("dve", nsl[:, 1, 64:128], nsl[:, 0, 0:64], r=[Tnsl], w=[Tnsl])
                yield "A"
                tm_, Ttm = pring.next()
                tmb = tm_[:].bitcast(BF16)
                S.tr(tmb[:, 0:128], nsl[:, 0, :], identb[:], r=[Tnsl, Tc], w=[Ttm])
                S.tr(tmb[:, 128:256], nsl[:, 1, :], identb[:], r=[Tnsl, Tc], w=[Ttm])
                S.cp("dve", QN[64:128, 1, :].rearrange("p (h q) -> p h q", h=4), tmb[64:128, 0:128].unsqueeze(1).to_broadcast([64, 4, 128]), r=[Ttm], w=[TnsT])
                S.cp("dve", QN[64:128, 0, :].rearrange("p (h q) -> p h q", h=4), tmb[64:128, 128:256].unsqueeze(1).to_broadcast([64, 4, 128]), r=[Ttm], w=[TnsT])
                tps_ = []
                for h in range(4):
                    pcb, Tpcb = pcbs[h]
                    tp_, Ttp = pring.next()
                    tpb = tp_[:].bitcast(BF16)
                    for c in range(4):
                        S.tr(tpb[:, c * 128:(c + 1) * 128], pcb[:, c * 128:(c + 1) * 128], identb[:], r=[Tpcb, Tc], w=[Ttp])
                    tps_.append((tpb, Ttp))
                pcTs = []
                for h in range(4):
                    tpb, Ttp = tps_[h]
                    pcT, TpcT = pcTr.next()
                    S.cp("dve", pcT[:], tpb[:, 0:512].rearrange("p (c q) -> p c q", c=4), r=[Ttp], w=[TpcT])
                    pcTs.append((pcT, TpcT))
                for h in range(4):
                    pcT, TpcT = pcTs[h]
                    for c in range(4):
                        S.mm(oc_ps[:, h * 65:(h + 1) * 65], pcT[:, c, :], vc[:, c, :], c == 0, c == 3, r=[TpcT, Tkv], w=[Toc])
                ow_sb, Tows = otr.next()
                S.cp("dve", ow_sb[:], ow_ps[:, 0:260].rearrange("p (h c) -> p h c", h=4), r=[Tow], w=[Tows])
                oc_sb, Tocs = otr.next()
                S.cp("dve", oc_sb[:], oc_ps[:, 0:260].rearrange("p (h c) -> p h c", h=4), r=[Toc], w=[Tocs])
                yield "B"
                os_ps, Tos = acc_ps[1], Tacc[1]

                attn_branch(list(range(qb + 1)), KE, vs, os_ps, Tos,
                            lambda dl: Btab[:, g, dl, :] if dl <= 1 else None, True)
                os_sb, Toss = otr.next()
                S.cp("act", os_sb[:], os_ps[:, 0:260].rearrange("p (h c) -> p h c", h=4), r=[Tos], w=[Toss])
                yield "S"
                yo, Tyo = yor.next()
                yo3 = yo[:].rearrange("p (h c) -> p h c", h=4)
                cf, Tcf = smr.next()
                g3 = gt_[:].rearrange("p (h b) -> p h b", b=3)
                for bi, (osb, Tosb) in enumerate(((oc_sb, Tocs), (os_sb, Toss), (ow_sb, Tows))):
                    S.ts("dve", cf[:, bi * 4:(bi + 1) * 4], osb[:, :, 64], 1e-30, None, ALU.max, None, r=[Tosb], w=[Tcf])
                    S.op("dve", lambda e_, o=cf, b_=bi: e_.reciprocal(out=o[:, b_ * 4:(b_ + 1) * 4], in_=o[:, b_ * 4:(b_ + 1) * 4]), r=[Tcf], w=[Tcf])
                    S.tt("dve", cf[:, bi * 4:(bi + 1) * 4], cf[:, bi * 4:(bi + 1) * 4], g3[:, g * 4:(g + 1) * 4, bi], ALU.mult, r=[Tcf, Tgt], w=[Tcf])
                    cfb = cf[:, bi * 4:(bi + 1) * 4].unsqueeze(2).to_broadcast([128, 4, 64])
                    if bi == 0:
                        S.tt("dve", yo3, osb[:, :, 0:64], cfb, ALU.mult, r=[Tosb, Tcf], w=[Tyo])
                    else:
                        S.tt("dve", osb[:, :, 0:64], osb[:, :, 0:64], cfb, ALU.mult, r=[Tosb, Tcf], w=[Tosb])
                        S.tt("dve", yo3, yo3, osb[:, :, 0:64], ALU.add, r=[Tosb, Tyo], w=[Tyo])
                S.dma("pool", d["YN"][qo * 128:(qo + 1) * 128, g * 256:(g + 1) * 256], yo[:], r=[Tyo], w=[Tok()])

            gens = [block_gen(qb) for qb in range(QB0, NT)]
            n_ = len(gens)
            next(gens[0])
            next(gens[0])
            for i in range(n_):
                if i + 1 < n_:
                    next(gens[i + 1])
                next(gens[i])
                if i + 1 < n_:
                    next(gens[i + 1])
                for _ in gens[i]:
                    pass
        S.barrier()


def post_phase(S, C, d):
    nc = C.nc
    SL = C.SL
    OWN = SL // 2
    XS = 1.0 / 16.0
    with ExitStack() as st:
        sb = lambda n, shape, dt: st.enter_context(nc.sbuf_tensor("p_" + n, shape, dt))
        pring = Ring([st.enter_context(nc.psum_tensor("p_ps%d" % i, [128, 512], F32)) for i in range(8)], "pps")
        Tw = Tok("pw")
        ident = C.ident
        wout = sb("wout", [128, 8, D], BF16)
        wq = sb("wq", [128, 8, D], BF16)
        wo = sb("wo", [128, 8, D], BF16)
        for dc in range(8):
            load_cast(S, "pool", wout[:, dc, :], d["mix_w_out"][dc * 128:(dc + 1) * 128, :], w=[Tw])
            load_cast(S, "pool", wq[:, dc, :], d["xattn_wq"][dc * 128:(dc + 1) * 128, :], w=[Tw])
            load_cast(S, "pool", wo[:, dc, :], d["xattn_wo"][dc * 128:(dc + 1) * 128, :], w=[Tw])
        lnp = {}
        for n in ["ln2_g", "ln2_b", "ln3_g", "ln3_b"]:
            lnp[n] = sb(n, [128, D], F32)
            S.dma("sp", lnp[n][:], d[n].partition_broadcast(128), w=[C.Tconst])
        KT = sb("KT", [128, 8, 256], BF16)
        V = sb("V", [128, 2, 4, 257], BF16)
        with ExitStack() as st2:
            sb2 = lambda n, shape, dt: st2.enter_context(nc.sbuf_tensor("p2_" + n, shape, dt))
            wk = sb2("wk", [128, 8, D], BF16)
            wv = sb2("wv", [128, 8, D], BF16)
            for dc in range(8):
                load_cast(S, "pool", wk[:, dc, :], d["xattn_wk"][dc * 128:(dc + 1) * 128, :], w=[Tw])
                load_cast(S, "pool", wv[:, dc, :], d["xattn_wv"][dc * 128:(dc + 1) * 128, :], w=[Tw])
            mt_ = sb2("mem", [128, 2, D], F32)
            Tm = Tok("mem")
            memT = sb2("memT", [128, 8, 256], BF16)
            TmT = Tok("memT")
            S.dma("sp", mt_[:], d["mem"].rearrange("(s p) d -> p s d", p=128), w=[Tm])
            for s in range(2):
                for hb in range(2):
                    pb, Tp = pring.next()
                    for j in range(4):
                        dc = hb * 4 + j
                        S.tr(pb[:, j * 128:(j + 1) * 128], mt_[:, s, dc * 128:(dc + 1) * 128], ident[:], r=[Tm, C.Tconst], w=[Tp])
                    S.cp("dve", memT[:, hb * 4:(hb + 1) * 4, s * 128:(s + 1) * 128], pb[:].rearrange("p (j t) -> p j t", j=4), r=[Tp], w=[TmT])
            for cc in range(0, 8, 2):
                pb, Tp = pring.next()
                for k2 in range(2):
                    for dc in range(8):
                        S.mm(pb[:, k2 * 256:(k2 + 1) * 256], wk[:, dc, (cc + k2) * 128:(cc + k2 + 1) * 128], memT[:, dc, :], dc == 0, dc == 7, r=[Tw, TmT], w=[Tp])
                S.cp("act", KT[:, cc:cc + 2, :], pb[:].rearrange("p (k t) -> p k t", k=2), r=[Tp], w=[Tw])
            S.memset("dve", V[:, :, :, 256:257], 1.0, w=[Tw])
            for mt in range(2):
                for nb in range(2):
                    pb, Tp = pring.next()
                    for dc in range(8):
                        S.mm(pb[:], memT[:, dc, mt * 128:(mt + 1) * 128], wv[:, dc, nb * 512:(nb + 1) * 512], dc == 0, dc == 7, r=[Tw, TmT], w=[Tp])
                    S.cp("act", V[:, mt, nb * 2:(nb + 1) * 2, 0:256], pb[:].rearrange("p (h c) -> p h c", h=2), r=[Tp], w=[Tw])
            S.barrier()
        ycat = sb("ycat", [128, 2, D], F32)
        Tyc = Tok("ycat")
        x1t = sb("x1t", [128, 2, D], F32)
        Tx1 = Tok("x1t")
        x2t = sb("x2t", [128, 2, D], F32)
        Tx2 = Tok("x2t")
        oat = sb("oat", [128, 2, D], F32)
        Toa = Tok("oat")
        x3t = sb("x3t", [128, 2, D], F32)
        Tx3 = Tok("x3t")
        yT = sb("yT", [128, 8, 256], BF16)
        TyT = Tok("yT")
        x2T = sb("x2T", [128, 8, 256], BF16)
        Tx2T = Tok("x2T")
        oT = sb("oT", [128, 8, 256], BF16)
        ToT = Tok("oT")
        QxT = sb("QxT", [128, 8, 256], BF16)
        TQx = Tok("QxT")
        pTr = Ring([sb("pT%d" % i, [128, 2, 256], BF16) for i in range(2)], "ppT")
        rr = sb("rr", [128, D], F32)
        Trr = Tok("rr")
        lnt = {"st": sb("lnst", [128, 2, 6], F32), "mv": sb("lnmv", [128, 2], F32), "rstd": sb("lnrs", [128, 1], F32), "T": Tok("lnt")}
        zr = Ring([sb("z%d" % i, [128, 4], F32) for i in range(4)], "pz")

        def transp(src, Tsrc, dst, Tdst):
            k = 0
            for s in range(2):
                for hb in range(2):
                    pb, Tp = pring.next()
                    for j in range(4):
                        dc = hb * 4 + j
                        S.tr(pb[:, j * 128:(j + 1) * 128], src[:, s, dc * 128:(dc + 1) * 128], ident[:], r=[Tsrc, C.Tconst], w=[Tp])
                    k += 1
                    S.cp("act" if k % 2 else "dve", dst[:, hb * 4:(hb + 1) * 4, s * 128:(s + 1) * 128],
                         pb[:].rearrange("p (j t) -> p j t", j=4), r=[Tp], w=[Tdst])

        def proj_res_ln(xT_, TxT_, w_, res, Tres, gname, bname, out_t, Tout):
            for s in range(2):
                for nb in range(2):
                    pb, Tp = pring.next()
                    for dc in range(8):
                        S.mm(pb[:], xT_[:, dc, s * 128:(s + 1) * 128], w_[:, dc, nb * 512:(nb + 1) * 512], dc == 0, dc == 7, r=[TxT_, Tw], w=[Tp])
                    S.stt(rr[:, nb * 512:(nb + 1) * 512], res[:, s, nb * 512:(nb + 1) * 512], ALPHA, pb[:], ALU.mult, ALU.add, r=[Tres, Tp], w=[Trr])
                layer_norm_tile(S, C, rr[:], out_t[:, s, :], lnp[gname], lnp[bname], None, Trr, Tout, lnt)

        for g in range(OWN // 256):
            r0 = g * 256
            S.dma("sp", ycat[:, :, 0:512], d["YR"][r0:r0 + 256, :].rearrange("(s p) c -> p s c", p=128), w=[Tyc])
            S.dma("sp", ycat[:, :, 512:1024], d["YN"][r0:r0 + 256, :].rearrange("(s p) c -> p s c", p=128), w=[Tyc])
            S.dma("sp", x1t[:], d["x1"][OWN + r0:OWN + r0 + 256, :].rearrange("(s p) c -> p s c", p=128), w=[Tx1])
            transp(ycat, Tyc, yT, TyT)
            proj_res_ln(yT, TyT, wout, x1t, Tx1, "ln2_g", "ln2_b", x2t, Tx2)
            transp(x2t, Tx2, x2T, Tx2T)
            for cc in range(0, 8, 2):
                pb, Tp = pring.next()
                for k2 in range(2):
                    for dc in range(8):
                        S.mm(pb[:, k2 * 256:(k2 + 1) * 256], wq[:, dc, (cc + k2) * 128:(cc + k2 + 1) * 128], x2T[:, dc, :], dc == 0, dc == 7, r=[Tw, Tx2T], w=[Tp])
                S.cp("act", QxT[:, cc:cc + 2, :], pb[:].rearrange("p (k t) -> p k t", k=2), r=[Tp], w=[TQx])
            for hd in range(4):
                lp, Tlp = pring.next()
                for mt in range(2):
                    for cc in range(2):
                        S.mm(lp[:, mt * 256:(mt + 1) * 256], KT[:, hd * 2 + cc, mt * 128:(mt + 1) * 128], QxT[:, hd * 2 + cc, :], cc == 0, cc == 1, r=[Tw, TQx], w=[Tlp])
                pT, TpT = pTr.next()
                S.act(pT[:], lp[:].rearrange("p (m q) -> p m q", m=2), AF.Exp, r=[Tlp], w=[TpT], scale=XS)
                for s in range(2):
                    op_, Top = pring.next()
                    for mt in range(2):
                        S.mm(op_[:, 0:257], pT[:, mt, s * 128:(s + 1) * 128], V[:, mt, hd, :], mt == 0, mt == 1, r=[TpT, Tw], w=[Top])
                    z, Tz = zr.next()
                    S.op("dve", lambda e_, o=z, i=op_: e_.reciprocal(out=o[:, 0:1], in_=i[:, 256:257]), r=[Top], w=[Tz])
                    S.ts("dve", oat[:, s, hd * 256:(hd + 1) * 256], op_[:, 0:256], z[:, 0:1], None, ALU.mult, None, r=[Top, Tz], w=[Toa])
            transp(oat, Toa, oT, ToT)
            proj_res_ln(oT, ToT, wo, x2t, Tx2, "ln3_g", "ln3_b", x3t, Tx3)
            S.dma("sp", d["X3"][r0:r0 + 256, :].rearrange("(s p) c -> p s c", p=128), x3t[:], r=[Tx3], w=[Tok()])
        S.barrier()


def host_consts():
    t = np.arange(128)
    same = (t[:, None] // 64) == (t[None, :] // 64)
    c = {}
    c["ident"] = np.eye(128, dtype=np.float32)
    c["jmat"] = np.ascontiguousarray(np.eye(128, dtype=np.float32)[::-1])
    c["tri_incl"] = (same & (t[:, None] <= t[None, :])).astype(np.float32)
    c["tri_excl"] = (same & (t[:, None] < t[None, :])).astype(np.float32)
    c["tri_after"] = (same & (t[:, None] > t[None, :])).astype(np.float32)
    c["m_su"] = c["tri_excl"].copy()
    c["m_sl"] = c["tri_after"].copy()
    c["m_iu"] = c["tri_incl"].copy()
    c["bdmask"] = same.astype(np.float32)
    sel2 = np.zeros((128, 2), np.float32)
    sel2[63, 0] = 1.0
    sel2[127, 1] = 1.0
    c["sel2"] = sel2
    return c


def t5_bucket_np(dist):
    n = np.maximum(dist, 0)
    nf = np.maximum(n, 16).astype(np.float32)
    large = 16 + (np.log(nf / np.float32(16)) / np.float32(math.log(128 / 16)) * np.float32(16)).astype(np.int32)
    large = np.minimum(large, 31)
    return np.where(n < 16, n, large)


def host_consts2(SL, pad):
    c = {}
    NT = SL // 128
    dist = np.arange(768) - 127
    bk = t5_bucket_np(dist)
    ohf = np.zeros((33, 768), np.float32)
    ohf[bk, np.arange(768)] = 1.0
    ohf[31, :] -= 1.0
    ohf[32, :] = ((dist < 0) | (dist >= 512)).astype(np.float32)
    c["ohf"] = ohf
    tok = np.arange(SL)
    keyb = np.where(tok >= pad, 0.0, NEG).astype(np.float32).reshape(NT, 128).T
    c["keyb"] = np.ascontiguousarray(keyb)
    n = np.arange(512)
    c["cvrow"] = np.where((16 * n >= pad) & (n < SL // 16 - 1), 0.0, BIG8).astype(np.float32).reshape(1, 512)
    i = np.arange(128)
    c["d0"] = (np.arange(128)[None, :] - (i[:, None] >= 64)).astype(np.float32)
    fr = np.zeros((128, 128), np.float32)
    fr[:, pad // 64] = 1.0
    c["firstrow"] = fr
    c["emat2"] = ((np.arange(SL)[None, :] // 64) % 64 == np.arange(64)[:, None]).astype(np.float32)
    return c


def const2_shapes(SL):
    return [("ohf", [33, 768]), ("keyb", [128, SL // 128]), ("cvrow", [1, 512]), ("d0", [128, 128]),
            ("firstrow", [128, 128]), ("emat2", [64, SL])]


CONST_SHAPES = [("ident", [128, 128]), ("jmat", [128, 128]), ("tri_incl", [128, 128]), ("tri_excl", [128, 128]), ("tri_after", [128, 128]),
                ("m_su", [128, 128]), ("m_sl", [128, 128]), ("m_iu", [128, 128]), ("bdmask", [128, 128]), ("sel2", [128, 2])]

WEIGHT_SHAPES = [
    ("ffn1_w_gate", [D, DFF]), ("ffn1_w_up", [D, DFF]), ("ffn1_w_down", [DFF, D]), ("ln1_g", [D]), ("ln1_b", [D]),
    ("mix_w_in", [D, 3096]), ("rwkv_mu", [1792]), ("rwkv_w0", [512]), ("rwkv_w_up", [64, 512]), ("rwkv_a0", [512]),
    ("rwkv_a_up", [64, 512]), ("rwkv_g_up", [128, 512]), ("rwkv_k_k", [512]), ("rwkv_k_a", [512]), ("rwkv_r_k", [512]),
    ("rwkv_gn_g", [512]), ("rwkv_gn_b", [512]),
    ("nsa_pe_k", [32, 64]), ("nsa_w1_k", [2048, 128]), ("nsa_w2_k", [128, 64]),
    ("nsa_pe_v", [32, 64]), ("nsa_w1_v", [2048, 128]), ("nsa_w2_v", [128, 64]),
    ("mix_w_out", [D, D]), ("ln2_g", [D]), ("ln2_b", [D]),
    ("xattn_wq", [D, D]), ("xattn_wk", [D, D]), ("xattn_wv", [D, D]), ("xattn_wo", [D, D]),
    ("ln3_g", [D]), ("ln3_b", [D]),
    ("ffn2_w_gate", [D, DFF]), ("ffn2_w_up", [D, DFF]), ("ffn2_w_down", [DFF, D]), ("ln4_g", [D]), ("ln4_b", [D]),
]


def build(SL=8192, upto=99, debug=False, stop=99):
    nc = bass.Bass("TRN2", target_bir_lowering=False)
    C = Ctx()
    C.nc = nc
    C.SL = SL
    C.stop = stop
    NT = SL // 128
    OWN = SL // 2
    ext = lambda n, shape: nc.dram_tensor(n, shape, F32, kind="ExternalInput").ap()
    kind_scr = "ExternalOutput" if debug else "Internal"
    scr = lambda n, shape, dt=F32: nc.dram_tensor(n, shape, dt, kind=kind_scr).ap()
    d = {}
    d["x"] = ext("x", [SL, D])
    d["valid"] = ext("valid", [128, NT])
    for n, shape in CONST_SHAPES + WEIGHT_SHAPES + const2_shapes(SL):
        d[n] = ext(n, shape)
    d["x1"] = scr("x1", [SL, D])
    d["KT"] = scr("KT", [8, 64, SL], BF16)
    d["QT"] = scr("QT", [8, 64, OWN], BF16)
    d["VSW"] = scr("VSW", [SL, 256], BF16)
    d["GATE"] = scr("GATE", [OWN, 24])
    d["YR"] = scr("YR", [OWN, 512])
    d["YN"] = scr("YN", [OWN, 512])
    d["X3"] = scr("X3", [OWN, D])
    d["FB"] = scr("FB", [8, 768])
    d["mem"] = ext("mem", [256, D])
    d["rel_bias"] = ext("rel_bias", [32, 8])
    out = nc.dram_tensor("out", [OWN, D], F32, kind="ExternalOutput").ap()
    with ExitStack() as st:
        S = Sched(nc, st)
        C.S = S
        C.Tconst = Tok("const")
        C.ident = st.enter_context(nc.sbuf_tensor("sb_ident", [128, 128], F32))
        C.valid = st.enter_context(nc.sbuf_tensor("sb_valid", [128, NT], F32))
        S.dma("sp", C.ident[:], d["ident"][:, :], w=[C.Tconst])
        S.dma("sp", C.valid[:], d["valid"][:, :], w=[C.Tconst])
        ffn_phase(S, C, d["x"], d["x1"], d["ffn1_w_gate"], d["ffn1_w_up"], d["ffn1_w_down"],
                  d["ln1_g"], d["ln1_b"], SL // 256, True, "f1")
        if upto >= 2:
            mix_phase(S, C, d)
        if upto >= 3:
            nsa_phase(S, C, d)
        if upto >= 4:
            post_phase(S, C, d)
        if upto >= 5:
            d["out"] = out
            ffn_phase(S, C, d["X3"], out, d["ffn2_w_gate"], d["ffn2_w_up"], d["ffn2_w_down"],
                      d["ln4_g"], d["ln4_b"], OWN // 256, False, "f2")
        S.barrier()
        print("ninst", S.ninst)
    return nc


_NC_CACHE = {}


def kernel(**inputs):
    SL = 8192
    OWN = SL // 2
    x = np.asarray(inputs["x"], dtype=np.float32)
    B = x.shape[0]
    if "nc" not in _NC_CACHE:
        _NC_CACHE["nc"] = build(SL=SL, upto=99, debug=False)
    nc = _NC_CACHE["nc"]
    base = dict(host_consts())
    for n, shp in WEIGHT_SHAPES:
        a = np.asarray(inputs[n], dtype=np.float32)
        base[n] = np.ascontiguousarray(a.reshape(shp))
    base["rel_bias"] = np.ascontiguousarray(np.asarray(inputs["rel_bias"], dtype=np.float32))
    in_maps = []
    for c in range(8):
        b, half = c // 2, c % 2
        pad = OWN if half == 0 else 0
        m = dict(base)
        xl = np.zeros((SL, D), np.float32)
        if half == 0:
            xl[OWN:] = x[b, :OWN]
        else:
            xl[:] = x[b]
        m["x"] = xl
        valid = np.ones((128, SL // 128), np.float32)
        valid[:, :pad // 128] = 0.0
        m["valid"] = valid
        m["mem"] = np.ascontiguousarray(np.asarray(inputs["mem"], dtype=np.float32)[b])
        m.update(host_consts2(SL, pad))
        in_maps.append(m)
    res = run_bass_kernel_spmd(nc, in_maps, core_ids=list(range(8)))
    out = np.zeros((B, SL, D), np.float32)
    for c in range(8):
        b, half = c // 2, c % 2
        out[b, half * OWN:(half + 1) * OWN] = np.asarray(res.results[c]["out"], dtype=np.float32)
    return out
```

```python
import math
from contextlib import ExitStack
import numpy as np
import concourse.bass as bass
import concourse.mybir as mybir
from concourse.bass_utils import run_bass_kernel_spmd

F32 = mybir.dt.float32
BF16 = mybir.dt.bfloat16
AF = mybir.ActivationFunctionType
ALU = mybir.AluOpType
AX = mybir.AxisListType

D = 1024
DFF = 2816
NFC = DFF // 128
LN_EPS = 1e-5
ALPHA = 2.0 ** 0.25
NEG = -30000.0


class Tok:
    __slots__ = ("name", "w", "r")

    def __init__(self, name=""):
        self.name = name
        self.w = None
        self.r = {}


def toks(n, name=""):
    return [Tok(name + str(i)) for i in range(n)]


class Sched:
    NDMA = 24

    def __init__(self, nc, stack, same_engine_sync=True):
        self.nc = nc
        self.engs = {"pe": nc.tensor, "act": nc.scalar, "dve": nc.vector,
                     "pool": nc.gpsimd, "sp": nc.sync}
        self.sem = {}
        self.cnt = {}
        for k in self.engs:
            self.sem[k] = stack.enter_context(nc.semaphore("s_" + k))
            self.cnt[k] = 0
        for i in range(self.NDMA):
            k = "d%d" % i
            self.sem[k] = stack.enter_context(nc.semaphore("s_" + k))
            self.cnt[k] = 0
        self.seen = {k: {} for k in self.engs}
        self.dma_i = {"sw": 0, "hw": 0}
        self.ses = same_engine_sync
        self.ninst = 0

    def _wait(self, eng, deps):
        e = self.engs[eng]
        seen = self.seen[eng]
        for (c, v) in deps:
            if c == eng and (eng == "pe" or eng == "sp" or not self.ses):
                continue
            if seen.get(c, 0) >= v:
                continue
            e.wait_ge(self.sem[c], v)
            seen[c] = v

    def _deps(self, r, w):
        deps = []
        for t in r:
            if t.w is not None:
                deps.append(t.w)
        for t in w:
            if t.w is not None:
                deps.append(t.w)
            for c, v in t.r.items():
                deps.append((c, v))
        return deps

    def op(self, eng, fn, r=(), w=()):
        self._wait(eng, self._deps(r, w))
        ins = fn(self.engs[eng])
        self.cnt[eng] += 1
        v = self.cnt[eng]
        ins.then_inc(self.sem[eng], 1)
        for t in r:
            t.r[eng] = v
        for t in w:
            t.w = (eng, v)
            t.r = {}
        self.ninst += 1

    def dma(self, q, out, in_, r=(), w=(), **kw):
        if q == "pool":
            slot = "d%d" % (self.dma_i["sw"] % 8)
            self.dma_i["sw"] += 1
        else:
            slot = "d%d" % (8 + self.dma_i["hw"] % (self.NDMA - 8))
            self.dma_i["hw"] += 1
        deps = self._deps(r, w)
        if self.cnt[slot] > 0:
            deps.append((slot, self.cnt[slot]))
        self._wait(q, deps)
        ins = self.engs[q].dma_start(out=out, in_=in_, **kw)
        self.cnt[slot] += 16
        v = self.cnt[slot]
        ins.then_inc(self.sem[slot], 16)
        for t in r:
            t.r[slot] = v
        for t in w:
            t.w = (slot, v)
            t.r = {}
        self.ninst += 1

    def barrier(self, engs=("pe", "act", "dve", "pool", "sp")):
        allc = [(c, v) for c, v in self.cnt.items() if v > 0]
        for e in engs:
            self._wait(e, [(c, v) for (c, v) in allc if c != e or e not in ("pe", "sp")])

    def finish(self, tks, eng="sp"):
        deps = [t.w for t in tks if t.w is not None]
        self._wait(eng, deps)

    def mm(self, out, lhsT, rhs, start, stop, r, w):
        self.op("pe", lambda e: e.matmul(out, lhsT=lhsT, rhs=rhs, start=start, stop=stop,
                                         skip_group_check=True), r=r, w=w)

    def tr(self, out, in_, ident, r, w):
        self.op("pe", lambda e: e.transpose(out, in_, ident), r=r, w=w)

    def act(self, out, in_, func, r, w, eng="act", **kw):
        self.op(eng, lambda e: e.activation(out=out, in_=in_, func=func, **kw), r=r, w=w)

    def tt(self, eng, out, in0, in1, op, r, w):
        self.op(eng, lambda e: e.tensor_tensor(out=out, in0=in0, in1=in1, op=op), r=r, w=w)

    def ts(self, eng, out, in0, s1, s2, op0, op1, r, w, **kw):
        if op1 is None:
            self.op(eng, lambda e: e.tensor_scalar(out=out, in0=in0, scalar1=s1, scalar2=None, op0=op0, **kw), r=r, w=w)
        else:
            self.op(eng, lambda e: e.tensor_scalar(out=out, in0=in0, scalar1=s1, scalar2=s2, op0=op0, op1=op1, **kw), r=r, w=w)

    def stt(self, out, in0, scalar, in1, op0, op1, r, w):
        self.op("dve", lambda e: e.scalar_tensor_tensor(out=out, in0=in0, scalar=scalar, in1=in1, op0=op0, op1=op1), r=r, w=w)

    def cp(self, eng, out, in_, r, w):
        if eng == "act":
            self.op("act", lambda e: e.copy(out=out, in_=in_), r=r, w=w)
        else:
            self.op(eng, lambda e: e.tensor_copy(out=out, in_=in_), r=r, w=w)

    def memset(self, eng, ap, val, w):
        self.op(eng, lambda e: e.memset(ap, val), r=(), w=w)


class Ctx:
    pass


def load_cast(S, q, dst, src, w, r=()):
    S.dma("pool", dst, src, r=r, w=w, max_dma_last_dim=4096)


def layer_norm_tile(S, C, r_ap, out_ap, g_t, b_t, valid_ap, Tr, Tout, tmp):
    nc = C.nc
    st, mv, rstd = tmp["st"], tmp["mv"], tmp["rstd"]
    Tst = tmp["T"]
    S.op("dve", lambda e: e.bn_stats(out=st[:, 0, :], in_=r_ap[:, 0:512]), r=[Tr], w=[Tst])
    S.op("dve", lambda e: e.bn_stats(out=st[:, 1, :], in_=r_ap[:, 512:1024]), r=[Tr], w=[Tst])
    S.op("dve", lambda e: e.bn_aggr(out=mv[:], in_=st[:]), r=[Tst], w=[Tst])
    S.act(rstd[:], mv[:, 1:2], AF.Sqrt, r=[Tst], w=[Tst], bias=LN_EPS, scale=1.0)
    S.op("dve", lambda e: e.reciprocal(out=rstd[:], in_=rstd[:]), r=[Tst], w=[Tst])
    S.ts("dve", r_ap, r_ap, mv[:, 0:1], rstd[:, 0:1], ALU.subtract, ALU.mult, r=[Tst, Tr], w=[Tr])
    if valid_ap is not None:
        S.stt(r_ap, r_ap, valid_ap, g_t[:], ALU.mult, ALU.mult, r=[Tr, C.Tconst], w=[Tr])
        S.stt(out_ap, b_t[:], valid_ap, r_ap, ALU.mult, ALU.add, r=[Tr, C.Tconst], w=[Tout])
    else:
        S.tt("dve", r_ap, r_ap, g_t[:], ALU.mult, r=[Tr, C.Tconst], w=[Tr])
        S.tt("dve", out_ap, r_ap, b_t[:], ALU.add, r=[Tr, C.Tconst], w=[Tout])


def load_xT(S, C, src_rows_ap, xt, Txt, xT, TxT, tps, Ttps, ident, TG=2):
    S.dma("sp", xt[:], src_rows_ap.rearrange("(s p) d -> p s d", p=128), w=[Txt])
    make_xT(S, C, xt, Txt, xT, TxT, tps, Ttps, ident, TG)


def make_xT(S, C, xt, Txt, xT, TxT, tps, Ttps, ident, TG=2):
    k = 0
    for s in range(TG):
        for hb in range(2):
            pb = tps[k % 2]
            Tp = Ttps[k % 2]
            k += 1
            for j in range(4):
                dc = hb * 4 + j
                S.tr(pb[:, j * 128:(j + 1) * 128], xt[:, s, dc * 128:(dc + 1) * 128], ident[:], r=[Txt, C.Tconst], w=[Tp])
            eng = "dve" if (k % 2) else "act"
            S.cp(eng, xT[:, hb * 4:(hb + 1) * 4, s * 128:(s + 1) * 128],
                 pb[:].rearrange("p (j t) -> p j t", j=4), r=[Tp], w=[TxT])


def ffn_phase(S, C, src, dst, wg_d, wu_d, wd_d, g_d, b_d, ngroups, use_valid, name):
    nc = C.nc
    TG = 2
    GT = TG * 128
    with ExitStack() as st:
        sb = lambda n, shape, dt: st.enter_context(nc.sbuf_tensor(name + n, shape, dt))
        ps = lambda n: st.enter_context(nc.psum_tensor(name + n, [128, 512], F32))
        wg = sb("wg", [128, 8, DFF], BF16)
        wu = sb("wu", [128, 8, DFF], BF16)
        wd = sb("wd", [128, NFC, D], BF16)
        Twg, Twu, Twd = toks(8, "wg"), toks(8, "wu"), toks(NFC, "wd")
        gt = sb("g", [128, D], F32)
        bt = sb("b", [128, D], F32)
        S.dma("sp", gt[:], g_d.partition_broadcast(128), w=[C.Tconst])
        S.dma("sp", bt[:], b_d.partition_broadcast(128), w=[C.Tconst])
        for dc in range(8):
            load_cast(S, "pool", wg[:, dc, :], wg_d[dc * 128:(dc + 1) * 128, :], w=[Twg[dc]])
            load_cast(S, "pool", wu[:, dc, :], wu_d[dc * 128:(dc + 1) * 128, :], w=[Twu[dc]])
        for fc in range(NFC):
            load_cast(S, "pool", wd[:, fc, :], wd_d[fc * 128:(fc + 1) * 128, :], w=[Twd[fc]])
        xt = [sb("xt%d" % i, [128, TG, D], F32) for i in range(2)]
        Txt = toks(2, "xt")
        xT = [sb("xT%d" % i, [128, 8, GT], BF16) for i in range(2)]
        TxT = toks(2, "xT")
        hT = [sb("hT%d" % i, [128, NFC, GT], BF16) for i in range(2)]
        ThT = toks(2, "hT")
        sg = [sb("sg%d" % i, [128, GT], F32) for i in range(2)]
        Tsg = toks(2, "sg")
        rr = [sb("rr%d" % i, [128, D], F32) for i in range(2)]
        Trr = toks(2, "rr")
        oo = [sb("oo%d" % i, [128, D], F32) for i in range(2)]
        Too = toks(2, "oo")
        lnt = {"st": sb("lnst", [128, 2, 6], F32), "mv": sb("lnmv", [128, 2], F32),
               "rstd": sb("lnrs", [128, 1], F32), "T": Tok("lnt")}
        tps = [ps("tp0"), ps("tp1")]
        Ttps = toks(2, "tp")
        gups = [ps("g0"), ps("g1")]
        Tgu = toks(2, "gu")
        yps = [ps("y0"), ps("y1")]
        Ty = toks(2, "y")
        ti = 0
        for g in range(ngroups):
            b2 = g % 2
            load_xT(S, C, src[g * GT:(g + 1) * GT, :], xt[b2], Txt[b2], xT[b2], TxT[b2], tps, Ttps, C.ident, TG)
            for fc in range(NFC):
                p2 = fc % 2
                for dc in range(8):
                    S.mm(gups[p2][:, 0:GT], wg[:, dc, fc * 128:(fc + 1) * 128], xT[b2][:, dc, :], dc == 0, dc == 7,
                         r=[Twg[dc], TxT[b2]], w=[Tgu[p2]])
                for dc in range(8):
                    S.mm(gups[p2][:, GT:2 * GT], wu[:, dc, fc * 128:(fc + 1) * 128], xT[b2][:, dc, :], dc == 0, dc == 7,
                         r=[Twu[dc], TxT[b2]], w=[Tgu[p2]])
                S.act(sg[p2][:], gups[p2][:, 0:GT], AF.Silu, r=[Tgu[p2]], w=[Tsg[p2]])
                S.tt("dve", hT[b2][:, fc, :], sg[p2][:], gups[p2][:, GT:2 * GT], ALU.mult, r=[Tsg[p2], Tgu[p2]], w=[ThT[b2]])
            for s in range(TG):
                r2 = ti % 2
                ti += 1
                for nb in range(2):
                    for fc in range(NFC):
                        S.mm(yps[nb][:], hT[b2][:, fc, s * 128:(s + 1) * 128], wd[:, fc, nb * 512:(nb + 1) * 512],
                             fc == 0, fc == NFC - 1, r=[ThT[b2], Twd[fc]], w=[Ty[nb]])
                for nb in range(2):
                    S.act(rr[r2][:, nb * 512:(nb + 1) * 512], yps[nb][:], AF.Identity, r=[Ty[nb]], w=[Trr[r2]], scale=0.5)
                S.stt(rr[r2][:], xt[b2][:, s, :], ALPHA, rr[r2][:], ALU.mult, ALU.add, r=[Txt[b2], Trr[r2]], w=[Trr[r2]])
                tile = g * TG + s
                vap = C.valid[:, tile:tile + 1] if use_valid else None
                layer_norm_tile(S, C, rr[r2][:], oo[r2][:], gt, bt, vap, Trr[r2], Too[r2], lnt)
                S.dma("sp", dst[tile * 128:(tile + 1) * 128, :], oo[r2][:], r=[Too[r2]], w=[Tok()])
        S.barrier()


class Ring:
    def __init__(self, tiles, name):
        self.t = tiles
        self.T = toks(len(tiles), name)
        self.i = 0

    def next(self):
        i = self.i % len(self.t)
        self.i += 1
        return self.t[i], self.T[i]


HD = 64
NSA_OFF = 1792
SG_C = -0.6065306597126334


def mix_phase(S, C, d):
    nc = C.nc
    SL = C.SL
    NT = SL // 128
    OWN_T = NT // 2
    with ExitStack() as st:
        sb = lambda n, shape, dt: st.enter_context(nc.sbuf_tensor("m_" + n, shape, dt))
        pring = Ring([st.enter_context(nc.psum_tensor("m_ps%d" % i, [128, 512], F32)) for i in range(8)], "mps")
        Tw = Tok("mixw")
        W = sb("W", [128, 8, 3096], BF16)
        mub = sb("mub", [128, 1536], BF16)
        omb = sb("omb", [128, 1536], BF16)
        mucol = sb("mucol", [128, 4], F32)
        load_cast(S, "pool", mub[:], d["rwkv_mu"][0:1536].partition_broadcast(128), w=[Tw])
        S.ts("dve", omb[:], mub[:], -1.0, 1.0, ALU.mult, ALU.add, r=[Tw], w=[Tw])
        for c_ in range(2):
            S.dma("sp", mucol[:, c_:c_ + 1], d["rwkv_mu"][1536 + c_ * 128:1536 + (c_ + 1) * 128].rearrange("(p o) -> p o", o=1), w=[Tw])
        S.ts("dve", mucol[:, 2:4], mucol[:, 0:2], -1.0, 1.0, ALU.mult, ALU.add, r=[Tw], w=[Tw])
        for dc in range(8):
            load_cast(S, "pool", W[:, dc, :], d["mix_w_in"][dc * 128:(dc + 1) * 128, :], w=[Tw])
        lup = sb("lup", [128, 512], BF16)
        gup = sb("gup", [128, 512], BF16)
        load_cast(S, "pool", lup[0:64, :], d["rwkv_w_up"][:, :], w=[Tw])
        load_cast(S, "pool", lup[64:128, :], d["rwkv_a_up"][:, :], w=[Tw])
        load_cast(S, "pool", gup[:, :], d["rwkv_g_up"][:, :], w=[Tw])
        rows = {}
        for n in ["rwkv_w0", "rwkv_a0", "rwkv_k_k", "rwkv_k_a", "rwkv_r_k", "rwkv_gn_g", "rwkv_gn_b"]:
            rows[n] = sb(n, [128, 512], F32)
            S.dma("sp", rows[n][:], d[n].partition_broadcast(128), w=[Tw])
        cst = {}
        for n in ["tri_incl", "tri_excl", "tri_after", "m_su", "m_sl", "m_iu", "bdmask"]:
            cst[n] = sb(n, [128, 128], F32)
            S.dma("sp", cst[n][:], d[n][:, :], w=[Tw])
        sel2 = sb("sel2", [128, 2], F32)
        S.dma("sp", sel2[:], d["sel2"][:, :], w=[Tw])
        identb = sb("identb", [128, 128], BF16)
        S.cp("dve", identb[:], C.ident[:], r=[C.Tconst], w=[Tw])
        ident = C.ident

        def bc4(t):
            return t[:].unsqueeze(1).to_broadcast([128, 4, 128])

        def bc8(t):
            return t[:].unsqueeze(1).to_broadcast([128, 8, 128])

        xt = sb("xt", [128, 2, D], BF16)
        Txt = Tok("xt")
        xTe = [sb("xTe%d" % i, [128, 8, 257], BF16) for i in range(2)]
        TxTe = toks(2, "xTe")
        f32r = Ring([sb("f%d" % i, [128, 512], F32) for i in range(22)], "f32r")
        b16r = Ring([sb("h%d" % i, [128, 512], BF16) for i in range(21)], "b16r")
        mr = Ring([sb("M%d" % i, [128, 8, 128], BF16) for i in range(8)], "mr")
        keepr = Ring([sb("MK%d" % i, [128, 8, 128], BF16) for i in range(3)], "keepr")
        xtr = Ring([sb("XT%d" % i, [128, 4, 128], BF16) for i in range(8)], "xtr")
        smr = Ring([sb("sm%d" % i, [128, 16], F32) for i in range(12)], "smr")
        lw = [sb("lw%d" % i, [128, 256], BF16) for i in range(2)]
        lg = [sb("lg%d" % i, [128, 256], BF16) for i in range(2)]
        Tl = toks(2, "lora")
        STr = Ring([sb("ST%d" % i, [128, 4, 64], F32) for i in range(3)], "ST")
        GTr = Ring([sb("GT%d" % i, [128, 8, 128], F32) for i in range(1)], "GT")
        Hsr = Ring([sb("Hs%d" % i, [128, 8, 64], F32) for i in range(1)], "Hs")
        RHr = Ring([sb("RH%d" % i, [128, 4, 128], F32) for i in range(1)], "RH")
        kst = sb("kst", [64, 8, 256], BF16)
        Tkst = Tok("kst")
        qst = sb("qst", [64, 8, 256], BF16)
        Tqst = Tok("qst")
        vst = Ring([sb("vst%d" % i, [128, 256], BF16) for i in range(2)], "vst")
        gst = Ring([sb("gst%d" % i, [128, 24], F32) for i in range(2)], "gst")

        ST, TST = STr.next()
        S.memset("dve", ST[:], 0.0, w=[TST])
        stv = [ST, TST]
        S.memset("dve", xTe[1][:, :, 256:257], 0.0, w=[TxTe[1]])

        def evac_eng(k):
            return "act" if k % 2 else "dve"

        ek = [0]

        def tile_gen(g, s):
            b2 = g % 2
            own_g = (g * 2) >= OWN_T
            xcur = lambda dc, a, b, _x=xTe[b2]: _x[:, dc, 1 + a:1 + b]
            xprv = lambda dc, a, b, _x=xTe[b2]: _x[:, dc, a:b]
            TX = TxTe[b2]
            if s == 0:
                load_cast(S, "pool", xt[:], d["x1"][g * 256:(g + 1) * 256, :].rearrange("(s p) d -> p s d", p=128), w=[Txt])
                S.cp("pool", xTe[b2][:, :, 0:1], xTe[1 - b2][:, :, 256:257], r=[TxTe[1 - b2]], w=[TxTe[b2]])
                for s_ in range(2):
                    for hb in range(2):
                        pb, Tp = pring.next()
                        pbb_ = pb[:].bitcast(BF16)
                        for j in range(4):
                            dc = hb * 4 + j
                            S.tr(pbb_[:, j * 128:(j + 1) * 128], xt[:, s_, dc * 128:(dc + 1) * 128], identb[:], r=[Txt, Tw], w=[Tp])
                        ek[0] += 1
                        S.cp(evac_eng(ek[0]), xTe[b2][:, hb * 4:(hb + 1) * 4, 1 + s_ * 128:1 + (s_ + 1) * 128],
                             pbb_[:, 0:512].rearrange("p (j t) -> p j t", j=4), r=[Tp], w=[TxTe[b2]])
                pbz, Tpz = pring.next()
                pbp, Tpp = pring.next()
                for half, c0 in ((0, 1536), (1, 1664)):
                    for dc in range(8):
                        S.mm(pbz[:, half * 256:(half + 1) * 256], W[:, dc, c0:c0 + 128], xcur(dc, 0, 256), dc == 0, dc == 7, r=[Tw, TX], w=[Tpz])
                    for dc in range(8):
                        S.mm(pbp[:, half * 256:(half + 1) * 256], W[:, dc, c0:c0 + 128], xprv(dc, 0, 256), dc == 0, dc == 7, r=[Tw, TX], w=[Tpp])
                lz, Tlz = f32r.next()
                for half in range(2):
                    hs = slice(half * 256, (half + 1) * 256)
                    S.act(lz[:, hs], pbz[:, hs], AF.Identity, r=[Tpz, Tw], w=[Tlz], scale=mucol[:, 2 + half:3 + half])
                    S.stt(lz[:, hs], pbp[:, hs], mucol[:, half:half + 1], lz[:, hs], ALU.mult, ALU.add, r=[Tpp, Tlz, Tw], w=[Tlz])
                S.act(lw[b2][0:64, :], lz[0:64, 0:256], AF.Tanh, r=[Tlz], w=[Tl[b2]])
                S.cp("dve", lw[b2][64:128, :], lz[64:128, 0:256], r=[Tlz], w=[Tl[b2]])
                S.act(lg[b2][:], lz[:, 256:512], AF.Sigmoid, r=[Tlz], w=[Tl[b2]])
                slots = [512, 576, 640, 704, 768, 832, 1024, 1088]
                for bk in range(4):
                    pb, Tp = pring.next()
                    for k2 in range(2):
                        c0 = NSA_OFF + slots[bk * 2 + k2]
                        for dc in range(8):
                            S.mm(pb[0:64, k2 * 256:(k2 + 1) * 256], W[:, dc, c0:c0 + 64], xcur(dc, 0, 256), dc == 0, dc == 7, r=[Tw, TX], w=[Tp])
                    ek[0] += 1
                    S.cp(evac_eng(ek[0]), kst[:, bk * 2:(bk + 1) * 2, :], pb[0:64, :].rearrange("p (k t) -> p k t", k=2), r=[Tp], w=[Tkst])
                for k_ in range(8):
                    S.dma("sp", d["KT"][k_, :, g * 256:(g + 1) * 256], kst[:, k_, :], r=[Tkst], w=[Tok()])
                if own_g:
                    go = g - OWN_T // 2
                    for bk in range(4):
                        pb, Tp = pring.next()
                        for k2 in range(2):
                            c0 = NSA_OFF + (bk * 2 + k2) * 64
                            for dc in range(8):
                                S.mm(pb[0:64, k2 * 256:(k2 + 1) * 256], W[:, dc, c0:c0 + 64], xcur(dc, 0, 256), dc == 0, dc == 7, r=[Tw, TX], w=[Tp])
                        ek[0] += 1
                        S.cp(evac_eng(ek[0]), qst[:, bk * 2:(bk + 1) * 2, :], pb[0:64, :].rearrange("p (k t) -> p k t", k=2), r=[Tp], w=[Tqst])
                    for k_ in range(8):
                        S.dma("sp", d["QT"][k_, :, go * 256:(go + 1) * 256], qst[:, k_, :], r=[Tqst], w=[Tok()])
                yield "G"
            for _once in (0,):
                tile = g * 2 + s
                own = tile >= OWN_T
                a0, a1 = s * 128, (s + 1) * 128
                pb, Tp = pring.next()
                for k2, c0 in ((0, NSA_OFF + 896), (1, NSA_OFF + 1152)):
                    for dc in range(8):
                        S.mm(pb[:, k2 * 128:(k2 + 1) * 128], xcur(dc, a0, a1), W[:, dc, c0:c0 + 128], dc == 0, dc == 7, r=[Tw, TX], w=[Tp])
                if own:
                    for dc in range(8):
                        S.mm(pb[:, 256:280], xcur(dc, a0, a1), W[:, dc, NSA_OFF + 1280:NSA_OFF + 1304], dc == 0, dc == 7, r=[Tw, TX], w=[Tp])
                vt, Tv = vst.next()
                S.cp("act", vt[:], pb[:, 0:256], r=[Tp], w=[Tv])
                S.dma("sp", d["VSW"][tile * 128:(tile + 1) * 128, :], vt[:], r=[Tv], w=[Tok()])
                if own:
                    gt_, Tg_ = gst.next()
                    S.act(gt_[:], pb[:, 256:280], AF.Sigmoid, r=[Tp], w=[Tg_])
                    S.dma("sp", d["GATE"][(tile - OWN_T) * 128:(tile - OWN_T + 1) * 128, :], gt_[:], r=[Tg_], w=[Tok()])
                yield "S"
                sbs = []
                for q in range(3):
                    pbz, Tpz = pring.next()
                    pbp, Tpp = pring.next()
                    for dc in range(8):
                        S.mm(pbz[:], xcur(dc, a0, a1), W[:, dc, q * 512:(q + 1) * 512], dc == 0, dc == 7, r=[Tw, TX], w=[Tpz])
                    for dc in range(8):
                        S.mm(pbp[:], xprv(dc, a0, a1), W[:, dc, q * 512:(q + 1) * 512], dc == 0, dc == 7, r=[Tw, TX], w=[Tpp])
                    z_sb, Tzs = f32r.next()
                    S.tt("dve", z_sb[:], pbz[:], omb[:, q * 512:(q + 1) * 512], ALU.mult, r=[Tpz, Tw], w=[Tzs])
                    zp, Tzp = f32r.next()
                    S.tt("dve", zp[:], pbp[:], mub[:, q * 512:(q + 1) * 512], ALU.mult, r=[Tpp, Tw], w=[Tzp])
                    S.tt("pool", z_sb[:], z_sb[:], zp[:], ALU.add, r=[Tzs, Tzp], w=[Tzs])
                    sbs.append((z_sb, Tzs))
                (r_sb, Trs), (k_sb, Tks), (v_sb, Tvs) = sbs
                yield "S"
                w_ps, Twp = pring.next()
                S.mm(w_ps[:], lw[b2][0:64, a0:a1], lup[0:64, :], True, True, r=[Tl[b2], Tw], w=[Twp])
                a_ps, Tap = pring.next()
                S.mm(a_ps[:], lw[b2][64:128, a0:a1], lup[64:128, :], True, True, r=[Tl[b2], Tw], w=[Tap])
                Lt, TLt = f32r.next()
                S.tt("dve", Lt[:], w_ps[:], rows["rwkv_w0"][:], ALU.add, r=[Twp, Tw], w=[TLt])
                S.act(Lt[:], Lt[:], AF.Sigmoid, r=[TLt], w=[TLt])
                asg, Tas = f32r.next()
                S.tt("dve", asg[:], a_ps[:], rows["rwkv_a0"][:], ALU.add, r=[Tap, Tw], w=[Tas])
                S.act(asg[:], asg[:], AF.Sigmoid, r=[Tas], w=[Tas])
                if own:
                    g_ps, Tgp = pring.next()
                    S.mm(g_ps[:], lg[b2][:, a0:a1], gup[:, :], True, True, r=[Tl[b2], Tw], w=[Tgp])
                    g_sb, Tgs = b16r.next()
                    S.cp("act", g_sb[:], g_ps[:], r=[Tgp], w=[Tgs])
                yield "S"
                kk, Tkk = f32r.next()
                S.tt("pool", kk[:], k_sb[:], rows["rwkv_k_k"][:], ALU.mult, r=[Tks, Tw], w=[Tkk])
                tmp, Ttmp = f32r.next()
                S.tt("pool", tmp[:], kk[:], kk[:], ALU.mult, r=[Tkk], w=[Ttmp])
                sm, Tsm = smr.next()
                S.op("dve", lambda e, o=sm, i=tmp: e.tensor_reduce(out=o[:, 0:8], in_=i[:].rearrange("p (h j) -> p h j", h=8), axis=AX.X, op=ALU.add), r=[Ttmp], w=[Tsm])
                S.ts("dve", sm[:, 0:8], sm[:, 0:8], 1e-24, None, ALU.max, None, r=[Tsm], w=[Tsm])
                S.act(sm[:, 0:8], sm[:, 0:8], AF.Sqrt, r=[Tsm], w=[Tsm])
                S.op("dve", lambda e, o=sm: e.reciprocal(out=o[:, 0:8], in_=o[:, 0:8]), r=[Tsm], w=[Tsm])
                kk3 = kk[:].rearrange("p (h j) -> p h j", h=8)
                S.tt("dve", kk3, kk3, sm[:, 0:8].unsqueeze(2).to_broadcast([128, 8, 64]), ALU.mult, r=[Tkk, Tsm], w=[Tkk])
                yield "S"
                km, Tkm = f32r.next()
                S.stt(km[:], asg[:], -1.0, rows["rwkv_k_a"][:], ALU.add, ALU.mult, r=[Tas, Tw], w=[Tkm])
                S.stt(km[:], km[:], 1.0, k_sb[:], ALU.add, ALU.mult, r=[Tkm, Tks], w=[Tkm])
                yield "S"
                bv, Tbv = f32r.next()
                S.tt("pool", bv[:], kk[:], asg[:], ALU.mult, r=[Tkk, Tas], w=[Tbv])
                yield "S"
                if own:
                    bt_, Tbt = f32r.next()
                    S.tt("pool", bt_[:], r_sb[:], km[:], ALU.mult, r=[Trs, Tkm], w=[Tbt])
                    S.tt("pool", bt_[:], bt_[:], rows["rwkv_r_k"][:], ALU.mult, r=[Tbt, Tw], w=[Tbt])
                    S.op("dve", lambda e, o=sm, i=bt_: e.tensor_reduce(out=o[:, 8:16], in_=i[:].rearrange("p (h j) -> p h j", h=8), axis=AX.X, op=ALU.add), r=[Tbt], w=[Tsm])
                yield "S"
                c_ps, Tcp = pring.next()
                S.mm(c_ps[:], cst["tri_incl"][:], Lt[:], True, True, r=[Tw, TLt], w=[Tcp])
                x_ps, Txp = pring.next()
                S.mm(x_ps[:], cst["tri_excl"][:], Lt[:], True, True, r=[Tw, TLt], w=[Txp])
                d_ps, Tdp = pring.next()
                S.mm(d_ps[:], cst["tri_after"][:], Lt[:], True, True, r=[Tw, TLt], w=[Tdp])
                EL, TEL = f32r.next()
                ENL, TENL = f32r.next()
                ELm, TELm = f32r.next()
                EG, TEG = f32r.next()
                S.act(EL[:], c_ps[:], AF.Exp, r=[Tcp], w=[TEL], scale=SG_C)
                S.act(ENL[:], c_ps[:], AF.Exp, r=[Tcp], w=[TENL], scale=-SG_C)
                S.act(ELm[:], x_ps[:], AF.Exp, r=[Txp], w=[TELm], scale=SG_C)
                S.act(EG[:], d_ps[:], AF.Exp, r=[Tdp], w=[TEG], scale=SG_C)
                pbg, Tpg = pring.next()
                for p_ in range(4):
                    S.mm(pbg[:, p_ * 2:(p_ + 1) * 2], EL[:, p_ * 128:(p_ + 1) * 128], sel2[:], True, True, r=[TEL, Tw], w=[Tpg])
                gam, Tgam = smr.next()
                S.cp("dve", gam[:, 0:8], pbg[:, 0:8], r=[Tpg], w=[Tgam])
                yield "S"
                RT, TRT = b16r.next()
                KT, TKT = b16r.next()
                BT, TBT = b16r.next()
                AT, TAT = b16r.next()
                BG, TBG = b16r.next()
                KG, TKG = b16r.next()
                Vb, TVb = b16r.next()
                S.tt("dve", RT[:], r_sb[:], EL[:], ALU.mult, r=[Trs, TEL], w=[TRT])
                S.tt("pool", KT[:], km[:], ENL[:], ALU.mult, r=[Tkm, TENL], w=[TKT])
                S.tt("dve", BT[:], bv[:], ENL[:], ALU.mult, r=[Tbv, TENL], w=[TBT])
                S.stt(AT[:], kk[:], -1.0, ELm[:], ALU.mult, ALU.mult, r=[Tkk, TELm], w=[TAT])
                S.tt("pool", BG[:], bv[:], EG[:], ALU.mult, r=[Tbv, TEG], w=[TBG])
                S.tt("dve", KG[:], km[:], EG[:], ALU.mult, r=[Tkm, TEG], w=[TKG])
                S.cp("pool", Vb[:], v_sb[:], r=[Tvs], w=[TVb])
                yield "S"
                XT = {}
                for nm, X, TXq in (("r", RT, TRT), ("k", KT, TKT), ("b", BT, TBT), ("a", AT, TAT)):
                    pb, Tp = pring.next()
                    pbb = pb[:].bitcast(BF16)
                    for p in range(4):
                        S.tr(pbb[:, p * 128:(p + 1) * 128], X[:, p * 128:(p + 1) * 128], identb[:], r=[TXq, Tw], w=[Tp])
                    xT_, TxT_ = xtr.next()
                    ek[0] += 1
                    S.cp(evac_eng(ek[0]), xT_[:], pbb[:, 0:512].rearrange("p (q t) -> p q t", q=4), r=[Tp], w=[TxT_])
                    XT[nm] = (xT_, TxT_)

                yield "XDONE"
                def hsl(h):
                    return slice((h % 2) * 64, (h % 2) * 64 + 64), h // 2

                def mmat(lname, rname, mask, ring=None):
                    lx, Tlx = XT[lname]
                    rx, Trx = XT[rname]
                    M_, TM_ = (ring or mr).next()
                    for par in range(2):
                        pb, Tp = pring.next()
                        for hh in range(4):
                            h = hh * 2 + par
                            ps_, p_ = hsl(h)
                            S.mm(pb[:, hh * 128:(hh + 1) * 128], lx[ps_, p_, :], rx[ps_, p_, :], True, True, r=[Tlx, Trx], w=[Tp])
                        S.tt("dve", M_[:, par:8:2, :], pb[:].rearrange("p (h t) -> p h t", h=4), bc4(cst[mask]), ALU.mult, r=[Tp, Tw], w=[TM_])
                    return M_, TM_

                Mab, TMab = mmat("b", "a", "m_su")
                MabT, TMabT = mmat("a", "b", "m_sl")
                Mak, TMak = mmat("k", "a", "m_su", keepr)
                if own:
                    Mbr, TMbr = mmat("b", "r", "m_iu", keepr)
                    Mkr, TMkr = mmat("k", "r", "m_iu", keepr)

                def hmat(L_, TL_, R_, TR_, add=None):
                    O_, TO_ = mr.next()
                    for half in range(2):
                        pb, Tp = pring.next()
                        for hh in range(4):
                            h = half * 4 + hh
                            S.mm(pb[:, hh * 128:(hh + 1) * 128], L_[:, h, :], R_[:, h, :], True, True, r=[TL_, TR_], w=[Tp])
                        if add is None:
                            ek[0] += 1
                            S.cp(evac_eng(ek[0]), O_[:, half * 4:(half + 1) * 4, :], pb[:].rearrange("p (h t) -> p h t", h=4), r=[Tp], w=[TO_])
                        else:
                            A_, TA_ = add
                            S.tt("dve", O_[:, half * 4:(half + 1) * 4, :], pb[:].rearrange("p (h t) -> p h t", h=4),
                                 A_[:, half * 4:(half + 1) * 4, :], ALU.add, r=[Tp, TA_], w=[TO_])
                    return O_, TO_

                N_, TN_ = Mab, TMab
                NT_, TNT_ = MabT, TMabT
                P_, TP_ = mr.next()
                S.tt("dve", P_[:], N_[:], bc8(identb), ALU.add, r=[TN_, Tw], w=[TP_])
                for lvl in range(5):
                    NT2, TNT2 = hmat(N_, TN_, NT_, TNT_)
                    if lvl < 4:
                        N2, TN2 = hmat(NT_, TNT_, N_, TN_)
                    P_, TP_ = hmat(NT2, TNT2, P_, TP_, add=(P_, TP_))
                    NT_, TNT_ = NT2, TNT2
                    if lvl < 4:
                        N_, TN_ = N2, TN2
                    yield "L"
                Tm, TTm = P_, TP_

                def tokmat(L_, TL_, R_, TR_):
                    pb, Tp = pring.next()
                    for h in range(8):
                        S.mm(pb[:, h * 64:(h + 1) * 64], L_[:, h, :], R_[:, h * 64:(h + 1) * 64], True, True, r=[TL_, TR_], w=[Tp])
                    return pb, Tp

                pb, Tp = tokmat(Tm, TTm, AT, TAT)
                AH, TAH = b16r.next()
                S.cp("act", AH[:], pb[:], r=[Tp], w=[TAH])
                pb, Tp = tokmat(Mak, TMak, Vb, TVb)
                Wm_, TWm_ = b16r.next()
                S.cp("dve", Wm_[:], pb[:], r=[Tp], w=[TWm_])
                pb, Tp = tokmat(Tm, TTm, Wm_, TWm_)
                U0, TU0 = b16r.next()
                S.cp("act", U0[:], pb[:], r=[Tp], w=[TU0])
                if own:
                    pb, Tp = pring.next()
                    for h in range(8):
                        ps_, p_ = hsl(h)
                        S.mm(pb[ps_, p_ * 128:(p_ + 1) * 128], AH[:, h * 64:(h + 1) * 64], Mbr[:, h, :], True, True, r=[TAH, TMbr], w=[Tp])
                    RH, TRH = RHr.next()
                    S.tt("dve", RH[:], pb[:].rearrange("p (q t) -> p q t", q=4), XT["r"][0][:], ALU.add, r=[Tp, XT["r"][1]], w=[TRH])
                    pb, Tp = pring.next()
                    for h in range(8):
                        S.mm(pb[:, h * 64:(h + 1) * 64], Mbr[:, h, :], U0[:, h * 64:(h + 1) * 64], True, False, r=[TMbr, TU0], w=[Tp])
                        S.mm(pb[:, h * 64:(h + 1) * 64], Mkr[:, h, :], Vb[:, h * 64:(h + 1) * 64], False, True, r=[TMkr, TVb], w=[Tp])
                    Y0, TY0 = f32r.next()
                    S.cp("act", Y0[:], pb[:], r=[Tp], w=[TY0])
                GT, TGT = GTr.next()
                for c_ in range(2):
                    pb, Tp = pring.next()
                    for p_ in range(4):
                        S.mm(pb[:, p_ * 128:(p_ + 1) * 128], AH[c_ * 64:(c_ + 1) * 64, p_ * 128:(p_ + 1) * 128],
                             BG[c_ * 64:(c_ + 1) * 64, p_ * 128:(p_ + 1) * 128], True, True, r=[TAH, TBG], w=[Tp])
                    S.tt("dve", GT[:, c_:8:2, :], pb[:].rearrange("p (q t) -> p q t", q=4), bc4(cst["bdmask"]), ALU.mult, r=[Tp, Tw], w=[TGT])
                for pc_ in range(8):
                    S.stt(GT[:, pc_, :], ident[:], gam[:, pc_:pc_ + 1], GT[:, pc_, :], ALU.mult, ALU.add, r=[TGT, Tgam, C.Tconst], w=[TGT])
                Hs, THs = Hsr.next()
                for c_ in range(2):
                    pb, Tp = pring.next()
                    for p_ in range(4):
                        rs_ = slice(c_ * 64, (c_ + 1) * 64)
                        cs_ = slice(p_ * 128, (p_ + 1) * 128)
                        S.mm(pb[:, p_ * 128:(p_ + 1) * 128], BG[rs_, cs_], U0[rs_, cs_], True, False, r=[TBG, TU0], w=[Tp])
                        S.mm(pb[:, p_ * 128:(p_ + 1) * 128], KG[rs_, cs_], Vb[rs_, cs_], False, True, r=[TKG, TVb], w=[Tp])
                    pv = pb[:].rearrange("p (q t) -> p q t", q=4)
                    S.cp("dve", Hs[0:64, c_:8:2, :], pv[0:64, :, 0:64], r=[Tp], w=[THs])
                    S.cp("act", Hs[64:128, c_:8:2, :], pv[64:128, :, 64:128], r=[Tp], w=[THs])
                if own:
                    y_ps0, Typ0 = pring.next()
                    y_ps1, Typ1 = pring.next()
                ST, TST = stv
                for c_ in range(2):
                    if own:
                        for h in range(8):
                            ps_, p_ = hsl(h)
                            ypb, Typb = (y_ps0, Typ0) if h % 2 == 0 else (y_ps1, Typ1)
                            S.mm(ypb[c_ * 64:(c_ + 1) * 64, p_ * 64:(p_ + 1) * 64], RH[ps_, p_, c_ * 64:(c_ + 1) * 64],
                                 ST[ps_, p_, :], True, True, r=[TRH, TST], w=[Typb])
                    s_ps, Tsp = pring.next()
                    for p_ in range(4):
                        S.mm(s_ps[:, p_ * 64:(p_ + 1) * 64], GT[:, p_ * 2 + c_, :], ST[:, p_, :], True, True, r=[TGT, TST], w=[Tsp])
                    STn, TSTn = STr.next()
                    Hv = Hs[:].rearrange("p (q c) i -> p q c i", c=2)
                    S.tt("dve", STn[:], s_ps[:, 0:256].rearrange("p (q i) -> p q i", q=4), Hv[:, :, c_, :], ALU.add, r=[Tsp, THs], w=[TSTn])
                    ST, TST = STn, TSTn
                    stv[0], stv[1] = ST, TST
                if own:
                    y, Ty_ = f32r.next()
                    y3 = y[:].rearrange("p (h j) -> p h j", h=8)
                    Y03 = Y0[:].rearrange("p (h j) -> p h j", h=8)
                    S.tt("dve", y3[:, 0:8:2, :], y_ps0[:, 0:256].rearrange("p (q j) -> p q j", q=4), Y03[:, 0:8:2, :], ALU.add, r=[Typ0, TY0], w=[Ty_])
                    S.tt("dve", y3[:, 1:8:2, :], y_ps1[:, 0:256].rearrange("p (q j) -> p q j", q=4), Y03[:, 1:8:2, :], ALU.add, r=[Typ1, TY0], w=[Ty_])
                    st1, Tst1 = smr.next()
                    S.op("dve", lambda e, o=st1, i=y3: e.tensor_reduce(out=o[:, 0:8], in_=i, axis=AX.X, op=ALU.add), r=[Ty_], w=[Tst1])
                    sq, Tsq = f32r.next()
                    S.tt("pool", sq[:], y[:], y[:], ALU.mult, r=[Ty_], w=[Tsq])
                    S.op("dve", lambda e, o=st1, i=sq: e.tensor_reduce(out=o[:, 8:16], in_=i[:].rearrange("p (h j) -> p h j", h=8), axis=AX.X, op=ALU.add), r=[Tsq], w=[Tst1])
                    S.ts("dve", st1[:, 0:8], st1[:, 0:8], 1.0 / 64, None, ALU.mult, None, r=[Tst1], w=[Tst1])
                    st2, Tst2 = smr.next()
                    S.tt("dve", st2[:, 0:8], st1[:, 0:8], st1[:, 0:8], ALU.mult, r=[Tst1], w=[Tst2])
                    S.stt(st2[:, 0:8], st1[:, 8:16], 1.0 / 64, st2[:, 0:8], ALU.mult, ALU.subtract, r=[Tst1, Tst2], w=[Tst2])
                    S.act(st2[:, 0:8], st2[:, 0:8], AF.Sqrt, r=[Tst2], w=[Tst2], bias=64e-5, scale=1.0)
                    S.op("dve", lambda e, o=st2: e.reciprocal(out=o[:, 0:8], in_=o[:, 0:8]), r=[Tst2], w=[Tst2])
                    S.tt("dve", y3, y3, st1[:, 0:8].unsqueeze(2).to_broadcast([128, 8, 64]), ALU.subtract, r=[Ty_, Tst1], w=[Ty_])
                    S.tt("dve", y3, y3, st2[:, 0:8].unsqueeze(2).to_broadcast([128, 8, 64]), ALU.mult, r=[Ty_, Tst2], w=[Ty_])
                    S.tt("pool", y[:], y[:], rows["rwkv_gn_g"][:], ALU.mult, r=[Ty_, Tw], w=[Ty_])
                    S.tt("pool", y[:], y[:], rows["rwkv_gn_b"][:], ALU.add, r=[Ty_, Tw], w=[Ty_])
                    S.tt("dve", sq[:].rearrange("p (h j) -> p h j", h=8), Vb[:].rearrange("p (h j) -> p h j", h=8),
                         sm[:, 8:16].unsqueeze(2).to_broadcast([128, 8, 64]), ALU.mult, r=[TVb, Tsm], w=[Tsq])
                    S.tt("pool", y[:], y[:], sq[:], ALU.add, r=[Ty_, Tsq], w=[Ty_])
                    S.tt("dve", y[:], y[:], g_sb[:], ALU.mult, r=[Ty_, Tgs], w=[Ty_])
                    S.dma("sp", d["YR"][(tile - OWN_T) * 128:(tile - OWN_T + 1) * 128, :], y[:], r=[Ty_], w=[Tok()])

        tiles_ = [(g, s) for g in range(SL // 256) for s in range(2)]
        gens = [tile_gen(g, s) for (g, s) in tiles_]

        def run_x(gen):
            for v in gen:
                if v == "XDONE":
                    return

        run_x(gens[0])
        for i in range(len(gens)):
            a = gens[i]
            b = gens[i + 1] if i + 1 < len(gens) else None
            a_done = False
            b_done = b is None
            while not (a_done and b_done):
                if not a_done:
                    try:
                        next(a)
                    except StopIteration:
                        a_done = True
                if not b_done:
                    if next(b) == "XDONE":
                        b_done = True
        S.barrier()


SCALE = 0.125
BIG8 = -240000.0


def nsa_phase(S, C, d):
    nc = C.nc
    SL = C.SL
    NT = SL // 128
    QB0 = NT // 2
    NCMP = SL // 16 - 1
    with ExitStack() as st:
        sb = lambda n, shape, dt: st.enter_context(nc.sbuf_tensor("n_" + n, shape, dt))
        pring = Ring([st.enter_context(nc.psum_tensor("n_ps%d" % i, [128, 512], F32)) for i in range(5)], "nps")
        acc_ps = [st.enter_context(nc.psum_tensor("n_acc%d" % i, [128, 512], F32)) for i in range(3)]
        Tacc = toks(3, "nacc")
        Tc = Tok("nsac")
        identb = sb("identb", [128, 128], BF16)
        S.cp("dve", identb[:], C.ident[:], r=[C.Tconst], w=[Tc])
        ident = C.ident
        jmb = sb("jmb", [128, 128], BF16)
        load_cast(S, "pool", jmb[:], d["jmat"][:, :], w=[Tc])
        RB = sb("RB", [33, 8], F32)
        S.dma("sp", RB[0:32, :], d["rel_bias"][:, :], w=[Tc])
        S.op("act", lambda e: e.mul(out=RB[0:32, :], in_=RB[0:32, :], mul=8.0), r=[Tc], w=[Tc])
        S.memset("dve", RB[32:33, :], BIG8, w=[Tc])
        OHF = sb("OHF", [33, 768], F32)
        S.dma("sp", OHF[:], d["ohf"][:, :], w=[Tc])
        fb_sb = sb("fb", [8, 768], F32)
        for hf in range(2):
            pb, Tp = pring.next()
            S.mm(pb[0:8, 0:384], RB[:, :], OHF[:, hf * 384:(hf + 1) * 384], True, True, r=[Tc], w=[Tp])
            S.cp("dve", fb_sb[:, hf * 384:(hf + 1) * 384], pb[0:8, 0:384], r=[Tp], w=[Tc])
        Tfb = Tok("fbd")
        S.dma("sp", d["FB"][:, :], fb_sb[:], r=[Tc], w=[Tfb])
        FBt = d["FB"].tensor
        Btab = sb("Btab", [128, 2, 3, 512], BF16)
        PatC = sb("PatC", [128, 2, 4, 24], BF16)
        bstg = sb("bstg", [128, 128], F32)
        Tbs = Tok("bstg")
        with nc.allow_non_contiguous_dma(reason="one-time small bias table build"):
            for g in range(2):
                for di, dl in enumerate((0, 1, 4)):
                    for h in range(4):
                        off = (g * 4 + h) * 768 + dl * 128
                        src = bass.AP(FBt, off, [[1, 128], [1, 128]])
                        S.dma("sp", bstg[:], src, r=[Tfb], w=[Tbs])
                        S.cp("dve", Btab[:, g, di, h * 128:(h + 1) * 128], bstg[:], r=[Tbs], w=[Tc])
                for h in range(4):
                    off = (g * 4 + h) * 768 + 96 + 256
                    src = bass.AP(FBt, off, [[1, 128], [-16, 23]])
                    S.dma("sp", bstg[:, 0:23], src, r=[Tfb], w=[Tbs])
                    S.cp("dve", PatC[:, g, h, 0:23], bstg[:, 0:23], r=[Tbs], w=[Tc])
        keyb = sb("keyb", [128, NT], F32)
        S.dma("sp", keyb[:], d["keyb"][:, :], w=[Tc])
        cvrow = sb("cvrow", [1, 512], BF16)
        load_cast(S, "pool", cvrow[:], d["cvrow"][:, :], w=[Tc])
        onesr = sb("onesr", [1, 128], BF16)
        S.memset("dve", onesr[:], 1.0, w=[Tc])
        D0 = sb("D0", [128, 128], F32)
        S.dma("sp", D0[:], d["d0"][:, :], w=[Tc])
        firstr = sb("firstr", [128, 128], F32)
        S.dma("sp", firstr[:], d["firstrow"][:, :], w=[Tc])
        KE = sb("KE", [128, SL], BF16)
        KW = sb("KW", [128, SL], BF16)
        for c0 in range(0, SL, 2048):
            c1 = min(SL, c0 + 2048)
            load_cast(S, "pool", KE[64:128, c0:c1], d["emat2"][:, c0:c1], w=[Tc])
        S.memset("pool", KW[64:128, :], 0.0, w=[Tc])
        oh0 = sb("oh0", [128, 128], BF16)
        S.memset("dve", oh0[:], 0.0, w=[Tc])
        S.memset("dve", oh0[0:1, :], 1.0, w=[Tc])
        cv128 = sb("cv128", [128, 512], BF16)
        S.memset("dve", cv128[:], 0.0, w=[Tc])
        S.cp("dve", cv128[0:1, :], cvrow[:], r=[Tc], w=[Tc])
        w1 = [sb("w1%d" % i, [64, 32, 128], BF16) for i in range(2)]
        w2 = [sb("w2%d" % i, [128, 64], BF16) for i in range(2)]
        peT = [sb("peT%d" % i, [64, 32], BF16) for i in range(2)]
        with nc.allow_non_contiguous_dma(reason="small transposed pe load"):
            for i, (a, b, c) in enumerate((("nsa_w1_k", "nsa_w2_k", "nsa_pe_k"), ("nsa_w1_v", "nsa_w2_v", "nsa_pe_v"))):
                for l0 in range(0, 32, 8):
                    load_cast(S, "pool", w1[i][:, l0:l0 + 8, :], d[a][l0 * 64:(l0 + 8) * 64, :].rearrange("(l d) h -> d l h", d=64), w=[Tc])
                load_cast(S, "pool", w2[i][:, :], d[b][:, :], w=[Tc])
                load_cast(S, "pool", peT[i][:, :], d[c].rearrange("l d -> d l"), w=[Tc])
        cT = sb("cT", [64, SL], BF16)
        vs = sb("vs", [128, NT, 65], BF16)
        vw = sb("vw", [128, NT, 65], BF16)
        KcT = sb("KcT", [128, 512], BF16)
        S.memset("dve", KcT[64:128, :], 0.0, w=[Tc])
        vc = sb("vc", [128, 4, 65], BF16)
        hidb = sb("hidb", [128, 512], BF16)
        Tkv = Tok("kv")
        f32r = Ring([sb("f%d" % i, [128, 512], F32) for i in range(10)], "nf32")
        b16r = Ring([sb("b%d" % i, [128, 512], BF16) for i in range(8)], "nb16")
        pcTr = Ring([sb("pcT%d" % i, [128, 4, 128], BF16) for i in range(4)], "pcT")
        pcbr = Ring([sb("pcb%d" % i, [128, 512], BF16) for i in range(8)], "pcb")
        smr = Ring([sb("s%d" % i, [128, 16], F32) for i in range(12)], "nsm")
        sqr = Ring([sb("q%d" % i, [128, 128], F32) for i in range(10)], "nsq")
        QNs = [sb("QN%d" % i, [128, 2, 512], BF16) for i in range(4)]
        for q_ in QNs:
            S.memset("pool", q_[:], 0.0, w=[Tc])
        nsb = Ring([sb("nsb%d" % i, [128, 2, 128], BF16) for i in range(2)], "nsb")
        TqTs = toks(4, "qT")
        TnsTs = toks(4, "nsT")
        gtr = Ring([sb("gt%d" % i, [128, 24], F32) for i in range(3)], "gt")
        yor = Ring([sb("yo%d" % i, [128, 256], F32) for i in range(2)], "yo")
        otr = Ring([sb("ot%d" % i, [128, 4, 65], F32) for i in range(6)], "ot")

        for g in range(2):
            S.dma("sp", KE[0:64, :], d["KT"][4 + g, :, :], w=[Tkv])
            S.dma("sp", KW[0:64, :], d["KT"][6 + g, :, :], w=[Tkv])
            S.memset("dve", vs[:, :, 64:65], 1.0, w=[Tkv])
            S.memset("dve", vw[:, :, 64:65], 1.0, w=[Tkv])
            S.memset("dve", vc[:, :, 64:65], 1.0, w=[Tkv])
            for t0 in range(0, NT, 8):
                t1 = min(NT, t0 + 8)
                S.dma("sp", vs[:, t0:t1, 0:64], d["VSW"][t0 * 128:t1 * 128, g * 64:(g + 1) * 64].rearrange("(t p) c -> p t c", p=128), w=[Tkv])
                S.dma("sp", vw[:, t0:t1, 0:64], d["VSW"][t0 * 128:t1 * 128, 128 + g * 64:128 + (g + 1) * 64].rearrange("(t p) c -> p t c", p=128), w=[Tkv])
            for i in range(2):
                S.dma("sp", cT[:], d["KT"][i * 2 + g, :, :], w=[Tkv])
                hp, Thp = pring.next()
                src_ap = cT[:]
                for l in range(32):
                    rhs = bass.AP(cT[:].tensor, cT[:, l:l + 1].offset, [list(cT[:].ap[0]), [16, NCMP]])
                    S.mm(hp[:, 0:NCMP], w1[i][:, l, :], rhs, l == 0, l == 31, r=[Tc, Tkv], w=[Thp])
                cp_, Tcp_ = pring.next()
                for l in range(32):
                    S.mm(cp_[:, 0:1], w1[i][:, l, :], peT[i][:, l:l + 1], l == 0, l == 31, r=[Tc], w=[Tcp_])
                cpe, Tcpe = smr.next()
                S.cp("dve", cpe[:, 0:1], cp_[:, 0:1], r=[Tcp_], w=[Tcpe])
                u, Tu = f32r.next()
                S.act(u[:, 0:NCMP], hp[:, 0:NCMP], AF.Identity, r=[Thp, Tcpe], w=[Tu], bias=cpe[:, 0:1], scale=1.0)
                t, Tt = f32r.next()
                S.tt("dve", t[:, 0:NCMP], u[:, 0:NCMP], u[:, 0:NCMP], ALU.mult, r=[Tu], w=[Tt])
                S.ts("dve", t[:, 0:NCMP], t[:, 0:NCMP], 0.044715, 1.0, ALU.mult, ALU.add, r=[Tt], w=[Tt])
                S.tt("dve", t[:, 0:NCMP], t[:, 0:NCMP], u[:, 0:NCMP], ALU.mult, r=[Tt, Tu], w=[Tt])
                S.act(t[:, 0:NCMP], t[:, 0:NCMP], AF.Sigmoid, r=[Tt], w=[Tt], scale=1.5957691216057308)
                S.memset("pool", hidb[:], 0.0, w=[Tkv])
                S.tt("dve", hidb[:, 0:NCMP], t[:, 0:NCMP], u[:, 0:NCMP], ALU.mult, r=[Tt, Tu], w=[Tkv])
                if i == 0:
                    kp, Tkp = pring.next()
                    S.mm(kp[0:64, :], w2[0][:, :], hidb[:, :], True, True, r=[Tc, Tkv], w=[Tkp])
                    S.cp("act", KcT[0:64, :], kp[0:64, :], r=[Tkp], w=[Tkv])
                else:
                    kp, Tkp = pring.next()
                    for c in range(4):
                        S.mm(kp[:, c * 64:(c + 1) * 64], hidb[:, c * 128:(c + 1) * 128], w2[1][:, :], True, True, r=[Tc, Tkv], w=[Tkp])
                    S.cp("act", vc[:, :, 0:64], kp[:, 0:256].rearrange("p (c d) -> p c d", c=4), r=[Tkp], w=[Tkv])
            for t_, Tt_ in zip(f32r.t, f32r.T):
                S.memset("dve", t_[:], 0.0, w=[Tt_])

            def block_gen(qb):
                qo = qb - QB0
                QN = QNs[qb % 4]
                TqT = TqTs[qb % 4]
                TnsT = TnsTs[qb % 4]
                for lh in range(2):
                    S.dma("sp", QN[0:64, lh, :].rearrange("p (h q) -> p h q", h=4),
                          d["QT"][g * 4:(g + 1) * 4, :, qo * 128:(qo + 1) * 128].rearrange("h p q -> p h q"), w=[TqT])
                gt_, Tgt = gtr.next()
                S.dma("sp", gt_[:], d["GATE"][qo * 128:(qo + 1) * 128, :], w=[Tgt])
                nvis = min(8 * qb + 7, NCMP)
                c0 = max(0, 8 * qb - 16)
                c1 = nvis
                oc_ps, Toc = acc_ps[0], Tacc[0]
                zc, Tzc = smr.next()
                es = []
                for h in range(4):
                    lp, Tlp = pring.next()
                    S.mm(lp[:, 0:nvis], QN[:, 0, h * 128:(h + 1) * 128], KcT[:, 0:nvis], True, False, r=[TqT, Tkv], w=[Tlp])
                    S.mm(lp[:, c0:c1], identb[:], PatC[:, g, h, c0 - (8 * qb - 16):c1 - (8 * qb - 16)], False, False, r=[Tc], w=[Tlp])
                    S.mm(lp[:, 0:nvis], oh0[:], cv128[:, 0:nvis], False, True, r=[Tc], w=[Tlp])
                    e, Te = f32r.next()
                    S.act(e[:, 0:nvis], lp[:, 0:nvis], AF.Exp, r=[Tlp], w=[Te, Tzc], scale=SCALE, accum_out=zc[:, h:h + 1])
                    es.append((e, Te))
                pcbs = []
                for h in range(4):
                    e, Te = es[h]
                    pcb, Tpcb = pcbr.next()
                    S.cp("dve", pcb[:], e[:], r=[Te], w=[Tpcb])
                    pcbs.append((pcb, Tpcb))

                def attn_branch(kts, kT_, v_, acc, Tac, bias_fn, extra_fn):
                    LOOK = 2
                    pend = []

                    def logits(kt):
                        lp, Tlp = pring.next()
                        dl = qb - kt
                        bt = bias_fn(dl)
                        if extra_fn is not None:
                            S.mm(lp[:], kT_[:, kt * 128:(kt + 1) * 128], QN[:, kt // 32, :], True, bt is None, r=[Tkv, TqT, TnsT], w=[Tlp])
                        else:
                            S.mm(lp[:], kT_[:, kt * 128:(kt + 1) * 128], QN[:, 0, :], True, bt is None, r=[Tkv, TqT], w=[Tlp])
                        if bt is not None:
                            S.mm(lp[:], jmb[:], bt, False, True, r=[Tc], w=[Tlp])
                        pT, TpT = b16r.next()
                        S.act(pT[:], lp[:], AF.Exp, r=[Tlp, Tc], w=[TpT], scale=SCALE, bias=keyb[:, kt:kt + 1])
                        pend.append((kt, pT, TpT))

                    def pv(first, last):
                        kt, pT, TpT = pend.pop(0)
                        for h in range(4):
                            S.mm(acc[:, h * 65:(h + 1) * 65], pT[:, h * 128:(h + 1) * 128], v_[:, kt, :], first and h == 0, last and h == 3,
                                 r=[TpT, Tkv], w=[Tac])

                    n = len(kts)
                    for i in range(min(LOOK, n)):
                        logits(kts[i])
                    for i in range(n):
                        if i + LOOK < n:
                            logits(kts[i + LOOK])
                        pv(i == 0, i == n - 1)

                ow_ps, Tow = acc_ps[2], Tacc[2]
                kt0 = max(0, qb - 4)
                attn_branch(list(range(kt0, qb + 1)), KW, vw, ow_ps, Tow,
                            lambda dl: Btab[:, g, {0: 0, 1: 1, 4: 2}[dl], :] if dl in (0, 1, 4) else None, None)
                pcsum, Tpcs = f32r.next()
                for h in range(4):
                    e, Te = es[h]
                    S.ts("dve", zc[:, h:h + 1], zc[:, h:h + 1], 1e-30, None, ALU.max, None, r=[Tzc], w=[Tzc])
                    S.op("dve", lambda e_, o=zc, hh=h: e_.reciprocal(out=o[:, 4 + hh:5 + hh], in_=o[:, hh:hh + 1]), r=[Tzc], w=[Tzc])
                    if h == 0:
                        S.ts("dve", pcsum[:], e[:], zc[:, 4:5], None, ALU.mult, None, r=[Te, Tzc], w=[Tpcs])
                    else:
                        S.stt(pcsum[:], e[:], zc[:, 4 + h:5 + h], pcsum[:], ALU.mult, ALU.add, r=[Te, Tzc, Tpcs], w=[Tpcs])
                imp, Timp = sqr.next()
                pc3 = pcsum[:].rearrange("p (j r) -> p j r", r=4)
                S.op("dve", lambda e_, o=imp, i=pc3: e_.tensor_reduce(out=o[:], in_=i, axis=AX.X, op=ALU.add), r=[Tpcs], w=[Timp])
                S.tt("dve", imp[:, 1:128], imp[:, 1:128], pc3[:, 0:127, 3], ALU.add, r=[Tpcs, Timp], w=[Timp])
                val, Tval = sqr.next()
                S.ts("dve", val[:], D0[:], float(2 * qb), None, ALU.is_le, None, r=[Tc], w=[Tval])
                fc, Tfc = sqr.next()
                S.ts("dve", fc[:], D0[:], float(2 * qb - 1), None, ALU.is_ge, None, r=[Tc], w=[Tfc])
                S.tt("dve", fc[:], fc[:], val[:], ALU.mult, r=[Tfc, Tval], w=[Tfc])
                S.tt("dve", fc[:], fc[:], firstr[:], ALU.add, r=[Tfc, Tc], w=[Tfc])
                sc, Tsc = sqr.next()
                S.stt(sc[:], fc[:], 1e4, imp[:], ALU.mult, ALU.max, r=[Tfc, Timp], w=[Tsc])
                S.stt(sc[:], sc[:], 1.0, val[:], ALU.add, ALU.mult, r=[Tsc, Tval], w=[Tsc])
                S.ts("dve", sc[:], sc[:], -1.0, None, ALU.add, None, r=[Tsc], w=[Tsc])
                m8, Tm8 = smr.next()
                S.op("dve", lambda e_, o=m8, i=sc: e_.max(out=o[:, 0:8], in_=i[:]), r=[Tsc], w=[Tm8])
                sc2, Tsc2 = sqr.next()
                S.op("dve", lambda e_, o=sc2, a=m8, i=sc: e_.match_replace(out=o[:], in_to_replace=a[:, 0:8], in_values=i[:], imm_value=-2.0), r=[Tsc, Tm8], w=[Tsc2])
                S.op("dve", lambda e_, o=m8, i=sc2: e_.max(out=o[:, 8:16], in_=i[:]), r=[Tsc2], w=[Tm8])
                nsl, Tnsl = nsb.next()
                S.ts("dve", sc2[:], sc[:], m8[:, 15:16], None, ALU.is_ge, None, r=[Tsc, Tm8], w=[Tsc2])
                S.ts("dve", nsl[:, 0, :], sc2[:], -1.0, -BIG8, ALU.add, ALU.mult, r=[Tsc2], w=[Tnsl])
                S.cp("dve", nsl[:, 1, 0:64], nsl[:, 0, 64:128], r=[Tnsl], w=[Tnsl])
                S.cp("dve", nsl[:, 1, 64:128], nsl[:, 0, 0:64], r=[Tnsl], w=[Tnsl])
                yield "A"
                tm_, Ttm = pring.next()
                tmb = tm_[:].bitcast(BF16)
                S.tr(tmb[:, 0:128], nsl[:, 0, :], identb[:], r=[Tnsl, Tc], w=[Ttm])
                S.tr(tmb[:, 128:256], nsl[:, 1, :], identb[:], r=[Tnsl, Tc], w=[Ttm])
                S.cp("dve", QN[64:128, 1, :].rearrange("p (h q) -> p h q", h=4), tmb[64:128, 0:128].unsqueeze(1).to_broadcast([64, 4, 128]), r=[Ttm], w=[TnsT])
                S.cp("dve", QN[64:128, 0, :].rearrange("p (h q) -> p h q", h=4), tmb[64:128, 128:256].unsqueeze(1).to_broadcast([64, 4, 128]), r=[Ttm], w=[TnsT])
                tps_ = []
                for h in range(4):
                    pcb, Tpcb = pcbs[h]
                    tp_, Ttp = pring.next()
                    tpb = tp_[:].bitcast(BF16)
                    for c in range(4):
                        S.tr(tpb[:, c * 128:(c + 1) * 128], pcb[:, c * 128:(c + 1) * 128], identb[:], r=[Tpcb, Tc], w=[Ttp])
                    tps_.append((tpb, Ttp))
                pcTs = []
                for h in range(4):
                    tpb, Ttp = tps_[h]
                    pcT, TpcT = pcTr.next()
                    S.cp("dve", pcT[:], tpb[:, 0:512].rearrange("p (c q) -> p c q", c=4), r=[Ttp], w=[TpcT])
                    pcTs.append((pcT, TpcT))
                for h in range(4):
                    pcT, TpcT = pcTs[h]
                    for c in range(4):
                        S.mm(oc_ps[:, h * 65:(h + 1) * 65], pcT[:, c, :], vc[:, c, :], c == 0, c == 3, r=[TpcT, Tkv], w=[Toc])
                ow_sb, Tows = otr.next()
                S.cp("dve", ow_sb[:], ow_ps[:, 0:260].rearrange("p (h c) -> p h c", h=4), r=[Tow], w=[Tows])
                oc_sb, Tocs = otr.next()
                S.cp("dve", oc_sb[:], oc_ps[:, 0:260].rearrange("p (h c) -> p h c", h=4), r=[Toc], w=[Tocs])
                yield "B"
                os_ps, Tos = acc_ps[1], Tacc[1]

                attn_branch(list(range(qb + 1)), KE, vs, os_ps, Tos,
                            lambda dl: Btab[:, g, dl, :] if dl <= 1 else None, True)
                os_sb, Toss = otr.next()
                S.cp("act", os_sb[:], os_ps[:, 0:260].rearrange("p (h c) -> p h c", h=4), r=[Tos], w=[Toss])
                yield "S"
                yo, Tyo = yor.next()
                yo3 = yo[:].rearrange("p (h c) -> p h c", h=4)
                cf, Tcf = smr.next()
                g3 = gt_[:].rearrange("p (h b) -> p h b", b=3)
                for bi, (osb, Tosb) in enumerate(((oc_sb, Tocs), (os_sb, Toss), (ow_sb, Tows))):
                    S.ts("dve", cf[:, bi * 4:(bi + 1) * 4], osb[:, :, 64], 1e-30, None, ALU.max, None, r=[Tosb], w=[Tcf])
                    S.op("dve", lambda e_, o=cf, b_=bi: e_.reciprocal(out=o[:, b_ * 4:(b_ + 1) * 4], in_=o[:, b_ * 4:(b_ + 1) * 4]), r=[Tcf], w=[Tcf])
                    S.tt("dve", cf[:, bi * 4:(bi + 1) * 4], cf[:, bi * 4:(bi + 1) * 4], g3[:, g * 4:(g + 1) * 4, bi], ALU.mult, r=[Tcf, Tgt], w=[Tcf])
                    cfb = cf[:, bi * 4:(bi + 1) * 4].unsqueeze(2).to_broadcast([128, 4, 64])
                    if bi == 0:
                        S.tt("dve", yo3, osb[:, :, 0:64], cfb, ALU.mult, r=[Tosb, Tcf], w=[Tyo])
                    else:
                        S.tt("dve", osb[:, :, 0:64], osb[:, :, 0:64], cfb, ALU.mult, r=[Tosb, Tcf], w=[Tosb])
                        S.tt("dve", yo3, yo3, osb[:, :, 0:64], ALU.add, r=[Tosb, Tyo], w=[Tyo])
                S.dma("pool", d["YN"][qo * 128:(qo + 1) * 128, g * 256:(g + 1) * 256], yo[:], r=[Tyo], w=[Tok()])

            gens = [block_gen(qb) for qb in range(QB0, NT)]
            n_ = len(gens)
            next(gens[0])
            next(gens[0])
            for i in range(n_):
                if i + 1 < n_:
                    next(gens[i + 1])
                next(gens[i])
                if i + 1 < n_:
                    next(gens[i + 1])
                for _ in gens[i]:
                    pass
        S.barrier()


def post_phase(S, C, d):
    nc = C.nc
    SL = C.SL
    OWN = SL // 2
    XS = 1.0 / 16.0
    with ExitStack() as st:
        sb = lambda n, shape, dt: st.enter_context(nc.sbuf_tensor("p_" + n, shape, dt))
        pring = Ring([st.enter_context(nc.psum_tensor("p_ps%d" % i, [128, 512], F32)) for i in range(8)], "pps")
        Tw = Tok("pw")
        ident = C.ident
        wout = sb("wout", [128, 8, D], BF16)
        wq = sb("wq", [128, 8, D], BF16)
        wo = sb("wo", [128, 8, D], BF16)
        for dc in range(8):
            load_cast(S, "pool", wout[:, dc, :], d["mix_w_out"][dc * 128:(dc + 1) * 128, :], w=[Tw])
            load_cast(S, "pool", wq[:, dc, :], d["xattn_wq"][dc * 128:(dc + 1) * 128, :], w=[Tw])
            load_cast(S, "pool", wo[:, dc, :], d["xattn_wo"][dc * 128:(dc + 1) * 128, :], w=[Tw])
        lnp = {}
        for n in ["ln2_g", "ln2_b", "ln3_g", "ln3_b"]:
            lnp[n] = sb(n, [128, D], F32)
            S.dma("sp", lnp[n][:], d[n].partition_broadcast(128), w=[C.Tconst])
        KT = sb("KT", [128, 8, 256], BF16)
        V = sb("V", [128, 2, 4, 257], BF16)
        with ExitStack() as st2:
            sb2 = lambda n, shape, dt: st2.enter_context(nc.sbuf_tensor("p2_" + n, shape, dt))
            wk = sb2("wk", [128, 8, D], BF16)
            wv = sb2("wv", [128, 8, D], BF16)
            for dc in range(8):
                load_cast(S, "pool", wk[:, dc, :], d["xattn_wk"][dc * 128:(dc + 1) * 128, :], w=[Tw])
                load_cast(S, "pool", wv[:, dc, :], d["xattn_wv"][dc * 128:(dc + 1) * 128, :], w=[Tw])
            mt_ = sb2("mem", [128, 2, D], F32)
            Tm = Tok("mem")
            memT = sb2("memT", [128, 8, 256], BF16)
            TmT = Tok("memT")
            S.dma("sp", mt_[:], d["mem"].rearrange("(s p) d -> p s d", p=128), w=[Tm])
            for s in range(2):
                for hb in range(2):
                    pb, Tp = pring.next()
                    for j in range(4):
                        dc = hb * 4 + j
                        S.tr(pb[:, j * 128:(j + 1) * 128], mt_[:, s, dc * 128:(dc + 1) * 128], ident[:], r=[Tm, C.Tconst], w=[Tp])
                    S.cp("dve", memT[:, hb * 4:(hb + 1) * 4, s * 128:(s + 1) * 128], pb[:].rearrange("p (j t) -> p j t", j=4), r=[Tp], w=[TmT])
            for cc in range(0, 8, 2):
                pb, Tp = pring.next()
                for k2 in range(2):
                    for dc in range(8):
                        S.mm(pb[:, k2 * 256:(k2 + 1) * 256], wk[:, dc, (cc + k2) * 128:(cc + k2 + 1) * 128], memT[:, dc, :], dc == 0, dc == 7, r=[Tw, TmT], w=[Tp])
                S.cp("act", KT[:, cc:cc + 2, :], pb[:].rearrange("p (k t) -> p k t", k=2), r=[Tp], w=[Tw])
            S.memset("dve", V[:, :, :, 256:257], 1.0, w=[Tw])
            for mt in range(2):
                for nb in range(2):
                    pb, Tp = pring.next()
                    for dc in range(8):
                        S.mm(pb[:], memT[:, dc, mt * 128:(mt + 1) * 128], wv[:, dc, nb * 512:(nb + 1) * 512], dc == 0, dc == 7, r=[Tw, TmT], w=[Tp])
                    S.cp("act", V[:, mt, nb * 2:(nb + 1) * 2, 0:256], pb[:].rearrange("p (h c) -> p h c", h=2), r=[Tp], w=[Tw])
            S.barrier()
        sets = []
        for k_ in range(2):
            W_ = {}
            for n_ in ["ycat", "x1t", "x2t", "oat", "x3t"]:
                W_[n_] = sb("%s%d" % (n_, k_), [128, 2, D], F32)
                W_["T" + n_] = Tok(n_)
            for n_ in ["yT", "x2T", "oT", "QxT"]:
                W_[n_] = sb("%s%d" % (n_, k_), [128, 8, 256], BF16)
                W_["T" + n_] = Tok(n_)
            W_["rr"] = sb("rr%d" % k_, [128, D], F32)
            W_["Trr"] = Tok("rr")
            W_["lnt"] = {"st": sb("lnst%d" % k_, [128, 2, 6], F32), "mv": sb("lnmv%d" % k_, [128, 2], F32),
                         "rstd": sb("lnrs%d" % k_, [128, 1], F32), "T": Tok("lnt")}
            sets.append(W_)
        pTr = Ring([sb("pT%d" % i, [128, 2, 256], BF16) for i in range(4)], "ppT")
        zr = Ring([sb("z%d" % i, [128, 4], F32) for i in range(8)], "pz")

        def transp(src, Tsrc, dst, Tdst):
            k = 0
            for s in range(2):
                for hb in range(2):
                    pb, Tp = pring.next()
                    for j in range(4):
                        dc = hb * 4 + j
                        S.tr(pb[:, j * 128:(j + 1) * 128], src[:, s, dc * 128:(dc + 1) * 128], ident[:], r=[Tsrc, C.Tconst], w=[Tp])
                    k += 1
                    S.cp("act" if k % 2 else "dve", dst[:, hb * 4:(hb + 1) * 4, s * 128:(s + 1) * 128],
                         pb[:].rearrange("p (j t) -> p j t", j=4), r=[Tp], w=[Tdst])

        def proj_res_ln(W_, xT_, TxT_, w_, res, Tres, gname, bname, out_t, Tout):
            rr, Trr, lnt = W_["rr"], W_["Trr"], W_["lnt"]
            for s in range(2):
                for nb in range(2):
                    pb, Tp = pring.next()
                    for dc in range(8):
                        S.mm(pb[:], xT_[:, dc, s * 128:(s + 1) * 128], w_[:, dc, nb * 512:(nb + 1) * 512], dc == 0, dc == 7, r=[TxT_, Tw], w=[Tp])
                    S.stt(rr[:, nb * 512:(nb + 1) * 512], res[:, s, nb * 512:(nb + 1) * 512], ALPHA, pb[:], ALU.mult, ALU.add, r=[Tres, Tp], w=[Trr])
                layer_norm_tile(S, C, rr[:], out_t[:, s, :], lnp[gname], lnp[bname], None, Trr, Tout, lnt)
                yield "P"

        def group_gen(g):
            W_ = sets[g % 2]
            ycat, x1t, x2t, oat, x3t = W_["ycat"], W_["x1t"], W_["x2t"], W_["oat"], W_["x3t"]
            yT, x2T, oT, QxT = W_["yT"], W_["x2T"], W_["oT"], W_["QxT"]
            Tyc, Tx1, Tx2, Toa, Tx3 = W_["Tycat"], W_["Tx1t"], W_["Tx2t"], W_["Toat"], W_["Tx3t"]
            TyT, Tx2T, ToT, TQx = W_["TyT"], W_["Tx2T"], W_["ToT"], W_["TQxT"]
            r0 = g * 256
            S.dma("sp", ycat[:, :, 0:512], d["YR"][r0:r0 + 256, :].rearrange("(s p) c -> p s c", p=128), w=[Tyc])
            S.dma("sp", ycat[:, :, 512:1024], d["YN"][r0:r0 + 256, :].rearrange("(s p) c -> p s c", p=128), w=[Tyc])
            S.dma("sp", x1t[:], d["x1"][OWN + r0:OWN + r0 + 256, :].rearrange("(s p) c -> p s c", p=128), w=[Tx1])
            yield "L"
            transp(ycat, Tyc, yT, TyT)
            yield "T"
            yield from proj_res_ln(W_, yT, TyT, wout, x1t, Tx1, "ln2_g", "ln2_b", x2t, Tx2)
            transp(x2t, Tx2, x2T, Tx2T)
            yield "T"
            for cc in range(0, 8, 2):
                pb, Tp = pring.next()
                for k2 in range(2):
                    for dc in range(8):
                        S.mm(pb[:, k2 * 256:(k2 + 1) * 256], wq[:, dc, (cc + k2) * 128:(cc + k2 + 1) * 128], x2T[:, dc, :], dc == 0, dc == 7, r=[Tw, Tx2T], w=[Tp])
                S.cp("act", QxT[:, cc:cc + 2, :], pb[:].rearrange("p (k t) -> p k t", k=2), r=[Tp], w=[TQx])
            yield "Q"
            for hd in range(4):
                lp, Tlp = pring.next()
                for mt in range(2):
                    for cc in range(2):
                        S.mm(lp[:, mt * 256:(mt + 1) * 256], KT[:, hd * 2 + cc, mt * 128:(mt + 1) * 128], QxT[:, hd * 2 + cc, :], cc == 0, cc == 1, r=[Tw, TQx], w=[Tlp])
                pT, TpT = pTr.next()
                S.act(pT[:], lp[:].rearrange("p (m q) -> p m q", m=2), AF.Exp, r=[Tlp], w=[TpT], scale=XS)
                for s in range(2):
                    op_, Top = pring.next()
                    for mt in range(2):
                        S.mm(op_[:, 0:257], pT[:, mt, s * 128:(s + 1) * 128], V[:, mt, hd, :], mt == 0, mt == 1, r=[TpT, Tw], w=[Top])
                    z, Tz = zr.next()
                    S.op("dve", lambda e_, o=z, i=op_: e_.reciprocal(out=o[:, 0:1], in_=i[:, 256:257]), r=[Top], w=[Tz])
                    S.ts("dve", oat[:, s, hd * 256:(hd + 1) * 256], op_[:, 0:256], z[:, 0:1], None, ALU.mult, None, r=[Top, Tz], w=[Toa])
                yield "A"
            transp(oat, Toa, oT, ToT)
            yield "T"
            yield from proj_res_ln(W_, oT, ToT, wo, x2t, Tx2, "ln3_g", "ln3_b", x3t, Tx3)
            S.dma("sp", d["X3"][r0:r0 + 256, :].rearrange("(s p) c -> p s c", p=128), x3t[:], r=[Tx3], w=[Tok()])

        queue_ = [group_gen(g) for g in range(OWN // 256)]
        active_ = [queue_.pop(0)]
        for _ in range(6):
            next(active_[0])
        while queue_ or active_:
            if len(active_) < 2 and queue_:
                active_.append(queue_.pop(0))
            for gen_ in list(active_):
                try:
                    next(gen_)
                except StopIteration:
                    active_.remove(gen_)
        S.barrier()


def host_consts():
    t = np.arange(128)
    same = (t[:, None] // 64) == (t[None, :] // 64)
    c = {}
    c["ident"] = np.eye(128, dtype=np.float32)
    c["jmat"] = np.ascontiguousarray(np.eye(128, dtype=np.float32)[::-1])
    c["tri_incl"] = (same & (t[:, None] <= t[None, :])).astype(np.float32)
    c["tri_excl"] = (same & (t[:, None] < t[None, :])).astype(np.float32)
    c["tri_after"] = (same & (t[:, None] > t[None, :])).astype(np.float32)
    c["m_su"] = c["tri_excl"].copy()
    c["m_sl"] = c["tri_after"].copy()
    c["m_iu"] = c["tri_incl"].copy()
    c["bdmask"] = same.astype(np.float32)
    sel2 = np.zeros((128, 2), np.float32)
    sel2[63, 0] = 1.0
    sel2[127, 1] = 1.0
    c["sel2"] = sel2
    return c


def t5_bucket_np(dist):
    n = np.maximum(dist, 0)
    nf = np.maximum(n, 16).astype(np.float32)
    large = 16 + (np.log(nf / np.float32(16)) / np.float32(math.log(128 / 16)) * np.float32(16)).astype(np.int32)
    large = np.minimum(large, 31)
    return np.where(n < 16, n, large)


def host_consts2(SL, pad):
    c = {}
    NT = SL // 128
    dist = np.arange(768) - 127
    bk = t5_bucket_np(dist)
    ohf = np.zeros((33, 768), np.float32)
    ohf[bk, np.arange(768)] = 1.0
    ohf[31, :] -= 1.0
    ohf[32, :] = ((dist < 0) | (dist >= 512)).astype(np.float32)
    c["ohf"] = ohf
    tok = np.arange(SL)
    keyb = np.where(tok >= pad, 0.0, NEG).astype(np.float32).reshape(NT, 128).T
    c["keyb"] = np.ascontiguousarray(keyb)
    n = np.arange(512)
    c["cvrow"] = np.where((16 * n >= pad) & (n < SL // 16 - 1), 0.0, BIG8).astype(np.float32).reshape(1, 512)
    i = np.arange(128)
    c["d0"] = (np.arange(128)[None, :] - (i[:, None] >= 64)).astype(np.float32)
    fr = np.zeros((128, 128), np.float32)
    fr[:, pad // 64] = 1.0
    c["firstrow"] = fr
    c["emat2"] = ((np.arange(SL)[None, :] // 64) % 64 == np.arange(64)[:, None]).astype(np.float32)
    return c


def const2_shapes(SL):
    return [("ohf", [33, 768]), ("keyb", [128, SL // 128]), ("cvrow", [1, 512]), ("d0", [128, 128]),
            ("firstrow", [128, 128]), ("emat2", [64, SL])]


CONST_SHAPES = [("ident", [128, 128]), ("jmat", [128, 128]), ("tri_incl", [128, 128]), ("tri_excl", [128, 128]), ("tri_after", [128, 128]),
                ("m_su", [128, 128]), ("m_sl", [128, 128]), ("m_iu", [128, 128]), ("bdmask", [128, 128]), ("sel2", [128, 2])]

WEIGHT_SHAPES = [
    ("ffn1_w_gate", [D, DFF]), ("ffn1_w_up", [D, DFF]), ("ffn1_w_down", [DFF, D]), ("ln1_g", [D]), ("ln1_b", [D]),
    ("mix_w_in", [D, 3096]), ("rwkv_mu", [1792]), ("rwkv_w0", [512]), ("rwkv_w_up", [64, 512]), ("rwkv_a0", [512]),
    ("rwkv_a_up", [64, 512]), ("rwkv_g_up", [128, 512]), ("rwkv_k_k", [512]), ("rwkv_k_a", [512]), ("rwkv_r_k", [512]),
    ("rwkv_gn_g", [512]), ("rwkv_gn_b", [512]),
    ("nsa_pe_k", [32, 64]), ("nsa_w1_k", [2048, 128]), ("nsa_w2_k", [128, 64]),
    ("nsa_pe_v", [32, 64]), ("nsa_w1_v", [2048, 128]), ("nsa_w2_v", [128, 64]),
    ("mix_w_out", [D, D]), ("ln2_g", [D]), ("ln2_b", [D]),
    ("xattn_wq", [D, D]), ("xattn_wk", [D, D]), ("xattn_wv", [D, D]), ("xattn_wo", [D, D]),
    ("ln3_g", [D]), ("ln3_b", [D]),
    ("ffn2_w_gate", [D, DFF]), ("ffn2_w_up", [D, DFF]), ("ffn2_w_down", [DFF, D]), ("ln4_g", [D]), ("ln4_b", [D]),
]


def build(SL=8192, upto=99, debug=False, stop=99):
    nc = bass.Bass("TRN2", target_bir_lowering=False)
    C = Ctx()
    C.nc = nc
    C.SL = SL
    C.stop = stop
    NT = SL // 128
    OWN = SL // 2
    ext = lambda n, shape: nc.dram_tensor(n, shape, F32, kind="ExternalInput").ap()
    kind_scr = "ExternalOutput" if debug else "Internal"
    scr = lambda n, shape, dt=F32: nc.dram_tensor(n, shape, dt, kind=kind_scr).ap()
    d = {}
    d["x"] = ext("x", [SL, D])
    d["valid"] = ext("valid", [128, NT])
    for n, shape in CONST_SHAPES + WEIGHT_SHAPES + const2_shapes(SL):
        d[n] = ext(n, shape)
    d["x1"] = scr("x1", [SL, D])
    d["KT"] = scr("KT", [8, 64, SL], BF16)
    d["QT"] = scr("QT", [8, 64, OWN], BF16)
    d["VSW"] = scr("VSW", [SL, 256], BF16)
    d["GATE"] = scr("GATE", [OWN, 24])
    d["YR"] = scr("YR", [OWN, 512])
    d["YN"] = scr("YN", [OWN, 512])
    d["X3"] = scr("X3", [OWN, D])
    d["FB"] = scr("FB", [8, 768])
    d["mem"] = ext("mem", [256, D])
    d["rel_bias"] = ext("rel_bias", [32, 8])
    out = nc.dram_tensor("out", [OWN, D], F32, kind="ExternalOutput").ap()
    with ExitStack() as st:
        S = Sched(nc, st)
        C.S = S
        C.Tconst = Tok("const")
        C.ident = st.enter_context(nc.sbuf_tensor("sb_ident", [128, 128], F32))
        C.valid = st.enter_context(nc.sbuf_tensor("sb_valid", [128, NT], F32))
        S.dma("sp", C.ident[:], d["ident"][:, :], w=[C.Tconst])
        S.dma("sp", C.valid[:], d["valid"][:, :], w=[C.Tconst])
        ffn_phase(S, C, d["x"], d["x1"], d["ffn1_w_gate"], d["ffn1_w_up"], d["ffn1_w_down"],
                  d["ln1_g"], d["ln1_b"], SL // 256, True, "f1")
        if upto >= 2:
            mix_phase(S, C, d)
        if upto >= 3:
            nsa_phase(S, C, d)
        if upto >= 4:
            post_phase(S, C, d)
        if upto >= 5:
            d["out"] = out
            ffn_phase(S, C, d["X3"], out, d["ffn2_w_gate"], d["ffn2_w_up"], d["ffn2_w_down"],
                      d["ln4_g"], d["ln4_b"], OWN // 256, False, "f2")
        S.barrier()
        print("ninst", S.ninst)
    return nc


_NC_CACHE = {}


def kernel(**inputs):
    SL = 8192
    OWN = SL // 2
    x = np.asarray(inputs["x"], dtype=np.float32)
    B = x.shape[0]
    if "nc" not in _NC_CACHE:
        _NC_CACHE["nc"] = build(SL=SL, upto=99, debug=False)
    nc = _NC_CACHE["nc"]
    base = dict(host_consts())
    for n, shp in WEIGHT_SHAPES:
        a = np.asarray(inputs[n], dtype=np.float32)
        base[n] = np.ascontiguousarray(a.reshape(shp))
    base["rel_bias"] = np.ascontiguousarray(np.asarray(inputs["rel_bias"], dtype=np.float32))
    in_maps = []
    for c in range(8):
        b, half = c // 2, c % 2
        pad = OWN if half == 0 else 0
        m = dict(base)
        xl = np.zeros((SL, D), np.float32)
        if half == 0:
            xl[OWN:] = x[b, :OWN]
        else:
            xl[:] = x[b]
        m["x"] = xl
        valid = np.ones((128, SL // 128), np.float32)
        valid[:, :pad // 128] = 0.0
        m["valid"] = valid
        m["mem"] = np.ascontiguousarray(np.asarray(inputs["mem"], dtype=np.float32)[b])
        m.update(host_consts2(SL, pad))
        in_maps.append(m)
    res = run_bass_kernel_spmd(nc, in_maps, core_ids=list(range(8)))
    out = np.zeros((B, SL, D), np.float32)
    for c in range(8):
        b, half = c // 2, c % 2
        out[b, half * OWN:(half + 1) * OWN] = np.asarray(res.results[c]["out"], dtype=np.float32)
    return out
```

```python
import math
from contextlib import ExitStack
import numpy as np
import concourse.bass as bass
import concourse.mybir as mybir
from concourse.bass_utils import run_bass_kernel_spmd

F32 = mybir.dt.float32
BF16 = mybir.dt.bfloat16
AF = mybir.ActivationFunctionType
ALU = mybir.AluOpType
AX = mybir.AxisListType

D = 1024
DFF = 2816
NFC = DFF // 128
LN_EPS = 1e-5
ALPHA = 2.0 ** 0.25
NEG = -30000.0


class Tok:
    __slots__ = ("name", "w", "r")

    def __init__(self, name=""):
        self.name = name
        self.w = None
        self.r = {}


def toks(n, name=""):
    return [Tok(name + str(i)) for i in range(n)]


class Sched:
    NDMA = 24

    def __init__(self, nc, stack, same_engine_sync=True):
        self.nc = nc
        self.engs = {"pe": nc.tensor, "act": nc.scalar, "dve": nc.vector,
                     "pool": nc.gpsimd, "sp": nc.sync}
        self.sem = {}
        self.cnt = {}
        for k in self.engs:
            self.sem[k] = stack.enter_context(nc.semaphore("s_" + k))
            self.cnt[k] = 0
        for i in range(self.NDMA):
            k = "d%d" % i
            self.sem[k] = stack.enter_context(nc.semaphore("s_" + k))
            self.cnt[k] = 0
        self.seen = {k: {} for k in self.engs}
        self.dma_i = {"sw": 0, "hw": 0}
        self.ses = same_engine_sync
        self.ninst = 0

    def _wait(self, eng, deps):
        e = self.engs[eng]
        seen = self.seen[eng]
        for (c, v) in deps:
            if c == eng and (eng == "pe" or eng == "sp" or not self.ses):
                continue
            if seen.get(c, 0) >= v:
                continue
            e.wait_ge(self.sem[c], v)
            seen[c] = v

    def _deps(self, r, w):
        deps = []
        for t in r:
            if t.w is not None:
                deps.append(t.w)
        for t in w:
            if t.w is not None:
                deps.append(t.w)
            for c, v in t.r.items():
                deps.append((c, v))
        return deps

    def op(self, eng, fn, r=(), w=()):
        self._wait(eng, self._deps(r, w))
        ins = fn(self.engs[eng])
        self.cnt[eng] += 1
        v = self.cnt[eng]
        ins.then_inc(self.sem[eng], 1)
        for t in r:
            t.r[eng] = v
        for t in w:
            t.w = (eng, v)
            t.r = {}
        self.ninst += 1

    def dma(self, q, out, in_, r=(), w=(), **kw):
        if q == "pool":
            slot = "d%d" % (self.dma_i["sw"] % 8)
            self.dma_i["sw"] += 1
        else:
            slot = "d%d" % (8 + self.dma_i["hw"] % (self.NDMA - 8))
            self.dma_i["hw"] += 1
        deps = self._deps(r, w)
        if self.cnt[slot] > 0:
            deps.append((slot, self.cnt[slot]))
        self._wait(q, deps)
        ins = self.engs[q].dma_start(out=out, in_=in_, **kw)
        self.cnt[slot] += 16
        v = self.cnt[slot]
        ins.then_inc(self.sem[slot], 16)
        for t in r:
            t.r[slot] = v
        for t in w:
            t.w = (slot, v)
            t.r = {}
        self.ninst += 1

    def barrier(self, engs=("pe", "act", "dve", "pool", "sp")):
        allc = [(c, v) for c, v in self.cnt.items() if v > 0]
        for e in engs:
            self._wait(e, [(c, v) for (c, v) in allc if c != e or e not in ("pe", "sp")])

    def finish(self, tks, eng="sp"):
        deps = [t.w for t in tks if t.w is not None]
        self._wait(eng, deps)

    def mm(self, out, lhsT, rhs, start, stop, r, w):
        self.op("pe", lambda e: e.matmul(out, lhsT=lhsT, rhs=rhs, start=start, stop=stop,
                                         skip_group_check=True), r=r, w=w)

    def tr(self, out, in_, ident, r, w):
        self.op("pe", lambda e: e.transpose(out, in_, ident), r=r, w=w)

    def act(self, out, in_, func, r, w, eng="act", **kw):
        self.op(eng, lambda e: e.activation(out=out, in_=in_, func=func, **kw), r=r, w=w)

    def tt(self, eng, out, in0, in1, op, r, w):
        self.op(eng, lambda e: e.tensor_tensor(out=out, in0=in0, in1=in1, op=op), r=r, w=w)

    def ts(self, eng, out, in0, s1, s2, op0, op1, r, w, **kw):
        if op1 is None:
            self.op(eng, lambda e: e.tensor_scalar(out=out, in0=in0, scalar1=s1, scalar2=None, op0=op0, **kw), r=r, w=w)
        else:
            self.op(eng, lambda e: e.tensor_scalar(out=out, in0=in0, scalar1=s1, scalar2=s2, op0=op0, op1=op1, **kw), r=r, w=w)

    def stt(self, out, in0, scalar, in1, op0, op1, r, w):
        self.op("dve", lambda e: e.scalar_tensor_tensor(out=out, in0=in0, scalar=scalar, in1=in1, op0=op0, op1=op1), r=r, w=w)

    def cp(self, eng, out, in_, r, w):
        if eng == "act":
            self.op("act", lambda e: e.copy(out=out, in_=in_), r=r, w=w)
        else:
            self.op(eng, lambda e: e.tensor_copy(out=out, in_=in_), r=r, w=w)

    def memset(self, eng, ap, val, w):
        self.op(eng, lambda e: e.memset(ap, val), r=(), w=w)


class Ctx:
    pass


def load_cast(S, q, dst, src, w, r=()):
    S.dma("pool", dst, src, r=r, w=w, max_dma_last_dim=4096)


def layer_norm_tile(S, C, r_ap, out_ap, g_t, b_t, valid_ap, Tr, Tout, tmp):
    nc = C.nc
    st, mv, rstd = tmp["st"], tmp["mv"], tmp["rstd"]
    Tst = tmp["T"]
    S.op("dve", lambda e: e.bn_stats(out=st[:, 0, :], in_=r_ap[:, 0:512]), r=[Tr], w=[Tst])
    S.op("dve", lambda e: e.bn_stats(out=st[:, 1, :], in_=r_ap[:, 512:1024]), r=[Tr], w=[Tst])
    S.op("dve", lambda e: e.bn_aggr(out=mv[:], in_=st[:]), r=[Tst], w=[Tst])
    S.act(rstd[:], mv[:, 1:2], AF.Sqrt, r=[Tst], w=[Tst], bias=LN_EPS, scale=1.0)
    S.op("dve", lambda e: e.reciprocal(out=rstd[:], in_=rstd[:]), r=[Tst], w=[Tst])
    S.ts("dve", r_ap, r_ap, mv[:, 0:1], rstd[:, 0:1], ALU.subtract, ALU.mult, r=[Tst, Tr], w=[Tr])
    if valid_ap is not None:
        S.stt(r_ap, r_ap, valid_ap, g_t[:], ALU.mult, ALU.mult, r=[Tr, C.Tconst], w=[Tr])
        S.stt(out_ap, b_t[:], valid_ap, r_ap, ALU.mult, ALU.add, r=[Tr, C.Tconst], w=[Tout])
    else:
        S.tt("dve", r_ap, r_ap, g_t[:], ALU.mult, r=[Tr, C.Tconst], w=[Tr])
        S.tt("dve", out_ap, r_ap, b_t[:], ALU.add, r=[Tr, C.Tconst], w=[Tout])


def load_xT(S, C, src_rows_ap, xt, Txt, xT, TxT, tps, Ttps, ident, TG=2):
    S.dma("sp", xt[:], src_rows_ap.rearrange("(s p) d -> p s d", p=128), w=[Txt])
    make_xT(S, C, xt, Txt, xT, TxT, tps, Ttps, ident, TG)


def make_xT(S, C, xt, Txt, xT, TxT, tps, Ttps, ident, TG=2):
    k = 0
    for s in range(TG):
        for hb in range(2):
            pb = tps[k % 2]
            Tp = Ttps[k % 2]
            k += 1
            for j in range(4):
                dc = hb * 4 + j
                S.tr(pb[:, j * 128:(j + 1) * 128], xt[:, s, dc * 128:(dc + 1) * 128], ident[:], r=[Txt, C.Tconst], w=[Tp])
            eng = "dve" if (k % 2) else "act"
            S.cp(eng, xT[:, hb * 4:(hb + 1) * 4, s * 128:(s + 1) * 128],
                 pb[:].rearrange("p (j t) -> p j t", j=4), r=[Tp], w=[TxT])


def ffn_phase(S, C, src, dst, wg_d, wu_d, wd_d, g_d, b_d, ngroups, use_valid, name):
    nc = C.nc
    TG = 2
    GT = TG * 128
    with ExitStack() as st:
        sb = lambda n, shape, dt: st.enter_context(nc.sbuf_tensor(name + n, shape, dt))
        ps = lambda n: st.enter_context(nc.psum_tensor(name + n, [128, 512], F32))
        wg = sb("wg", [128, 8, DFF], BF16)
        wu = sb("wu", [128, 8, DFF], BF16)
        wd = sb("wd", [128, NFC, D], BF16)
        Twg, Twu, Twd = toks(8, "wg"), toks(8, "wu"), toks(NFC, "wd")
        gt = sb("g", [128, D], F32)
        bt = sb("b", [128, D], F32)
        S.dma("sp", gt[:], g_d.partition_broadcast(128), w=[C.Tconst])
        S.dma("sp", bt[:], b_d.partition_broadcast(128), w=[C.Tconst])
        for dc in range(8):
            load_cast(S, "pool", wg[:, dc, :], wg_d[dc * 128:(dc + 1) * 128, :], w=[Twg[dc]])
            load_cast(S, "pool", wu[:, dc, :], wu_d[dc * 128:(dc + 1) * 128, :], w=[Twu[dc]])
        for fc in range(NFC):
            load_cast(S, "pool", wd[:, fc, :], wd_d[fc * 128:(fc + 1) * 128, :], w=[Twd[fc]])
        xt = [sb("xt%d" % i, [128, TG, D], F32) for i in range(2)]
        Txt = toks(2, "xt")
        xT = [sb("xT%d" % i, [128, 8, GT], BF16) for i in range(2)]
        TxT = toks(2, "xT")
        hT = [sb("hT%d" % i, [128, NFC, GT], BF16) for i in range(2)]
        ThT = toks(2, "hT")
        sg = [sb("sg%d" % i, [128, GT], F32) for i in range(2)]
        Tsg = toks(2, "sg")
        rr = [sb("rr%d" % i, [128, D], F32) for i in range(2)]
        Trr = toks(2, "rr")
        oo = [sb("oo%d" % i, [128, D], F32) for i in range(2)]
        Too = toks(2, "oo")
        lnt = {"st": sb("lnst", [128, 2, 6], F32), "mv": sb("lnmv", [128, 2], F32),
               "rstd": sb("lnrs", [128, 1], F32), "T": Tok("lnt")}
        tps = [ps("tp0"), ps("tp1")]
        Ttps = toks(2, "tp")
        gups = [ps("g0"), ps("g1")]
        Tgu = toks(2, "gu")
        yps = [ps("y0"), ps("y1")]
        Ty = toks(2, "y")
        ti = 0
        for g in range(ngroups):
            b2 = g % 2
            load_xT(S, C, src[g * GT:(g + 1) * GT, :], xt[b2], Txt[b2], xT[b2], TxT[b2], tps, Ttps, C.ident, TG)
            for fc in range(NFC):
                p2 = fc % 2
                for dc in range(8):
                    S.mm(gups[p2][:, 0:GT], wg[:, dc, fc * 128:(fc + 1) * 128], xT[b2][:, dc, :], dc == 0, dc == 7,
                         r=[Twg[dc], TxT[b2]], w=[Tgu[p2]])
                for dc in range(8):
                    S.mm(gups[p2][:, GT:2 * GT], wu[:, dc, fc * 128:(fc + 1) * 128], xT[b2][:, dc, :], dc == 0, dc == 7,
                         r=[Twu[dc], TxT[b2]], w=[Tgu[p2]])
                S.act(sg[p2][:], gups[p2][:, 0:GT], AF.Silu, r=[Tgu[p2]], w=[Tsg[p2]])
                S.tt("dve", hT[b2][:, fc, :], sg[p2][:], gups[p2][:, GT:2 * GT], ALU.mult, r=[Tsg[p2], Tgu[p2]], w=[ThT[b2]])
            for s in range(TG):
                r2 = ti % 2
                ti += 1
                for nb in range(2):
                    for fc in range(NFC):
                        S.mm(yps[nb][:], hT[b2][:, fc, s * 128:(s + 1) * 128], wd[:, fc, nb * 512:(nb + 1) * 512],
                             fc == 0, fc == NFC - 1, r=[ThT[b2], Twd[fc]], w=[Ty[nb]])
                for nb in range(2):
                    S.act(rr[r2][:, nb * 512:(nb + 1) * 512], yps[nb][:], AF.Identity, r=[Ty[nb]], w=[Trr[r2]], scale=0.5)
                S.stt(rr[r2][:], xt[b2][:, s, :], ALPHA, rr[r2][:], ALU.mult, ALU.add, r=[Txt[b2], Trr[r2]], w=[Trr[r2]])
                tile = g * TG + s
                vap = C.valid[:, tile:tile + 1] if use_valid else None
                layer_norm_tile(S, C, rr[r2][:], oo[r2][:], gt, bt, vap, Trr[r2], Too[r2], lnt)
                S.dma("pool", dst[tile * 128:(tile + 1) * 128, :], oo[r2][:], r=[Too[r2]], w=[Tok()])
        S.barrier()


class Ring:
    def __init__(self, tiles, name):
        self.t = tiles
        self.T = toks(len(tiles), name)
        self.i = 0

    def next(self):
        i = self.i % len(self.t)
        self.i += 1
        return self.t[i], self.T[i]


HD = 64
NSA_OFF = 1792
SG_C = -0.6065306597126334


def mix_phase(S, C, d):
    nc = C.nc
    SL = C.SL
    NT = SL // 128
    OWN_T = NT // 2
    with ExitStack() as st:
        sb = lambda n, shape, dt: st.enter_context(nc.sbuf_tensor("m_" + n, shape, dt))
        pring = Ring([st.enter_context(nc.psum_tensor("m_ps%d" % i, [128, 512], F32)) for i in range(8)], "mps")
        Tw = Tok("mixw")
        W = sb("W", [128, 8, 3096], BF16)
        mub = sb("mub", [128, 1536], BF16)
        omb = sb("omb", [128, 1536], BF16)
        mucol = sb("mucol", [128, 4], F32)
        load_cast(S, "pool", mub[:], d["rwkv_mu"][0:1536].partition_broadcast(128), w=[Tw])
        S.ts("dve", omb[:], mub[:], -1.0, 1.0, ALU.mult, ALU.add, r=[Tw], w=[Tw])
        for c_ in range(2):
            S.dma("sp", mucol[:, c_:c_ + 1], d["rwkv_mu"][1536 + c_ * 128:1536 + (c_ + 1) * 128].rearrange("(p o) -> p o", o=1), w=[Tw])
        S.ts("dve", mucol[:, 2:4], mucol[:, 0:2], -1.0, 1.0, ALU.mult, ALU.add, r=[Tw], w=[Tw])
        for dc in range(8):
            load_cast(S, "pool", W[:, dc, :], d["mix_w_in"][dc * 128:(dc + 1) * 128, :], w=[Tw])
        lup = sb("lup", [128, 512], BF16)
        gup = sb("gup", [128, 512], BF16)
        load_cast(S, "pool", lup[0:64, :], d["rwkv_w_up"][:, :], w=[Tw])
        load_cast(S, "pool", lup[64:128, :], d["rwkv_a_up"][:, :], w=[Tw])
        load_cast(S, "pool", gup[:, :], d["rwkv_g_up"][:, :], w=[Tw])
        rows = {}
        for n in ["rwkv_w0", "rwkv_a0", "rwkv_k_k", "rwkv_k_a", "rwkv_r_k", "rwkv_gn_g", "rwkv_gn_b"]:
            rows[n] = sb(n, [128, 512], F32)
            S.dma("sp", rows[n][:], d[n].partition_broadcast(128), w=[Tw])
        cst = {}
        for n in ["tri_incl", "tri_excl", "tri_after", "m_su", "m_sl", "m_iu", "bdmask"]:
            cst[n] = sb(n, [128, 128], F32)
            S.dma("sp", cst[n][:], d[n][:, :], w=[Tw])
        sel2 = sb("sel2", [128, 2], F32)
        S.dma("sp", sel2[:], d["sel2"][:, :], w=[Tw])
        identb = sb("identb", [128, 128], BF16)
        S.cp("dve", identb[:], C.ident[:], r=[C.Tconst], w=[Tw])
        ident = C.ident

        def bc4(t):
            return t[:].unsqueeze(1).to_broadcast([128, 4, 128])

        def bc8(t):
            return t[:].unsqueeze(1).to_broadcast([128, 8, 128])

        xt = sb("xt", [128, 2, D], BF16)
        Txt = Tok("xt")
        xTe = [sb("xTe%d" % i, [128, 8, 257], BF16) for i in range(2)]
        TxTe = toks(2, "xTe")
        f32r = Ring([sb("f%d" % i, [128, 512], F32) for i in range(22)], "f32r")
        b16r = Ring([sb("h%d" % i, [128, 512], BF16) for i in range(21)], "b16r")
        mr = Ring([sb("M%d" % i, [128, 8, 128], BF16) for i in range(8)], "mr")
        keepr = Ring([sb("MK%d" % i, [128, 8, 128], BF16) for i in range(3)], "keepr")
        xtr = Ring([sb("XT%d" % i, [128, 4, 128], BF16) for i in range(8)], "xtr")
        smr = Ring([sb("sm%d" % i, [128, 16], F32) for i in range(12)], "smr")
        lw = [sb("lw%d" % i, [128, 256], BF16) for i in range(2)]
        lg = [sb("lg%d" % i, [128, 256], BF16) for i in range(2)]
        Tl = toks(2, "lora")
        STr = Ring([sb("ST%d" % i, [128, 4, 64], F32) for i in range(3)], "ST")
        GTr = Ring([sb("GT%d" % i, [128, 8, 128], F32) for i in range(1)], "GT")
        Hsr = Ring([sb("Hs%d" % i, [128, 8, 64], F32) for i in range(1)], "Hs")
        RHr = Ring([sb("RH%d" % i, [128, 4, 128], F32) for i in range(1)], "RH")
        kst = sb("kst", [64, 8, 256], BF16)
        Tkst = Tok("kst")
        qst = sb("qst", [64, 8, 256], BF16)
        Tqst = Tok("qst")
        vst = Ring([sb("vst%d" % i, [128, 256], BF16) for i in range(2)], "vst")
        gst = Ring([sb("gst%d" % i, [128, 24], F32) for i in range(2)], "gst")

        ST, TST = STr.next()
        S.memset("dve", ST[:], 0.0, w=[TST])
        stv = [ST, TST]
        S.memset("dve", xTe[1][:, :, 256:257], 0.0, w=[TxTe[1]])

        def evac_eng(k):
            return "act" if k % 2 else "dve"

        ek = [0]

        def tile_gen(g, s):
            b2 = g % 2
            own_g = (g * 2) >= OWN_T
            xcur = lambda dc, a, b, _x=xTe[b2]: _x[:, dc, 1 + a:1 + b]
            xprv = lambda dc, a, b, _x=xTe[b2]: _x[:, dc, a:b]
            TX = TxTe[b2]
            if s == 0:
                load_cast(S, "pool", xt[:], d["x1"][g * 256:(g + 1) * 256, :].rearrange("(s p) d -> p s d", p=128), w=[Txt])
                S.cp("pool", xTe[b2][:, :, 0:1], xTe[1 - b2][:, :, 256:257], r=[TxTe[1 - b2]], w=[TxTe[b2]])
                for s_ in range(2):
                    for hb in range(2):
                        pb, Tp = pring.next()
                        pbb_ = pb[:].bitcast(BF16)
                        for j in range(4):
                            dc = hb * 4 + j
                            S.tr(pbb_[:, j * 128:(j + 1) * 128], xt[:, s_, dc * 128:(dc + 1) * 128], identb[:], r=[Txt, Tw], w=[Tp])
                        ek[0] += 1
                        S.cp(evac_eng(ek[0]), xTe[b2][:, hb * 4:(hb + 1) * 4, 1 + s_ * 128:1 + (s_ + 1) * 128],
                             pbb_[:, 0:512].rearrange("p (j t) -> p j t", j=4), r=[Tp], w=[TxTe[b2]])
                pbz, Tpz = pring.next()
                pbp, Tpp = pring.next()
                for half, c0 in ((0, 1536), (1, 1664)):
                    for dc in range(8):
                        S.mm(pbz[:, half * 256:(half + 1) * 256], W[:, dc, c0:c0 + 128], xcur(dc, 0, 256), dc == 0, dc == 7, r=[Tw, TX], w=[Tpz])
                    for dc in range(8):
                        S.mm(pbp[:, half * 256:(half + 1) * 256], W[:, dc, c0:c0 + 128], xprv(dc, 0, 256), dc == 0, dc == 7, r=[Tw, TX], w=[Tpp])
                lz, Tlz = f32r.next()
                for half in range(2):
                    hs = slice(half * 256, (half + 1) * 256)
                    S.act(lz[:, hs], pbz[:, hs], AF.Identity, r=[Tpz, Tw], w=[Tlz], scale=mucol[:, 2 + half:3 + half])
                    S.stt(lz[:, hs], pbp[:, hs], mucol[:, half:half + 1], lz[:, hs], ALU.mult, ALU.add, r=[Tpp, Tlz, Tw], w=[Tlz])
                S.act(lw[b2][0:64, :], lz[0:64, 0:256], AF.Tanh, r=[Tlz], w=[Tl[b2]])
                S.cp("dve", lw[b2][64:128, :], lz[64:128, 0:256], r=[Tlz], w=[Tl[b2]])
                S.act(lg[b2][:], lz[:, 256:512], AF.Sigmoid, r=[Tlz], w=[Tl[b2]])
                slots = [512, 576, 640, 704, 768, 832, 1024, 1088]
                for bk in range(4):
                    pb, Tp = pring.next()
                    for k2 in range(2):
                        c0 = NSA_OFF + slots[bk * 2 + k2]
                        for dc in range(8):
                            S.mm(pb[0:64, k2 * 256:(k2 + 1) * 256], W[:, dc, c0:c0 + 64], xcur(dc, 0, 256), dc == 0, dc == 7, r=[Tw, TX], w=[Tp])
                    ek[0] += 1
                    S.cp(evac_eng(ek[0]), kst[:, bk * 2:(bk + 1) * 2, :], pb[0:64, :].rearrange("p (k t) -> p k t", k=2), r=[Tp], w=[Tkst])
                for k_ in range(8):
                    S.dma("sp", d["KT"][k_, :, g * 256:(g + 1) * 256], kst[:, k_, :], r=[Tkst], w=[Tok()])
                if own_g:
                    go = g - OWN_T // 2
                    for bk in range(4):
                        pb, Tp = pring.next()
                        for k2 in range(2):
                            c0 = NSA_OFF + (bk * 2 + k2) * 64
                            for dc in range(8):
                                S.mm(pb[0:64, k2 * 256:(k2 + 1) * 256], W[:, dc, c0:c0 + 64], xcur(dc, 0, 256), dc == 0, dc == 7, r=[Tw, TX], w=[Tp])
                        ek[0] += 1
                        S.cp(evac_eng(ek[0]), qst[:, bk * 2:(bk + 1) * 2, :], pb[0:64, :].rearrange("p (k t) -> p k t", k=2), r=[Tp], w=[Tqst])
                    for k_ in range(8):
                        S.dma("sp", d["QT"][k_, :, go * 256:(go + 1) * 256], qst[:, k_, :], r=[Tqst], w=[Tok()])
                yield "G"
            for _once in (0,):
                tile = g * 2 + s
                own = tile >= OWN_T
                a0, a1 = s * 128, (s + 1) * 128
                pb, Tp = pring.next()
                for k2, c0 in ((0, NSA_OFF + 896), (1, NSA_OFF + 1152)):
                    for dc in range(8):
                        S.mm(pb[:, k2 * 128:(k2 + 1) * 128], xcur(dc, a0, a1), W[:, dc, c0:c0 + 128], dc == 0, dc == 7, r=[Tw, TX], w=[Tp])
                if own:
                    for dc in range(8):
                        S.mm(pb[:, 256:280], xcur(dc, a0, a1), W[:, dc, NSA_OFF + 1280:NSA_OFF + 1304], dc == 0, dc == 7, r=[Tw, TX], w=[Tp])
                vt, Tv = vst.next()
                S.cp("act", vt[:], pb[:, 0:256], r=[Tp], w=[Tv])
                S.dma("sp", d["VSW"][tile * 128:(tile + 1) * 128, :], vt[:], r=[Tv], w=[Tok()])
                if own:
                    gt_, Tg_ = gst.next()
                    S.act(gt_[:], pb[:, 256:280], AF.Sigmoid, r=[Tp], w=[Tg_])
                    S.dma("sp", d["GATE"][(tile - OWN_T) * 128:(tile - OWN_T + 1) * 128, :], gt_[:], r=[Tg_], w=[Tok()])
                yield "S"
                sbs = []
                for q in range(3):
                    pbz, Tpz = pring.next()
                    pbp, Tpp = pring.next()
                    for dc in range(8):
                        S.mm(pbz[:], xcur(dc, a0, a1), W[:, dc, q * 512:(q + 1) * 512], dc == 0, dc == 7, r=[Tw, TX], w=[Tpz])
                    for dc in range(8):
                        S.mm(pbp[:], xprv(dc, a0, a1), W[:, dc, q * 512:(q + 1) * 512], dc == 0, dc == 7, r=[Tw, TX], w=[Tpp])
                    z_sb, Tzs = f32r.next()
                    S.tt("dve", z_sb[:], pbz[:], omb[:, q * 512:(q + 1) * 512], ALU.mult, r=[Tpz, Tw], w=[Tzs])
                    zp, Tzp = f32r.next()
                    S.tt("dve", zp[:], pbp[:], mub[:, q * 512:(q + 1) * 512], ALU.mult, r=[Tpp, Tw], w=[Tzp])
                    S.tt("pool", z_sb[:], z_sb[:], zp[:], ALU.add, r=[Tzs, Tzp], w=[Tzs])
                    sbs.append((z_sb, Tzs))
                (r_sb, Trs), (k_sb, Tks), (v_sb, Tvs) = sbs
                yield "S"
                w_ps, Twp = pring.next()
                S.mm(w_ps[:], lw[b2][0:64, a0:a1], lup[0:64, :], True, True, r=[Tl[b2], Tw], w=[Twp])
                a_ps, Tap = pring.next()
                S.mm(a_ps[:], lw[b2][64:128, a0:a1], lup[64:128, :], True, True, r=[Tl[b2], Tw], w=[Tap])
                Lt, TLt = f32r.next()
                S.tt("dve", Lt[:], w_ps[:], rows["rwkv_w0"][:], ALU.add, r=[Twp, Tw], w=[TLt])
                S.act(Lt[:], Lt[:], AF.Sigmoid, r=[TLt], w=[TLt])
                asg, Tas = f32r.next()
                S.tt("dve", asg[:], a_ps[:], rows["rwkv_a0"][:], ALU.add, r=[Tap, Tw], w=[Tas])
                S.act(asg[:], asg[:], AF.Sigmoid, r=[Tas], w=[Tas])
                if own:
                    g_ps, Tgp = pring.next()
                    S.mm(g_ps[:], lg[b2][:, a0:a1], gup[:, :], True, True, r=[Tl[b2], Tw], w=[Tgp])
                    g_sb, Tgs = b16r.next()
                    S.cp("act", g_sb[:], g_ps[:], r=[Tgp], w=[Tgs])
                yield "S"
                kk, Tkk = f32r.next()
                S.tt("pool", kk[:], k_sb[:], rows["rwkv_k_k"][:], ALU.mult, r=[Tks, Tw], w=[Tkk])
                tmp, Ttmp = f32r.next()
                S.tt("pool", tmp[:], kk[:], kk[:], ALU.mult, r=[Tkk], w=[Ttmp])
                sm, Tsm = smr.next()
                S.op("dve", lambda e, o=sm, i=tmp: e.tensor_reduce(out=o[:, 0:8], in_=i[:].rearrange("p (h j) -> p h j", h=8), axis=AX.X, op=ALU.add), r=[Ttmp], w=[Tsm])
                S.ts("dve", sm[:, 0:8], sm[:, 0:8], 1e-24, None, ALU.max, None, r=[Tsm], w=[Tsm])
                S.act(sm[:, 0:8], sm[:, 0:8], AF.Sqrt, r=[Tsm], w=[Tsm])
                S.op("dve", lambda e, o=sm: e.reciprocal(out=o[:, 0:8], in_=o[:, 0:8]), r=[Tsm], w=[Tsm])
                kk3 = kk[:].rearrange("p (h j) -> p h j", h=8)
                S.tt("dve", kk3, kk3, sm[:, 0:8].unsqueeze(2).to_broadcast([128, 8, 64]), ALU.mult, r=[Tkk, Tsm], w=[Tkk])
                yield "S"
                km, Tkm = f32r.next()
                S.stt(km[:], asg[:], -1.0, rows["rwkv_k_a"][:], ALU.add, ALU.mult, r=[Tas, Tw], w=[Tkm])
                S.stt(km[:], km[:], 1.0, k_sb[:], ALU.add, ALU.mult, r=[Tkm, Tks], w=[Tkm])
                yield "S"
                bv, Tbv = f32r.next()
                S.tt("pool", bv[:], kk[:], asg[:], ALU.mult, r=[Tkk, Tas], w=[Tbv])
                yield "S"
                if own:
                    bt_, Tbt = f32r.next()
                    S.tt("pool", bt_[:], r_sb[:], km[:], ALU.mult, r=[Trs, Tkm], w=[Tbt])
                    S.tt("pool", bt_[:], bt_[:], rows["rwkv_r_k"][:], ALU.mult, r=[Tbt, Tw], w=[Tbt])
                    S.op("dve", lambda e, o=sm, i=bt_: e.tensor_reduce(out=o[:, 8:16], in_=i[:].rearrange("p (h j) -> p h j", h=8), axis=AX.X, op=ALU.add), r=[Tbt], w=[Tsm])
                yield "S"
                c_ps, Tcp = pring.next()
                S.mm(c_ps[:], cst["tri_incl"][:], Lt[:], True, True, r=[Tw, TLt], w=[Tcp])
                x_ps, Txp = pring.next()
                S.mm(x_ps[:], cst["tri_excl"][:], Lt[:], True, True, r=[Tw, TLt], w=[Txp])
                d_ps, Tdp = pring.next()
                S.mm(d_ps[:], cst["tri_after"][:], Lt[:], True, True, r=[Tw, TLt], w=[Tdp])
                EL, TEL = f32r.next()
                ENL, TENL = f32r.next()
                ELm, TELm = f32r.next()
                EG, TEG = f32r.next()
                S.act(EL[:], c_ps[:], AF.Exp, r=[Tcp], w=[TEL], scale=SG_C)
                S.act(ENL[:], c_ps[:], AF.Exp, r=[Tcp], w=[TENL], scale=-SG_C)
                S.act(ELm[:], x_ps[:], AF.Exp, r=[Txp], w=[TELm], scale=SG_C)
                S.act(EG[:], d_ps[:], AF.Exp, r=[Tdp], w=[TEG], scale=SG_C)
                pbg, Tpg = pring.next()
                for p_ in range(4):
                    S.mm(pbg[:, p_ * 2:(p_ + 1) * 2], EL[:, p_ * 128:(p_ + 1) * 128], sel2[:], True, True, r=[TEL, Tw], w=[Tpg])
                gam, Tgam = smr.next()
                S.cp("dve", gam[:, 0:8], pbg[:, 0:8], r=[Tpg], w=[Tgam])
                yield "S"
                RT, TRT = b16r.next()
                KT, TKT = b16r.next()
                BT, TBT = b16r.next()
                AT, TAT = b16r.next()
                BG, TBG = b16r.next()
                KG, TKG = b16r.next()
                Vb, TVb = b16r.next()
                S.tt("dve", RT[:], r_sb[:], EL[:], ALU.mult, r=[Trs, TEL], w=[TRT])
                S.tt("pool", KT[:], km[:], ENL[:], ALU.mult, r=[Tkm, TENL], w=[TKT])
                S.tt("dve", BT[:], bv[:], ENL[:], ALU.mult, r=[Tbv, TENL], w=[TBT])
                S.stt(AT[:], kk[:], -1.0, ELm[:], ALU.mult, ALU.mult, r=[Tkk, TELm], w=[TAT])
                S.tt("pool", BG[:], bv[:], EG[:], ALU.mult, r=[Tbv, TEG], w=[TBG])
                S.tt("dve", KG[:], km[:], EG[:], ALU.mult, r=[Tkm, TEG], w=[TKG])
                S.cp("pool", Vb[:], v_sb[:], r=[Tvs], w=[TVb])
                yield "S"
                XT = {}
                for nm, X, TXq in (("r", RT, TRT), ("k", KT, TKT), ("b", BT, TBT), ("a", AT, TAT)):
                    pb, Tp = pring.next()
                    pbb = pb[:].bitcast(BF16)
                    for p in range(4):
                        S.tr(pbb[:, p * 128:(p + 1) * 128], X[:, p * 128:(p + 1) * 128], identb[:], r=[TXq, Tw], w=[Tp])
                    xT_, TxT_ = xtr.next()
                    ek[0] += 1
                    S.cp(evac_eng(ek[0]), xT_[:], pbb[:, 0:512].rearrange("p (q t) -> p q t", q=4), r=[Tp], w=[TxT_])
                    XT[nm] = (xT_, TxT_)

                yield "XDONE"
                def hsl(h):
                    return slice((h % 2) * 64, (h % 2) * 64 + 64), h // 2

                def mmat(lname, rname, mask, ring=None):
                    lx, Tlx = XT[lname]
                    rx, Trx = XT[rname]
                    M_, TM_ = (ring or mr).next()
                    for par in range(2):
                        pb, Tp = pring.next()
                        for hh in range(4):
                            h = hh * 2 + par
                            ps_, p_ = hsl(h)
                            S.mm(pb[:, hh * 128:(hh + 1) * 128], lx[ps_, p_, :], rx[ps_, p_, :], True, True, r=[Tlx, Trx], w=[Tp])
                        S.tt("dve", M_[:, par:8:2, :], pb[:].rearrange("p (h t) -> p h t", h=4), bc4(cst[mask]), ALU.mult, r=[Tp, Tw], w=[TM_])
                    return M_, TM_

                Mab, TMab = mmat("b", "a", "m_su")
                MabT, TMabT = mmat("a", "b", "m_sl")
                Mak, TMak = mmat("k", "a", "m_su", keepr)
                if own:
                    Mbr, TMbr = mmat("b", "r", "m_iu", keepr)
                    Mkr, TMkr = mmat("k", "r", "m_iu", keepr)

                def hmat(L_, TL_, R_, TR_, add=None):
                    O_, TO_ = mr.next()
                    for half in range(2):
                        pb, Tp = pring.next()
                        for hh in range(4):
                            h = half * 4 + hh
                            S.mm(pb[:, hh * 128:(hh + 1) * 128], L_[:, h, :], R_[:, h, :], True, True, r=[TL_, TR_], w=[Tp])
                        if add is None:
                            ek[0] += 1
                            S.cp(evac_eng(ek[0]), O_[:, half * 4:(half + 1) * 4, :], pb[:].rearrange("p (h t) -> p h t", h=4), r=[Tp], w=[TO_])
                        else:
                            A_, TA_ = add
                            S.tt("dve", O_[:, half * 4:(half + 1) * 4, :], pb[:].rearrange("p (h t) -> p h t", h=4),
                                 A_[:, half * 4:(half + 1) * 4, :], ALU.add, r=[Tp, TA_], w=[TO_])
                    return O_, TO_

                N_, TN_ = Mab, TMab
                NT_, TNT_ = MabT, TMabT
                P_, TP_ = mr.next()
                S.tt("dve", P_[:], N_[:], bc8(identb), ALU.add, r=[TN_, Tw], w=[TP_])
                for lvl in range(5):
                    NT2, TNT2 = hmat(N_, TN_, NT_, TNT_)
                    if lvl < 4:
                        N2, TN2 = hmat(NT_, TNT_, N_, TN_)
                    P_, TP_ = hmat(NT2, TNT2, P_, TP_, add=(P_, TP_))
                    NT_, TNT_ = NT2, TNT2
                    if lvl < 4:
                        N_, TN_ = N2, TN2
                    yield "L"
                Tm, TTm = P_, TP_

                def tokmat(L_, TL_, R_, TR_):
                    pb, Tp = pring.next()
                    for h in range(8):
                        S.mm(pb[:, h * 64:(h + 1) * 64], L_[:, h, :], R_[:, h * 64:(h + 1) * 64], True, True, r=[TL_, TR_], w=[Tp])
                    return pb, Tp

                pb, Tp = tokmat(Tm, TTm, AT, TAT)
                AH, TAH = b16r.next()
                S.cp("act", AH[:], pb[:], r=[Tp], w=[TAH])
                pb, Tp = tokmat(Mak, TMak, Vb, TVb)
                Wm_, TWm_ = b16r.next()
                S.cp("dve", Wm_[:], pb[:], r=[Tp], w=[TWm_])
                pb, Tp = tokmat(Tm, TTm, Wm_, TWm_)
                U0, TU0 = b16r.next()
                S.cp("act", U0[:], pb[:], r=[Tp], w=[TU0])
                if own:
                    pb, Tp = pring.next()
                    for h in range(8):
                        ps_, p_ = hsl(h)
                        S.mm(pb[ps_, p_ * 128:(p_ + 1) * 128], AH[:, h * 64:(h + 1) * 64], Mbr[:, h, :], True, True, r=[TAH, TMbr], w=[Tp])
                    RH, TRH = RHr.next()
                    S.tt("dve", RH[:], pb[:].rearrange("p (q t) -> p q t", q=4), XT["r"][0][:], ALU.add, r=[Tp, XT["r"][1]], w=[TRH])
                    pb, Tp = pring.next()
                    for h in range(8):
                        S.mm(pb[:, h * 64:(h + 1) * 64], Mbr[:, h, :], U0[:, h * 64:(h + 1) * 64], True, False, r=[TMbr, TU0], w=[Tp])
                        S.mm(pb[:, h * 64:(h + 1) * 64], Mkr[:, h, :], Vb[:, h * 64:(h + 1) * 64], False, True, r=[TMkr, TVb], w=[Tp])
                    Y0, TY0 = f32r.next()
                    S.cp("act", Y0[:], pb[:], r=[Tp], w=[TY0])
                GT, TGT = GTr.next()
                for c_ in range(2):
                    pb, Tp = pring.next()
                    for p_ in range(4):
                        S.mm(pb[:, p_ * 128:(p_ + 1) * 128], AH[c_ * 64:(c_ + 1) * 64, p_ * 128:(p_ + 1) * 128],
                             BG[c_ * 64:(c_ + 1) * 64, p_ * 128:(p_ + 1) * 128], True, True, r=[TAH, TBG], w=[Tp])
                    S.tt("dve", GT[:, c_:8:2, :], pb[:].rearrange("p (q t) -> p q t", q=4), bc4(cst["bdmask"]), ALU.mult, r=[Tp, Tw], w=[TGT])
                for pc_ in range(8):
                    S.stt(GT[:, pc_, :], ident[:], gam[:, pc_:pc_ + 1], GT[:, pc_, :], ALU.mult, ALU.add, r=[TGT, Tgam, C.Tconst], w=[TGT])
                Hs, THs = Hsr.next()
                for c_ in range(2):
                    pb, Tp = pring.next()
                    for p_ in range(4):
                        rs_ = slice(c_ * 64, (c_ + 1) * 64)
                        cs_ = slice(p_ * 128, (p_ + 1) * 128)
                        S.mm(pb[:, p_ * 128:(p_ + 1) * 128], BG[rs_, cs_], U0[rs_, cs_], True, False, r=[TBG, TU0], w=[Tp])
                        S.mm(pb[:, p_ * 128:(p_ + 1) * 128], KG[rs_, cs_], Vb[rs_, cs_], False, True, r=[TKG, TVb], w=[Tp])
                    pv = pb[:].rearrange("p (q t) -> p q t", q=4)
                    S.cp("dve", Hs[0:64, c_:8:2, :], pv[0:64, :, 0:64], r=[Tp], w=[THs])
                    S.cp("act", Hs[64:128, c_:8:2, :], pv[64:128, :, 64:128], r=[Tp], w=[THs])
                if own:
                    y_ps0, Typ0 = pring.next()
                    y_ps1, Typ1 = pring.next()
                ST, TST = stv
                for c_ in range(2):
                    if own:
                        for h in range(8):
                            ps_, p_ = hsl(h)
                            ypb, Typb = (y_ps0, Typ0) if h % 2 == 0 else (y_ps1, Typ1)
                            S.mm(ypb[c_ * 64:(c_ + 1) * 64, p_ * 64:(p_ + 1) * 64], RH[ps_, p_, c_ * 64:(c_ + 1) * 64],
                                 ST[ps_, p_, :], True, True, r=[TRH, TST], w=[Typb])
                    s_ps, Tsp = pring.next()
                    for p_ in range(4):
                        S.mm(s_ps[:, p_ * 64:(p_ + 1) * 64], GT[:, p_ * 2 + c_, :], ST[:, p_, :], True, True, r=[TGT, TST], w=[Tsp])
                    STn, TSTn = STr.next()
                    Hv = Hs[:].rearrange("p (q c) i -> p q c i", c=2)
                    S.tt("dve", STn[:], s_ps[:, 0:256].rearrange("p (q i) -> p q i", q=4), Hv[:, :, c_, :], ALU.add, r=[Tsp, THs], w=[TSTn])
                    ST, TST = STn, TSTn
                    stv[0], stv[1] = ST, TST
                if own:
                    y, Ty_ = f32r.next()
                    y3 = y[:].rearrange("p (h j) -> p h j", h=8)
                    Y03 = Y0[:].rearrange("p (h j) -> p h j", h=8)
                    S.tt("dve", y3[:, 0:8:2, :], y_ps0[:, 0:256].rearrange("p (q j) -> p q j", q=4), Y03[:, 0:8:2, :], ALU.add, r=[Typ0, TY0], w=[Ty_])
                    S.tt("dve", y3[:, 1:8:2, :], y_ps1[:, 0:256].rearrange("p (q j) -> p q j", q=4), Y03[:, 1:8:2, :], ALU.add, r=[Typ1, TY0], w=[Ty_])
                    st1, Tst1 = smr.next()
                    S.op("dve", lambda e, o=st1, i=y3: e.tensor_reduce(out=o[:, 0:8], in_=i, axis=AX.X, op=ALU.add), r=[Ty_], w=[Tst1])
                    sq, Tsq = f32r.next()
                    S.tt("pool", sq[:], y[:], y[:], ALU.mult, r=[Ty_], w=[Tsq])
                    S.op("dve", lambda e, o=st1, i=sq: e.tensor_reduce(out=o[:, 8:16], in_=i[:].rearrange("p (h j) -> p h j", h=8), axis=AX.X, op=ALU.add), r=[Tsq], w=[Tst1])
                    S.ts("dve", st1[:, 0:8], st1[:, 0:8], 1.0 / 64, None, ALU.mult, None, r=[Tst1], w=[Tst1])
                    st2, Tst2 = smr.next()
                    S.tt("dve", st2[:, 0:8], st1[:, 0:8], st1[:, 0:8], ALU.mult, r=[Tst1], w=[Tst2])
                    S.stt(st2[:, 0:8], st1[:, 8:16], 1.0 / 64, st2[:, 0:8], ALU.mult, ALU.subtract, r=[Tst1, Tst2], w=[Tst2])
                    S.act(st2[:, 0:8], st2[:, 0:8], AF.Sqrt, r=[Tst2], w=[Tst2], bias=64e-5, scale=1.0)
                    S.op("dve", lambda e, o=st2: e.reciprocal(out=o[:, 0:8], in_=o[:, 0:8]), r=[Tst2], w=[Tst2])
                    S.tt("dve", y3, y3, st1[:, 0:8].unsqueeze(2).to_broadcast([128, 8, 64]), ALU.subtract, r=[Ty_, Tst1], w=[Ty_])
                    S.tt("dve", y3, y3, st2[:, 0:8].unsqueeze(2).to_broadcast([128, 8, 64]), ALU.mult, r=[Ty_, Tst2], w=[Ty_])
                    S.tt("pool", y[:], y[:], rows["rwkv_gn_g"][:], ALU.mult, r=[Ty_, Tw], w=[Ty_])
                    S.tt("pool", y[:], y[:], rows["rwkv_gn_b"][:], ALU.add, r=[Ty_, Tw], w=[Ty_])
                    S.tt("dve", sq[:].rearrange("p (h j) -> p h j", h=8), Vb[:].rearrange("p (h j) -> p h j", h=8),
                         sm[:, 8:16].unsqueeze(2).to_broadcast([128, 8, 64]), ALU.mult, r=[TVb, Tsm], w=[Tsq])
                    S.tt("pool", y[:], y[:], sq[:], ALU.add, r=[Ty_, Tsq], w=[Ty_])
                    S.tt("dve", y[:], y[:], g_sb[:], ALU.mult, r=[Ty_, Tgs], w=[Ty_])
                    S.dma("sp", d["YR"][(tile - OWN_T) * 128:(tile - OWN_T + 1) * 128, :], y[:], r=[Ty_], w=[Tok()])

        tiles_ = [(g, s) for g in range(SL // 256) for s in range(2)]
        gens = [tile_gen(g, s) for (g, s) in tiles_]

        def run_x(gen):
            for v in gen:
                if v == "XDONE":
                    return

        run_x(gens[0])
        for i in range(len(gens)):
            a = gens[i]
            b = gens[i + 1] if i + 1 < len(gens) else None
            a_done = False
            b_done = b is None
            while not (a_done and b_done):
                if not a_done:
                    try:
                        next(a)
                    except StopIteration:
                        a_done = True
                if not b_done:
                    if next(b) == "XDONE":
                        b_done = True
        S.barrier()


SCALE = 0.125
BIG8 = -240000.0


def nsa_phase(S, C, d):
    nc = C.nc
    SL = C.SL
    NT = SL // 128
    QB0 = NT // 2
    NCMP = SL // 16 - 1
    with ExitStack() as st:
        sb = lambda n, shape, dt: st.enter_context(nc.sbuf_tensor("n_" + n, shape, dt))
        pring = Ring([st.enter_context(nc.psum_tensor("n_ps%d" % i, [128, 512], F32)) for i in range(5)], "nps")
        acc_ps = [st.enter_context(nc.psum_tensor("n_acc%d" % i, [128, 512], F32)) for i in range(3)]
        Tacc = toks(3, "nacc")
        Tc = Tok("nsac")
        identb = sb("identb", [128, 128], BF16)
        S.cp("dve", identb[:], C.ident[:], r=[C.Tconst], w=[Tc])
        ident = C.ident
        jmb = sb("jmb", [128, 128], BF16)
        load_cast(S, "pool", jmb[:], d["jmat"][:, :], w=[Tc])
        RB = sb("RB", [33, 8], F32)
        S.dma("sp", RB[0:32, :], d["rel_bias"][:, :], w=[Tc])
        S.op("act", lambda e: e.mul(out=RB[0:32, :], in_=RB[0:32, :], mul=8.0), r=[Tc], w=[Tc])
        S.memset("dve", RB[32:33, :], BIG8, w=[Tc])
        OHF = sb("OHF", [33, 768], F32)
        S.dma("sp", OHF[:], d["ohf"][:, :], w=[Tc])
        fb_sb = sb("fb", [8, 768], F32)
        for hf in range(2):
            pb, Tp = pring.next()
            S.mm(pb[0:8, 0:384], RB[:, :], OHF[:, hf * 384:(hf + 1) * 384], True, True, r=[Tc], w=[Tp])
            S.cp("dve", fb_sb[:, hf * 384:(hf + 1) * 384], pb[0:8, 0:384], r=[Tp], w=[Tc])
        Tfb = Tok("fbd")
        S.dma("sp", d["FB"][:, :], fb_sb[:], r=[Tc], w=[Tfb])
        FBt = d["FB"].tensor
        Btab = sb("Btab", [128, 2, 3, 512], BF16)
        PatC = sb("PatC", [128, 2, 4, 24], BF16)
        bstgs = [sb("bstg%d" % i, [128, 128], F32) for i in range(4)]
        Tbss = toks(4, "bstg")
        bsi = [0]

        def next_bstg():
            bsi[0] += 1
            return bstgs[bsi[0] % 4], Tbss[bsi[0] % 4]
        with nc.allow_non_contiguous_dma(reason="one-time small bias table build"):
            for g in range(2):
                for di, dl in enumerate((0, 1, 4)):
                    for h in range(4):
                        off = (g * 4 + h) * 768 + dl * 128
                        src = bass.AP(FBt, off, [[1, 128], [1, 128]])
                        bstg, Tbs = next_bstg()
                        S.dma("sp", bstg[:], src, r=[Tfb], w=[Tbs])
                        S.cp("dve", Btab[:, g, di, h * 128:(h + 1) * 128], bstg[:], r=[Tbs], w=[Tc])
                for h in range(4):
                    off = (g * 4 + h) * 768 + 96 + 256
                    src = bass.AP(FBt, off, [[1, 128], [-16, 23]])
                    bstg, Tbs = next_bstg()
                    S.dma("sp", bstg[:, 0:23], src, r=[Tfb], w=[Tbs])
                    S.cp("dve", PatC[:, g, h, 0:23], bstg[:, 0:23], r=[Tbs], w=[Tc])
        keyb = sb("keyb", [128, NT], F32)
        S.dma("sp", keyb[:], d["keyb"][:, :], w=[Tc])
        cvrow = sb("cvrow", [1, 512], BF16)
        load_cast(S, "pool", cvrow[:], d["cvrow"][:, :], w=[Tc])
        onesr = sb("onesr", [1, 128], BF16)
        S.memset("dve", onesr[:], 1.0, w=[Tc])
        D0 = sb("D0", [128, 128], F32)
        S.dma("sp", D0[:], d["d0"][:, :], w=[Tc])
        firstr = sb("firstr", [128, 128], F32)
        S.dma("sp", firstr[:], d["firstrow"][:, :], w=[Tc])
        KE = sb("KE", [128, SL], BF16)
        KW = sb("KW", [128, SL], BF16)
        for c0 in range(0, SL, 2048):
            c1 = min(SL, c0 + 2048)
            load_cast(S, "pool", KE[64:128, c0:c1], d["emat2"][:, c0:c1], w=[Tc])
        S.memset("pool", KW[64:128, :], 0.0, w=[Tc])
        oh0 = sb("oh0", [128, 128], BF16)
        S.memset("dve", oh0[:], 0.0, w=[Tc])
        S.memset("dve", oh0[0:1, :], 1.0, w=[Tc])
        cv128 = sb("cv128", [128, 512], BF16)
        S.memset("dve", cv128[:], 0.0, w=[Tc])
        S.cp("dve", cv128[0:1, :], cvrow[:], r=[Tc], w=[Tc])
        w1 = [sb("w1%d" % i, [64, 32, 128], BF16) for i in range(2)]
        w2 = [sb("w2%d" % i, [128, 64], BF16) for i in range(2)]
        peT = [sb("peT%d" % i, [64, 32], BF16) for i in range(2)]
        with nc.allow_non_contiguous_dma(reason="small transposed pe load"):
            for i, (a, b, c) in enumerate((("nsa_w1_k", "nsa_w2_k", "nsa_pe_k"), ("nsa_w1_v", "nsa_w2_v", "nsa_pe_v"))):
                for l0 in range(0, 32, 8):
                    load_cast(S, "pool", w1[i][:, l0:l0 + 8, :], d[a][l0 * 64:(l0 + 8) * 64, :].rearrange("(l d) h -> d l h", d=64), w=[Tc])
                load_cast(S, "pool", w2[i][:, :], d[b][:, :], w=[Tc])
                load_cast(S, "pool", peT[i][:, :], d[c].rearrange("l d -> d l"), w=[Tc])
        cT = sb("cT", [64, SL], BF16)
        vs = sb("vs", [128, NT, 65], BF16)
        vw = sb("vw", [128, NT, 65], BF16)
        KcT = sb("KcT", [128, 512], BF16)
        S.memset("dve", KcT[64:128, :], 0.0, w=[Tc])
        vc = sb("vc", [128, 4, 65], BF16)
        hidb = sb("hidb", [128, 512], BF16)
        Tkv = Tok("kv")
        f32r = Ring([sb("f%d" % i, [128, 512], F32) for i in range(10)], "nf32")
        b16r = Ring([sb("b%d" % i, [128, 512], BF16) for i in range(8)], "nb16")
        pcTr = Ring([sb("pcT%d" % i, [128, 4, 128], BF16) for i in range(4)], "pcT")
        pcbr = Ring([sb("pcb%d" % i, [128, 512], BF16) for i in range(8)], "pcb")
        smr = Ring([sb("s%d" % i, [128, 16], F32) for i in range(12)], "nsm")
        sqr = Ring([sb("q%d" % i, [128, 128], F32) for i in range(10)], "nsq")
        QNs = [sb("QN%d" % i, [128, 2, 512], BF16) for i in range(4)]
        for q_ in QNs:
            S.memset("pool", q_[:], 0.0, w=[Tc])
        nsb = Ring([sb("nsb%d" % i, [128, 2, 128], BF16) for i in range(2)], "nsb")
        TqTs = toks(4, "qT")
        TnsTs = toks(4, "nsT")
        gtr = Ring([sb("gt%d" % i, [128, 24], F32) for i in range(3)], "gt")
        yor = Ring([sb("yo%d" % i, [128, 256], F32) for i in range(2)], "yo")
        otr = Ring([sb("ot%d" % i, [128, 4, 65], F32) for i in range(6)], "ot")

        for g in range(2):
            S.dma("sp", KE[0:64, :], d["KT"][4 + g, :, :], w=[Tkv])
            S.dma("sp", KW[0:64, :], d["KT"][6 + g, :, :], w=[Tkv])
            S.memset("dve", vs[:, :, 64:65], 1.0, w=[Tkv])
            S.memset("dve", vw[:, :, 64:65], 1.0, w=[Tkv])
            S.memset("dve", vc[:, :, 64:65], 1.0, w=[Tkv])
            for t0 in range(0, NT, 8):
                t1 = min(NT, t0 + 8)
                S.dma("sp", vs[:, t0:t1, 0:64], d["VSW"][t0 * 128:t1 * 128, g * 64:(g + 1) * 64].rearrange("(t p) c -> p t c", p=128), w=[Tkv])
                S.dma("sp", vw[:, t0:t1, 0:64], d["VSW"][t0 * 128:t1 * 128, 128 + g * 64:128 + (g + 1) * 64].rearrange("(t p) c -> p t c", p=128), w=[Tkv])
            for i in range(2):
                S.dma("sp", cT[:], d["KT"][i * 2 + g, :, :], w=[Tkv])
                hp, Thp = pring.next()
                src_ap = cT[:]
                for l in range(32):
                    rhs = bass.AP(cT[:].tensor, cT[:, l:l + 1].offset, [list(cT[:].ap[0]), [16, NCMP]])
                    S.mm(hp[:, 0:NCMP], w1[i][:, l, :], rhs, l == 0, l == 31, r=[Tc, Tkv], w=[Thp])
                cp_, Tcp_ = pring.next()
                for l in range(32):
                    S.mm(cp_[:, 0:1], w1[i][:, l, :], peT[i][:, l:l + 1], l == 0, l == 31, r=[Tc], w=[Tcp_])
                cpe, Tcpe = smr.next()
                S.cp("dve", cpe[:, 0:1], cp_[:, 0:1], r=[Tcp_], w=[Tcpe])
                u, Tu = f32r.next()
                S.act(u[:, 0:NCMP], hp[:, 0:NCMP], AF.Identity, r=[Thp, Tcpe], w=[Tu], bias=cpe[:, 0:1], scale=1.0)
                t, Tt = f32r.next()
                S.tt("dve", t[:, 0:NCMP], u[:, 0:NCMP], u[:, 0:NCMP], ALU.mult, r=[Tu], w=[Tt])
                S.ts("dve", t[:, 0:NCMP], t[:, 0:NCMP], 0.044715, 1.0, ALU.mult, ALU.add, r=[Tt], w=[Tt])
                S.tt("dve", t[:, 0:NCMP], t[:, 0:NCMP], u[:, 0:NCMP], ALU.mult, r=[Tt, Tu], w=[Tt])
                S.act(t[:, 0:NCMP], t[:, 0:NCMP], AF.Sigmoid, r=[Tt], w=[Tt], scale=1.5957691216057308)
                S.memset("pool", hidb[:], 0.0, w=[Tkv])
                S.tt("dve", hidb[:, 0:NCMP], t[:, 0:NCMP], u[:, 0:NCMP], ALU.mult, r=[Tt, Tu], w=[Tkv])
                if i == 0:
                    kp, Tkp = pring.next()
                    S.mm(kp[0:64, :], w2[0][:, :], hidb[:, :], True, True, r=[Tc, Tkv], w=[Tkp])
                    S.cp("act", KcT[0:64, :], kp[0:64, :], r=[Tkp], w=[Tkv])
                else:
                    kp, Tkp = pring.next()
                    for c in range(4):
                        S.mm(kp[:, c * 64:(c + 1) * 64], hidb[:, c * 128:(c + 1) * 128], w2[1][:, :], True, True, r=[Tc, Tkv], w=[Tkp])
                    S.cp("act", vc[:, :, 0:64], kp[:, 0:256].rearrange("p (c d) -> p c d", c=4), r=[Tkp], w=[Tkv])
            for t_, Tt_ in zip(f32r.t, f32r.T):
                S.memset("dve", t_[:], 0.0, w=[Tt_])

            def block_gen(qb):
                qo = qb - QB0
                QN = QNs[qb % 4]
                TqT = TqTs[qb % 4]
                TnsT = TnsTs[qb % 4]
                for lh in range(2):
                    S.dma("sp", QN[0:64, lh, :].rearrange("p (h q) -> p h q", h=4),
                          d["QT"][g * 4:(g + 1) * 4, :, qo * 128:(qo + 1) * 128].rearrange("h p q -> p h q"), w=[TqT])
                gt_, Tgt = gtr.next()
                S.dma("sp", gt_[:], d["GATE"][qo * 128:(qo + 1) * 128, :], w=[Tgt])
                nvis = min(8 * qb + 7, NCMP)
                c0 = max(0, 8 * qb - 16)
                c1 = nvis
                oc_ps, Toc = acc_ps[0], Tacc[0]
                zc, Tzc = smr.next()
                es = []
                for h in range(4):
                    lp, Tlp = pring.next()
                    S.mm(lp[:, 0:nvis], QN[:, 0, h * 128:(h + 1) * 128], KcT[:, 0:nvis], True, False, r=[TqT, Tkv], w=[Tlp])
                    S.mm(lp[:, c0:c1], identb[:], PatC[:, g, h, c0 - (8 * qb - 16):c1 - (8 * qb - 16)], False, False, r=[Tc], w=[Tlp])
                    S.mm(lp[:, 0:nvis], oh0[:], cv128[:, 0:nvis], False, True, r=[Tc], w=[Tlp])
                    e, Te = f32r.next()
                    S.act(e[:, 0:nvis], lp[:, 0:nvis], AF.Exp, r=[Tlp], w=[Te, Tzc], scale=SCALE, accum_out=zc[:, h:h + 1])
                    es.append((e, Te))
                pcbs = []
                for h in range(4):
                    e, Te = es[h]
                    pcb, Tpcb = pcbr.next()
                    S.cp("dve", pcb[:], e[:], r=[Te], w=[Tpcb])
                    pcbs.append((pcb, Tpcb))

                def attn_branch(kts, kT_, v_, acc, Tac, bias_fn, extra_fn):
                    LOOK = 2
                    pend = []

                    def logits(kt):
                        lp, Tlp = pring.next()
                        dl = qb - kt
                        bt = bias_fn(dl)
                        if extra_fn is not None:
                            S.mm(lp[:], kT_[:, kt * 128:(kt + 1) * 128], QN[:, kt // 32, :], True, bt is None, r=[Tkv, TqT, TnsT], w=[Tlp])
                        else:
                            S.mm(lp[:], kT_[:, kt * 128:(kt + 1) * 128], QN[:, 0, :], True, bt is None, r=[Tkv, TqT], w=[Tlp])
                        if bt is not None:
                            S.mm(lp[:], jmb[:], bt, False, True, r=[Tc], w=[Tlp])
                        pT, TpT = b16r.next()
                        S.act(pT[:], lp[:], AF.Exp, r=[Tlp, Tc], w=[TpT], scale=SCALE, bias=keyb[:, kt:kt + 1])
                        pend.append((kt, pT, TpT))

                    def pv(first, last):
                        kt, pT, TpT = pend.pop(0)
                        for h in range(4):
                            S.mm(acc[:, h * 65:(h + 1) * 65], pT[:, h * 128:(h + 1) * 128], v_[:, kt, :], first and h == 0, last and h == 3,
                                 r=[TpT, Tkv], w=[Tac])

                    n = len(kts)
                    for i in range(min(LOOK, n)):
                        logits(kts[i])
                    for i in range(n):
                        if i + LOOK < n:
                            logits(kts[i + LOOK])
                        pv(i == 0, i == n - 1)

                ow_ps, Tow = acc_ps[2], Tacc[2]
                kt0 = max(0, qb - 4)
                attn_branch(list(range(kt0, qb + 1)), KW, vw, ow_ps, Tow,
                            lambda dl: Btab[:, g, {0: 0, 1: 1, 4: 2}[dl], :] if dl in (0, 1, 4) else None, None)
                pcsum, Tpcs = f32r.next()
                for h in range(4):
                    e, Te = es[h]
                    S.ts("dve", zc[:, h:h + 1], zc[:, h:h + 1], 1e-30, None, ALU.max, None, r=[Tzc], w=[Tzc])
                    S.op("dve", lambda e_, o=zc, hh=h: e_.reciprocal(out=o[:, 4 + hh:5 + hh], in_=o[:, hh:hh + 1]), r=[Tzc], w=[Tzc])
                    if h == 0:
                        S.ts("dve", pcsum[:], e[:], zc[:, 4:5], None, ALU.mult, None, r=[Te, Tzc], w=[Tpcs])
                    else:
                        S.stt(pcsum[:], e[:], zc[:, 4 + h:5 + h], pcsum[:], ALU.mult, ALU.add, r=[Te, Tzc, Tpcs], w=[Tpcs])
                imp, Timp = sqr.next()
                pc3 = pcsum[:].rearrange("p (j r) -> p j r", r=4)
                S.op("dve", lambda e_, o=imp, i=pc3: e_.tensor_reduce(out=o[:], in_=i, axis=AX.X, op=ALU.add), r=[Tpcs], w=[Timp])
                S.tt("dve", imp[:, 1:128], imp[:, 1:128], pc3[:, 0:127, 3], ALU.add, r=[Tpcs, Timp], w=[Timp])
                val, Tval = sqr.next()
                S.ts("dve", val[:], D0[:], float(2 * qb), None, ALU.is_le, None, r=[Tc], w=[Tval])
                fc, Tfc = sqr.next()
                S.ts("dve", fc[:], D0[:], float(2 * qb - 1), None, ALU.is_ge, None, r=[Tc], w=[Tfc])
                S.tt("dve", fc[:], fc[:], val[:], ALU.mult, r=[Tfc, Tval], w=[Tfc])
                S.tt("dve", fc[:], fc[:], firstr[:], ALU.add, r=[Tfc, Tc], w=[Tfc])
                sc, Tsc = sqr.next()
                S.stt(sc[:], fc[:], 1e4, imp[:], ALU.mult, ALU.max, r=[Tfc, Timp], w=[Tsc])
                S.stt(sc[:], sc[:], 1.0, val[:], ALU.add, ALU.mult, r=[Tsc, Tval], w=[Tsc])
                S.ts("dve", sc[:], sc[:], -1.0, None, ALU.add, None, r=[Tsc], w=[Tsc])
                m8, Tm8 = smr.next()
                S.op("dve", lambda e_, o=m8, i=sc: e_.max(out=o[:, 0:8], in_=i[:]), r=[Tsc], w=[Tm8])
                sc2, Tsc2 = sqr.next()
                S.op("dve", lambda e_, o=sc2, a=m8, i=sc: e_.match_replace(out=o[:], in_to_replace=a[:, 0:8], in_values=i[:], imm_value=-2.0), r=[Tsc, Tm8], w=[Tsc2])
                S.op("dve", lambda e_, o=m8, i=sc2: e_.max(out=o[:, 8:16], in_=i[:]), r=[Tsc2], w=[Tm8])
                nsl, Tnsl = nsb.next()
                S.ts("dve", sc2[:], sc[:], m8[:, 15:16], None, ALU.is_ge, None, r=[Tsc, Tm8], w=[Tsc2])
                S.ts("dve", nsl[:, 0, :], sc2[:], -1.0, -BIG8, ALU.add, ALU.mult, r=[Tsc2], w=[Tnsl])
                S.cp("dve", nsl[:, 1, 0:64], nsl[:, 0, 64:128], r=[Tnsl], w=[Tnsl])
                S.cp("dve", nsl[:, 1, 64:128], nsl[:, 0, 0:64], r=[Tnsl], w=[Tnsl])
                yield "A"
                tm_, Ttm = pring.next()
                tmb = tm_[:].bitcast(BF16)
                S.tr(tmb[:, 0:128], nsl[:, 0, :], identb[:], r=[Tnsl, Tc], w=[Ttm])
                S.tr(tmb[:, 128:256], nsl[:, 1, :], identb[:], r=[Tnsl, Tc], w=[Ttm])
                S.cp("dve", QN[64:128, 1, :].rearrange("p (h q) -> p h q", h=4), tmb[64:128, 0:128].unsqueeze(1).to_broadcast([64, 4, 128]), r=[Ttm], w=[TnsT])
                S.cp("dve", QN[64:128, 0, :].rearrange("p (h q) -> p h q", h=4), tmb[64:128, 128:256].unsqueeze(1).to_broadcast([64, 4, 128]), r=[Ttm], w=[TnsT])
                tps_ = []
                for h in range(4):
                    pcb, Tpcb = pcbs[h]
                    tp_, Ttp = pring.next()
                    tpb = tp_[:].bitcast(BF16)
                    for c in range(4):
                        S.tr(tpb[:, c * 128:(c + 1) * 128], pcb[:, c * 128:(c + 1) * 128], identb[:], r=[Tpcb, Tc], w=[Ttp])
                    tps_.append((tpb, Ttp))
                pcTs = []
                for h in range(4):
                    tpb, Ttp = tps_[h]
                    pcT, TpcT = pcTr.next()
                    S.cp("dve", pcT[:], tpb[:, 0:512].rearrange("p (c q) -> p c q", c=4), r=[Ttp], w=[TpcT])
                    pcTs.append((pcT, TpcT))
                for h in range(4):
                    pcT, TpcT = pcTs[h]
                    for c in range(4):
                        S.mm(oc_ps[:, h * 65:(h + 1) * 65], pcT[:, c, :], vc[:, c, :], c == 0, c == 3, r=[TpcT, Tkv], w=[Toc])
                ow_sb, Tows = otr.next()
                S.cp("dve", ow_sb[:], ow_ps[:, 0:260].rearrange("p (h c) -> p h c", h=4), r=[Tow], w=[Tows])
                oc_sb, Tocs = otr.next()
                S.cp("dve", oc_sb[:], oc_ps[:, 0:260].rearrange("p (h c) -> p h c", h=4), r=[Toc], w=[Tocs])
                yield "B"
                os_ps, Tos = acc_ps[1], Tacc[1]

                attn_branch(list(range(qb + 1)), KE, vs, os_ps, Tos,
                            lambda dl: Btab[:, g, dl, :] if dl <= 1 else None, True)
                os_sb, Toss = otr.next()
                S.cp("act", os_sb[:], os_ps[:, 0:260].rearrange("p (h c) -> p h c", h=4), r=[Tos], w=[Toss])
                yield "S"
                yo, Tyo = yor.next()
                yo3 = yo[:].rearrange("p (h c) -> p h c", h=4)
                cf, Tcf = smr.next()
                g3 = gt_[:].rearrange("p (h b) -> p h b", b=3)
                for bi, (osb, Tosb) in enumerate(((oc_sb, Tocs), (os_sb, Toss), (ow_sb, Tows))):
                    S.ts("dve", cf[:, bi * 4:(bi + 1) * 4], osb[:, :, 64], 1e-30, None, ALU.max, None, r=[Tosb], w=[Tcf])
                    S.op("dve", lambda e_, o=cf, b_=bi: e_.reciprocal(out=o[:, b_ * 4:(b_ + 1) * 4], in_=o[:, b_ * 4:(b_ + 1) * 4]), r=[Tcf], w=[Tcf])
                    S.tt("dve", cf[:, bi * 4:(bi + 1) * 4], cf[:, bi * 4:(bi + 1) * 4], g3[:, g * 4:(g + 1) * 4, bi], ALU.mult, r=[Tcf, Tgt], w=[Tcf])
                    cfb = cf[:, bi * 4:(bi + 1) * 4].unsqueeze(2).to_broadcast([128, 4, 64])
                    if bi == 0:
                        S.tt("dve", yo3, osb[:, :, 0:64], cfb, ALU.mult, r=[Tosb, Tcf], w=[Tyo])
                    else:
                        S.tt("dve", osb[:, :, 0:64], osb[:, :, 0:64], cfb, ALU.mult, r=[Tosb, Tcf], w=[Tosb])
                        S.tt("dve", yo3, yo3, osb[:, :, 0:64], ALU.add, r=[Tosb, Tyo], w=[Tyo])
                S.dma("pool", d["YN"][qo * 128:(qo + 1) * 128, g * 256:(g + 1) * 256], yo[:], r=[Tyo], w=[Tok()])

            gens = [block_gen(qb) for qb in range(QB0, NT)]
            n_ = len(gens)
            next(gens[0])
            next(gens[0])
            for i in range(n_):
                if i + 1 < n_:
                    next(gens[i + 1])
                next(gens[i])
                if i + 1 < n_:
                    next(gens[i + 1])
                for _ in gens[i]:
                    pass
        S.barrier()


def post_phase(S, C, d):
    nc = C.nc
    SL = C.SL
    OWN = SL // 2
    XS = 1.0 / 16.0
    with ExitStack() as st:
        sb = lambda n, shape, dt: st.enter_context(nc.sbuf_tensor("p_" + n, shape, dt))
        pring = Ring([st.enter_context(nc.psum_tensor("p_ps%d" % i, [128, 512], F32)) for i in range(8)], "pps")
        Tw = Tok("pw")
        ident = C.ident
        wout = sb("wout", [128, 8, D], BF16)
        wq = sb("wq", [128, 8, D], BF16)
        wo = sb("wo", [128, 8, D], BF16)
        for dc in range(8):
            load_cast(S, "pool", wout[:, dc, :], d["mix_w_out"][dc * 128:(dc + 1) * 128, :], w=[Tw])
            load_cast(S, "pool", wq[:, dc, :], d["xattn_wq"][dc * 128:(dc + 1) * 128, :], w=[Tw])
            load_cast(S, "pool", wo[:, dc, :], d["xattn_wo"][dc * 128:(dc + 1) * 128, :], w=[Tw])
        lnp = {}
        for n in ["ln2_g", "ln2_b", "ln3_g", "ln3_b"]:
            lnp[n] = sb(n, [128, D], F32)
            S.dma("sp", lnp[n][:], d[n].partition_broadcast(128), w=[C.Tconst])
        KT = sb("KT", [128, 8, 256], BF16)
        V = sb("V", [128, 2, 4, 257], BF16)
        with ExitStack() as st2:
            sb2 = lambda n, shape, dt: st2.enter_context(nc.sbuf_tensor("p2_" + n, shape, dt))
            wk = sb2("wk", [128, 8, D], BF16)
            wv = sb2("wv", [128, 8, D], BF16)
            for dc in range(8):
                load_cast(S, "pool", wk[:, dc, :], d["xattn_wk"][dc * 128:(dc + 1) * 128, :], w=[Tw])
                load_cast(S, "pool", wv[:, dc, :], d["xattn_wv"][dc * 128:(dc + 1) * 128, :], w=[Tw])
            mt_ = sb2("mem", [128, 2, D], F32)
            Tm = Tok("mem")
            memT = sb2("memT", [128, 8, 256], BF16)
            TmT = Tok("memT")
            S.dma("sp", mt_[:], d["mem"].rearrange("(s p) d -> p s d", p=128), w=[Tm])
            for s in range(2):
                for hb in range(2):
                    pb, Tp = pring.next()
                    for j in range(4):
                        dc = hb * 4 + j
                        S.tr(pb[:, j * 128:(j + 1) * 128], mt_[:, s, dc * 128:(dc + 1) * 128], ident[:], r=[Tm, C.Tconst], w=[Tp])
                    S.cp("dve", memT[:, hb * 4:(hb + 1) * 4, s * 128:(s + 1) * 128], pb[:].rearrange("p (j t) -> p j t", j=4), r=[Tp], w=[TmT])
            for cc in range(0, 8, 2):
                pb, Tp = pring.next()
                for k2 in range(2):
                    for dc in range(8):
                        S.mm(pb[:, k2 * 256:(k2 + 1) * 256], wk[:, dc, (cc + k2) * 128:(cc + k2 + 1) * 128], memT[:, dc, :], dc == 0, dc == 7, r=[Tw, TmT], w=[Tp])
                S.cp("act", KT[:, cc:cc + 2, :], pb[:].rearrange("p (k t) -> p k t", k=2), r=[Tp], w=[Tw])
            S.memset("dve", V[:, :, :, 256:257], 1.0, w=[Tw])
            for mt in range(2):
                for nb in range(2):
                    pb, Tp = pring.next()
                    for dc in range(8):
                        S.mm(pb[:], memT[:, dc, mt * 128:(mt + 1) * 128], wv[:, dc, nb * 512:(nb + 1) * 512], dc == 0, dc == 7, r=[Tw, TmT], w=[Tp])
                    S.cp("act", V[:, mt, nb * 2:(nb + 1) * 2, 0:256], pb[:].rearrange("p (h c) -> p h c", h=2), r=[Tp], w=[Tw])
            S.barrier()
        sets = []
        for k_ in range(2):
            W_ = {}
            for n_ in ["ycat", "x1t", "x2t", "oat", "x3t"]:
                W_[n_] = sb("%s%d" % (n_, k_), [128, 2, D], F32)
                W_["T" + n_] = Tok(n_)
            for n_ in ["yT", "x2T", "oT", "QxT"]:
                W_[n_] = sb("%s%d" % (n_, k_), [128, 8, 256], BF16)
                W_["T" + n_] = Tok(n_)
            W_["rr"] = sb("rr%d" % k_, [128, D], F32)
            W_["Trr"] = Tok("rr")
            W_["lnt"] = {"st": sb("lnst%d" % k_, [128, 2, 6], F32), "mv": sb("lnmv%d" % k_, [128, 2], F32),
                         "rstd": sb("lnrs%d" % k_, [128, 1], F32), "T": Tok("lnt")}
            sets.append(W_)
        pTr = Ring([sb("pT%d" % i, [128, 2, 256], BF16) for i in range(4)], "ppT")
        zr = Ring([sb("z%d" % i, [128, 4], F32) for i in range(8)], "pz")

        def transp(src, Tsrc, dst, Tdst):
            k = 0
            for s in range(2):
                for hb in range(2):
                    pb, Tp = pring.next()
                    for j in range(4):
                        dc = hb * 4 + j
                        S.tr(pb[:, j * 128:(j + 1) * 128], src[:, s, dc * 128:(dc + 1) * 128], ident[:], r=[Tsrc, C.Tconst], w=[Tp])
                    k += 1
                    S.cp("act" if k % 2 else "dve", dst[:, hb * 4:(hb + 1) * 4, s * 128:(s + 1) * 128],
                         pb[:].rearrange("p (j t) -> p j t", j=4), r=[Tp], w=[Tdst])

        def proj_res_ln(W_, xT_, TxT_, w_, res, Tres, gname, bname, out_t, Tout):
            rr, Trr, lnt = W_["rr"], W_["Trr"], W_["lnt"]
            for s in range(2):
                for nb in range(2):
                    pb, Tp = pring.next()
                    for dc in range(8):
                        S.mm(pb[:], xT_[:, dc, s * 128:(s + 1) * 128], w_[:, dc, nb * 512:(nb + 1) * 512], dc == 0, dc == 7, r=[TxT_, Tw], w=[Tp])
                    S.stt(rr[:, nb * 512:(nb + 1) * 512], res[:, s, nb * 512:(nb + 1) * 512], ALPHA, pb[:], ALU.mult, ALU.add, r=[Tres, Tp], w=[Trr])
                layer_norm_tile(S, C, rr[:], out_t[:, s, :], lnp[gname], lnp[bname], None, Trr, Tout, lnt)
                yield "P"

        def group_gen(g):
            W_ = sets[g % 2]
            ycat, x1t, x2t, oat, x3t = W_["ycat"], W_["x1t"], W_["x2t"], W_["oat"], W_["x3t"]
            yT, x2T, oT, QxT = W_["yT"], W_["x2T"], W_["oT"], W_["QxT"]
            Tyc, Tx1, Tx2, Toa, Tx3 = W_["Tycat"], W_["Tx1t"], W_["Tx2t"], W_["Toat"], W_["Tx3t"]
            TyT, Tx2T, ToT, TQx = W_["TyT"], W_["Tx2T"], W_["ToT"], W_["TQxT"]
            r0 = g * 256
            S.dma("sp", ycat[:, :, 0:512], d["YR"][r0:r0 + 256, :].rearrange("(s p) c -> p s c", p=128), w=[Tyc])
            S.dma("sp", ycat[:, :, 512:1024], d["YN"][r0:r0 + 256, :].rearrange("(s p) c -> p s c", p=128), w=[Tyc])
            S.dma("sp", x1t[:], d["x1"][OWN + r0:OWN + r0 + 256, :].rearrange("(s p) c -> p s c", p=128), w=[Tx1])
            yield "L"
            transp(ycat, Tyc, yT, TyT)
            yield "T"
            yield from proj_res_ln(W_, yT, TyT, wout, x1t, Tx1, "ln2_g", "ln2_b", x2t, Tx2)
            transp(x2t, Tx2, x2T, Tx2T)
            yield "T"
            for cc in range(0, 8, 2):
                pb, Tp = pring.next()
                for k2 in range(2):
                    for dc in range(8):
                        S.mm(pb[:, k2 * 256:(k2 + 1) * 256], wq[:, dc, (cc + k2) * 128:(cc + k2 + 1) * 128], x2T[:, dc, :], dc == 0, dc == 7, r=[Tw, Tx2T], w=[Tp])
                S.cp("act", QxT[:, cc:cc + 2, :], pb[:].rearrange("p (k t) -> p k t", k=2), r=[Tp], w=[TQx])
            yield "Q"
            for hd in range(4):
                lp, Tlp = pring.next()
                for mt in range(2):
                    for cc in range(2):
                        S.mm(lp[:, mt * 256:(mt + 1) * 256], KT[:, hd * 2 + cc, mt * 128:(mt + 1) * 128], QxT[:, hd * 2 + cc, :], cc == 0, cc == 1, r=[Tw, TQx], w=[Tlp])
                pT, TpT = pTr.next()
                S.act(pT[:], lp[:].rearrange("p (m q) -> p m q", m=2), AF.Exp, r=[Tlp], w=[TpT], scale=XS)
                for s in range(2):
                    op_, Top = pring.next()
                    for mt in range(2):
                        S.mm(op_[:, 0:257], pT[:, mt, s * 128:(s + 1) * 128], V[:, mt, hd, :], mt == 0, mt == 1, r=[TpT, Tw], w=[Top])
                    z, Tz = zr.next()
                    S.op("dve", lambda e_, o=z, i=op_: e_.reciprocal(out=o[:, 0:1], in_=i[:, 256:257]), r=[Top], w=[Tz])
                    S.ts("dve", oat[:, s, hd * 256:(hd + 1) * 256], op_[:, 0:256], z[:, 0:1], None, ALU.mult, None, r=[Top, Tz], w=[Toa])
                yield "A"
            transp(oat, Toa, oT, ToT)
            yield "T"
            yield from proj_res_ln(W_, oT, ToT, wo, x2t, Tx2, "ln3_g", "ln3_b", x3t, Tx3)
            S.dma("sp", d["X3"][r0:r0 + 256, :].rearrange("(s p) c -> p s c", p=128), x3t[:], r=[Tx3], w=[Tok()])

        queue_ = [group_gen(g) for g in range(OWN // 256)]
        active_ = [queue_.pop(0)]
        for _ in range(6):
            next(active_[0])
        while queue_ or active_:
            if len(active_) < 2 and queue_:
                active_.append(queue_.pop(0))
            for gen_ in list(active_):
                try:
                    next(gen_)
                except StopIteration:
                    active_.remove(gen_)
        S.barrier()


def host_consts():
    t = np.arange(128)
    same = (t[:, None] // 64) == (t[None, :] // 64)
    c = {}
    c["ident"] = np.eye(128, dtype=np.float32)
    c["jmat"] = np.ascontiguousarray(np.eye(128, dtype=np.float32)[::-1])
    c["tri_incl"] = (same & (t[:, None] <= t[None, :])).astype(np.float32)
    c["tri_excl"] = (same & (t[:, None] < t[None, :])).astype(np.float32)
    c["tri_after"] = (same & (t[:, None] > t[None, :])).astype(np.float32)
    c["m_su"] = c["tri_excl"].copy()
    c["m_sl"] = c["tri_after"].copy()
    c["m_iu"] = c["tri_incl"].copy()
    c["bdmask"] = same.astype(np.float32)
    sel2 = np.zeros((128, 2), np.float32)
    sel2[63, 0] = 1.0
    sel2[127, 1] = 1.0
    c["sel2"] = sel2
    return c


def t5_bucket_np(dist):
    n = np.maximum(dist, 0)
    nf = np.maximum(n, 16).astype(np.float32)
    large = 16 + (np.log(nf / np.float32(16)) / np.float32(math.log(128 / 16)) * np.float32(16)).astype(np.int32)
    large = np.minimum(large, 31)
    return np.where(n < 16, n, large)


def host_consts2(SL, pad):
    c = {}
    NT = SL // 128
    dist = np.arange(768) - 127
    bk = t5_bucket_np(dist)
    ohf = np.zeros((33, 768), np.float32)
    ohf[bk, np.arange(768)] = 1.0
    ohf[31, :] -= 1.0
    ohf[32, :] = ((dist < 0) | (dist >= 512)).astype(np.float32)
    c["ohf"] = ohf
    tok = np.arange(SL)
    keyb = np.where(tok >= pad, 0.0, NEG).astype(np.float32).reshape(NT, 128).T
    c["keyb"] = np.ascontiguousarray(keyb)
    n = np.arange(512)
    c["cvrow"] = np.where((16 * n >= pad) & (n < SL // 16 - 1), 0.0, BIG8).astype(np.float32).reshape(1, 512)
    i = np.arange(128)
    c["d0"] = (np.arange(128)[None, :] - (i[:, None] >= 64)).astype(np.float32)
    fr = np.zeros((128, 128), np.float32)
    fr[:, pad // 64] = 1.0
    c["firstrow"] = fr
    c["emat2"] = ((np.arange(SL)[None, :] // 64) % 64 == np.arange(64)[:, None]).astype(np.float32)
    return c


def const2_shapes(SL):
    return [("ohf", [33, 768]), ("keyb", [128, SL // 128]), ("cvrow", [1, 512]), ("d0", [128, 128]),
            ("firstrow", [128, 128]), ("emat2", [64, SL])]


CONST_SHAPES = [("ident", [128, 128]), ("jmat", [128, 128]), ("tri_incl", [128, 128]), ("tri_excl", [128, 128]), ("tri_after", [128, 128]),
                ("m_su", [128, 128]), ("m_sl", [128, 128]), ("m_iu", [128, 128]), ("bdmask", [128, 128]), ("sel2", [128, 2])]

WEIGHT_SHAPES = [
    ("ffn1_w_gate", [D, DFF]), ("ffn1_w_up", [D, DFF]), ("ffn1_w_down", [DFF, D]), ("ln1_g", [D]), ("ln1_b", [D]),
    ("mix_w_in", [D, 3096]), ("rwkv_mu", [1792]), ("rwkv_w0", [512]), ("rwkv_w_up", [64, 512]), ("rwkv_a0", [512]),
    ("rwkv_a_up", [64, 512]), ("rwkv_g_up", [128, 512]), ("rwkv_k_k", [512]), ("rwkv_k_a", [512]), ("rwkv_r_k", [512]),
    ("rwkv_gn_g", [512]), ("rwkv_gn_b", [512]),
    ("nsa_pe_k", [32, 64]), ("nsa_w1_k", [2048, 128]), ("nsa_w2_k", [128, 64]),
    ("nsa_pe_v", [32, 64]), ("nsa_w1_v", [2048, 128]), ("nsa_w2_v", [128, 64]),
    ("mix_w_out", [D, D]), ("ln2_g", [D]), ("ln2_b", [D]),
    ("xattn_wq", [D, D]), ("xattn_wk", [D, D]), ("xattn_wv", [D, D]), ("xattn_wo", [D, D]),
    ("ln3_g", [D]), ("ln3_b", [D]),
    ("ffn2_w_gate", [D, DFF]), ("ffn2_w_up", [D, DFF]), ("ffn2_w_down", [DFF, D]), ("ln4_g", [D]), ("ln4_b", [D]),
]


def build(SL=8192, upto=99, debug=False, stop=99):
    nc = bass.Bass("TRN2", target_bir_lowering=False)
    C = Ctx()
    C.nc = nc
    C.SL = SL
    C.stop = stop
    NT = SL // 128
    OWN = SL // 2
    ext = lambda n, shape: nc.dram_tensor(n, shape, F32, kind="ExternalInput").ap()
    kind_scr = "ExternalOutput" if debug else "Internal"
    scr = lambda n, shape, dt=F32: nc.dram_tensor(n, shape, dt, kind=kind_scr).ap()
    d = {}
    d["x"] = ext("x", [SL, D])
    d["valid"] = ext("valid", [128, NT])
    for n, shape in CONST_SHAPES + WEIGHT_SHAPES + const2_shapes(SL):
        d[n] = ext(n, shape)
    d["x1"] = scr("x1", [SL, D])
    d["KT"] = scr("KT", [8, 64, SL], BF16)
    d["QT"] = scr("QT", [8, 64, OWN], BF16)
    d["VSW"] = scr("VSW", [SL, 256], BF16)
    d["GATE"] = scr("GATE", [OWN, 24])
    d["YR"] = scr("YR", [OWN, 512])
    d["YN"] = scr("YN", [OWN, 512])
    d["X3"] = scr("X3", [OWN, D])
    d["FB"] = scr("FB", [8, 768])
    d["mem"] = ext("mem", [256, D])
    d["rel_bias"] = ext("rel_bias", [32, 8])
    out = nc.dram_tensor("out", [OWN, D], F32, kind="ExternalOutput").ap()
    with ExitStack() as st:
        S = Sched(nc, st)
        C.S = S
        C.Tconst = Tok("const")
        C.ident = st.enter_context(nc.sbuf_tensor("sb_ident", [128, 128], F32))
        C.valid = st.enter_context(nc.sbuf_tensor("sb_valid", [128, NT], F32))
        S.dma("sp", C.ident[:], d["ident"][:, :], w=[C.Tconst])
        S.dma("sp", C.valid[:], d["valid"][:, :], w=[C.Tconst])
        ffn_phase(S, C, d["x"], d["x1"], d["ffn1_w_gate"], d["ffn1_w_up"], d["ffn1_w_down"],
                  d["ln1_g"], d["ln1_b"], SL // 256, True, "f1")
        if upto >= 2:
            mix_phase(S, C, d)
        if upto >= 3:
            nsa_phase(S, C, d)
        if upto >= 4:
            post_phase(S, C, d)
        if upto >= 5:
            d["out"] = out
            ffn_phase(S, C, d["X3"], out, d["ffn2_w_gate"], d["ffn2_w_up"], d["ffn2_w_down"],
                      d["ln4_g"], d["ln4_b"], OWN // 256, False, "f2")
        S.barrier()
        print("ninst", S.ninst)
    return nc


_NC_CACHE = {}


def kernel(**inputs):
    SL = 8192
    OWN = SL // 2
    x = np.asarray(inputs["x"], dtype=np.float32)
    B = x.shape[0]
    if "nc" not in _NC_CACHE:
        _NC_CACHE["nc"] = build(SL=SL, upto=99, debug=False)
    nc = _NC_CACHE["nc"]
    base = dict(host_consts())
    for n, shp in WEIGHT_SHAPES:
        a = np.asarray(inputs[n], dtype=np.float32)
        base[n] = np.ascontiguousarray(a.reshape(shp))
    base["rel_bias"] = np.ascontiguousarray(np.asarray(inputs["rel_bias"], dtype=np.float32))
    in_maps = []
    for c in range(8):
        b, half = c // 2, c % 2
        pad = OWN if half == 0 else 0
        m = dict(base)
        xl = np.zeros((SL, D), np.float32)
        if half == 0:
            xl[OWN:] = x[b, :OWN]
        else:
            xl[:] = x[b]
        m["x"] = xl
        valid = np.ones((128, SL // 128), np.float32)
        valid[:, :pad // 128] = 0.0
        m["valid"] = valid
        m["mem"] = np.ascontiguousarray(np.asarray(inputs["mem"], dtype=np.float32)[b])
        m.update(host_consts2(SL, pad))
        in_maps.append(m)
    res = run_bass_kernel_spmd(nc, in_maps, core_ids=list(range(8)))
    out = np.zeros((B, SL, D), np.float32)
    for c in range(8):
        b, half = c // 2, c % 2
        out[b, half * OWN:(half + 1) * OWN] = np.asarray(res.results[c]["out"], dtype=np.float32)
    return out
```
